# Optimizing a Trainium2 kernel written in Bass

```python
import math
import jax, jax.numpy as jnp
from jax import lax
import numpy as np

D_MODEL = 1024
BATCH = 4
SEQ = 4096
DEPTH = 1

HEAD_DIM = 64
N_HEADS_FOX = 8
N_HEADS_DIL = 8
FOX_WIDTH = N_HEADS_FOX * HEAD_DIM
DIL_WIDTH = N_HEADS_DIL * HEAD_DIM
DIL_PATTERNS = ((128, 1), (512, 4), (2048, 16))
ROPE_DIM = HEAD_DIM // 4
ROPE_THETA = 500000.0
Q_BLOCK = 128
D_FF = 2816
CONV_WIDTH = 3
RMS_EPS = 1e-6
NEG_INF = -1e30
IN_SPLITS = (FOX_WIDTH, FOX_WIDTH, FOX_WIDTH, N_HEADS_FOX,
             DIL_WIDTH, DIL_WIDTH, DIL_WIDTH, D_MODEL, D_MODEL)
IN_WIDTH = sum(IN_SPLITS)

kernel_name = "hybrid_fox_dilated_gated_convffn"


def rmsnorm(x, g):
    xf = x.astype(jnp.float32)
    inv = lax.rsqrt(jnp.mean(xf * xf, axis=-1, keepdims=True) + RMS_EPS)
    return (xf * inv * g.astype(jnp.float32)).astype(x.dtype)


def partial_rope(t):
    S = t.shape[1]
    half = ROPE_DIM // 2
    inv_freq = ROPE_THETA ** (-jnp.arange(half, dtype=jnp.float32) * 2.0 / ROPE_DIM)
    ang = jnp.arange(S, dtype=jnp.float32)[:, None] * inv_freq[None, :]
    cos = jnp.cos(ang)[:, None, :]
    sin = jnp.sin(ang)[:, None, :]
    tf = t.astype(jnp.float32)
    t1, t2, rest = tf[..., :half], tf[..., half:ROPE_DIM], tf[..., ROPE_DIM:]
    out = jnp.concatenate([t1 * cos - t2 * sin, t2 * cos + t1 * sin, rest], axis=-1)
    return out.astype(t.dtype)


def split_heads(t, n_heads):
    B, S, _ = t.shape
    return t.reshape(B, S, n_heads, HEAD_DIM)


def fox_attention(q, k, v, log_f):
    B, S, H, dh = q.shape
    nb = S // Q_BLOCK
    scale = 1.0 / math.sqrt(dh)
    F = jnp.cumsum(log_f, axis=1).transpose(0, 2, 1)
    kt = k.transpose(0, 2, 1, 3)
    vt = v.transpose(0, 2, 1, 3)
    q_blocks = q.transpose(0, 2, 1, 3).reshape(B, H, nb, Q_BLOCK, dh).transpose(2, 0, 1, 3, 4)
    f_blocks = F.reshape(B, H, nb, Q_BLOCK).transpose(2, 0, 1, 3)
    starts = jnp.arange(nb, dtype=jnp.int32) * Q_BLOCK
    kpos = jnp.arange(S, dtype=jnp.int32)

    def one_block(args):
        qb, fqb, start = args
        s = jnp.einsum('bhqd,bhkd->bhqk', qb, kt).astype(jnp.float32) * scale
        s = s + fqb[..., None] - F[:, :, None, :]
        qpos = start + jnp.arange(Q_BLOCK, dtype=jnp.int32)
        causal = kpos[None, :] <= qpos[:, None]
        s = jnp.where(causal[None, None], s, NEG_INF)
        p = jax.nn.softmax(s, axis=-1)
        return jnp.einsum('bhqk,bhkd->bhqd', p.astype(vt.dtype), vt)

    out = lax.map(one_block, (q_blocks, f_blocks, starts))
    return out.transpose(1, 0, 3, 2, 4).reshape(B, S, H, dh)


def dilated_branch(q, k, v, window, dilation):
    B, S, H, dh = q.shape
    L = S // dilation
    w_sub = window // dilation
    blk = w_sub
    nb = -(-L // blk)
    Lp = nb * blk
    scale = 1.0 / math.sqrt(dh)

    def prep(t):
        t = t.reshape(B, L, dilation, H, dh)
        t = jnp.pad(t, ((0, 0), (0, Lp - L), (0, 0), (0, 0), (0, 0)))
        return t.reshape(B, nb, blk, dilation, H, dh)

    def with_prev(t):
        prev = jnp.pad(t[:, :-1], ((0, 0), (1, 0), (0, 0), (0, 0), (0, 0), (0, 0)))
        return jnp.concatenate([prev, t], axis=2)

    qs = prep(q)
    kk = with_prev(prep(k))
    vv = with_prev(prep(v))
    s = jnp.einsum('bnqrhd,bnkrhd->bnrhqk', qs, kk).astype(jnp.float32) * scale
    qi = jnp.arange(blk)[:, None]
    ki = jnp.arange(2 * blk)[None, :]
    dist = qi + blk - ki
    band = (dist >= 0) & (dist <= w_sub)
    exists = (jnp.arange(nb)[:, None, None] > 0) | (ki[None] >= blk)
    valid = band[None] & exists
    s = jnp.where(valid[None, :, None, None], s, NEG_INF)
    lse = jax.nn.logsumexp(s, axis=-1)
    p = jnp.exp(s - lse[..., None])
    o = jnp.einsum('bnrhqk,bnkrhd->bnqrhd', p.astype(vv.dtype), vv)
    o = o.reshape(B, Lp, dilation, H, dh)[:, :L].reshape(B, S, H, dh)
    lse = lse.transpose(0, 1, 4, 2, 3).reshape(B, Lp, dilation, H)[:, :L].reshape(B, S, H)
    return o, lse


def dilated_attention(q, k, v):
    outs, lses = [], []
    for window, dilation in DIL_PATTERNS:
        o, l = dilated_branch(q, k, v, window, dilation)
        outs.append(o)
        lses.append(l)
    lse = jnp.stack(lses, axis=0)
    alpha = jax.nn.softmax(lse, axis=0)
    out = jnp.stack(outs, axis=0).astype(jnp.float32)
    return jnp.sum(alpha[..., None] * out, axis=0).astype(q.dtype)


def causal_dwconv(u, w, b):
    S = u.shape[1]
    up = jnp.pad(u, ((0, 0), (CONV_WIDTH - 1, 0), (0, 0)))
    y = sum(up[:, i:i + S] * w[i] for i in range(CONV_WIDTH))
    return y + b


def setup_inputs(seed: int = 0) -> dict:
    key = jax.random.key(seed)
    ks = jax.random.split(key, 20)
    f32 = jnp.float32

    def nrm(k, shape, fan_in):
        return jax.random.normal(k, shape, f32) * (fan_in ** -0.5)

    def gain(k):
        return 1.0 + 0.05 * jax.random.normal(k, (DEPTH, D_MODEL), f32)

    return {
        "x": jax.random.normal(ks[0], (BATCH, SEQ, D_MODEL), f32),
        "g_pre_mix": gain(ks[1]),
        "w_in": nrm(ks[2], (DEPTH, D_MODEL, IN_WIDTH), D_MODEL),
        "b_forget": 2.0 + 0.5 * jax.random.normal(ks[3], (DEPTH, N_HEADS_FOX), f32),
        "w_o_fox": nrm(ks[4], (DEPTH, FOX_WIDTH, D_MODEL), FOX_WIDTH),
        "w_o_dil": nrm(ks[5], (DEPTH, DIL_WIDTH, D_MODEL), DIL_WIDTH),
        "w_out": nrm(ks[6], (DEPTH, D_MODEL, D_MODEL), D_MODEL),
        "g_post_mix": gain(ks[7]),
        "g_pre_ffn": gain(ks[8]),
        "w_up": nrm(ks[9], (DEPTH, D_MODEL, 2 * D_FF), D_MODEL),
        "conv_w": nrm(ks[10], (DEPTH, CONV_WIDTH, 2 * D_FF), CONV_WIDTH),
        "conv_b": 0.02 * jax.random.normal(ks[11], (DEPTH, 2 * D_FF), f32),
        "w_down": nrm(ks[12], (DEPTH, D_FF, D_MODEL), D_FF),
        "g_post_ffn": gain(ks[13]),
    }


def reference(x, g_pre_mix, w_in, b_forget, w_o_fox, w_o_dil, w_out, g_post_mix,
              g_pre_ffn, w_up, conv_w, conv_b, w_down, g_post_ffn):
    B, S, _ = x.shape
    offsets = np.cumsum((0,) + IN_SPLITS)
    for l in range(DEPTH):
        h = rmsnorm(x, g_pre_mix[l])
        z = h @ w_in[l]
        qa, ka, va, fa, qb, kb, vb, ga, gb = [z[..., offsets[i]:offsets[i + 1]]
                                              for i in range(len(IN_SPLITS))]
        log_f = jax.nn.log_sigmoid((fa + b_forget[l]).astype(jnp.float32))
        ya = fox_attention(split_heads(qa, N_HEADS_FOX), split_heads(ka, N_HEADS_FOX),
                           split_heads(va, N_HEADS_FOX), log_f)
        ya = ya.reshape(B, S, FOX_WIDTH) @ w_o_fox[l]
        qd = partial_rope(split_heads(qb, N_HEADS_DIL))
        kd = partial_rope(split_heads(kb, N_HEADS_DIL))
        yb = dilated_attention(qd, kd, split_heads(vb, N_HEADS_DIL))
        yb = yb.reshape(B, S, DIL_WIDTH) @ w_o_dil[l]
        mixed = jax.nn.sigmoid(ga) * ya + jax.nn.sigmoid(gb) * yb
        x = x + rmsnorm(mixed @ w_out[l], g_post_mix[l])
        h = rmsnorm(x, g_pre_ffn[l])
        u = causal_dwconv(h @ w_up[l], conv_w[l], conv_b[l])
        a, b = u[..., :D_FF], u[..., D_FF:]
        m = jax.nn.gelu(a, approximate=True) * b
        x = x + rmsnorm(m @ w_down[l], g_post_ffn[l])
    return x
```

```python
import numpy as np
from contextlib import ExitStack
import concourse.bass as bass
import concourse.mybir as mybir
from concourse.bass_utils import run_bass_kernel_spmd

F32 = mybir.dt.float32
BF16 = mybir.dt.bfloat16
AF = mybir.ActivationFunctionType
ALU = mybir.AluOpType
NDSEM = 40

D = 1024
NBK = 33
NTOK = NBK * 128
KCH = [(0, 4), (4, 4), (8, 4), (12, 4), (16, 1), (17, 4), (21, 4), (25, 4), (29, 4)]
QCH = KCH[4:]
NQT = 17 * 128
OFF = dict(qa=0, ka=512, va=1024, fa=1536, qb=1544, kb=2056, vb=2568, ga=3080, gb=4104)
DFF = 2816
NFC = 44
EPS = 1e-6


class Buf:
    __slots__ = ("name", "w", "r", "wsmall", "wl")

    def __init__(self, name=""):
        self.name = name
        self.w = None
        self.r = {}
        self.wsmall = False
        self.wl = []


SKIP_SAME_ENGINE_RAW = True


class Rot:
    def __init__(self, items):
        self.items = items
        self.i = 0

    def next(self):
        it = self.items[self.i]
        self.i = (self.i + 1) % len(self.items)
        return it


class FW:
    ENG = ("pe", "act", "dve", "pool", "sp")

    def __init__(self, nc, es):
        self.nc = nc
        self.e = {"pe": nc.tensor, "act": nc.scalar, "dve": nc.vector,
                  "pool": nc.gpsimd, "sp": nc.sync}
        self.sem = {k: es.enter_context(nc.semaphore("s_" + k)) for k in self.ENG}
        self.cnt = {k: 0 for k in self.ENG}
        self.waited = {}
        self.dsems = [es.enter_context(nc.semaphore("d%d" % i)) for i in range(NDSEM)]
        self.dcnt = [0] * NDSEM
        self.dnext = 0

    def _wait(self, eng, tok):
        kind, key, val = tok
        wk = (eng, kind, key)
        if self.waited.get(wk, 0) >= val:
            return
        sem = self.sem[key] if kind == "e" else self.dsems[key]
        self.e[eng].wait_ge(sem, val)
        self.waited[wk] = val

    def _deps(self, eng, reads, writes, par=False):
        for b in reads:
            for tok in b.wl:
                self._wait(eng, tok)
            if b.w is not None:
                if (SKIP_SAME_ENGINE_RAW and b.w[0] == "e" and b.w[1] == eng and not b.wsmall
                        and eng != "pool"):
                    continue
                self._wait(eng, b.w)
        for b in writes:
            if par:
                continue
            for tok in b.wl:
                self._wait(eng, tok)
            if b.w is not None and not (b.w[0] == "e" and b.w[1] == eng):
                self._wait(eng, b.w)
            for tok in b.r.values():
                if tok[0] == "e" and tok[1] == eng:
                    continue
                self._wait(eng, tok)

    def op(self, eng, fn, reads=(), writes=(), inc=True, small=False):
        self._deps(eng, reads, writes)
        ins = fn()
        if inc:
            self.cnt[eng] += 1
            ins.then_inc(self.sem[eng], 1)
            tok = ("e", eng, self.cnt[eng])
        else:
            tok = ("e", eng, self.cnt[eng] + 1)
        for b in reads:
            b.r[("e", eng)] = tok
        for b in writes:
            b.w = tok
            b.wl = []
            b.r = {}
            b.wsmall = small
        return tok

    def dma(self, q, out_ap, in_ap, reads=(), writes=(), par=False):
        self._deps(q, reads, writes, par=par)
        if par:
            for b in writes:
                for tok in b.r.values():
                    self._wait(q, tok)
        i = self.dnext
        self.dnext = (i + 1) % NDSEM
        if self.dcnt[i] > 0:
            self._wait(q, ("d", i, self.dcnt[i]))
        self.dcnt[i] += 16
        self.e[q].dma_start(out=out_ap, in_=in_ap).then_inc(self.dsems[i], 16)
        tok = ("d", i, self.dcnt[i])
        for b in reads:
            b.r[("d", i)] = tok
        for b in writes:
            if par:
                b.wl.append(tok)
            else:
                b.w = tok
                b.wl = []
            b.r = {}
        return tok

    def barrier(self):
        import os
        if os.environ.get("KDEBUG"):
            print("barrier: sbuf remaining", self.nc.sbuf_bytes_remaining, "cnt", dict(self.cnt))
        for k in ("pe", "act", "dve", "pool"):
            if self.cnt[k] > 0:
                self._wait("sp", ("e", k, self.cnt[k]))
        for i in range(NDSEM):
            if self.dcnt[i] > 0:
                self._wait("sp", ("d", i, self.dcnt[i]))
        self.cnt["sp"] += 1
        self.e["sp"].nop().then_inc(self.sem["sp"], 1)
        for k in ("pe", "act", "dve", "pool"):
            self._wait(k, ("e", "sp", self.cnt["sp"]))


def build_program(debug=False):
    nc = bass.Bass("TRN2", target_bir_lowering=False)

    def din(name, shape):
        return nc.dram_tensor(name, shape, F32, kind="ExternalInput").ap()

    xk = din("xk", [NTOK, D])
    w_in = din("w_in", [D, 5128])
    w_o_fox = din("w_o_fox", [512, D])
    w_o_dil = din("w_o_dil", [512, D])
    w_out = din("w_out", [D, D])
    w_up = din("w_up", [D, 2 * DFF])
    w_down = din("w_down", [DFF, D])
    gT1_d = din("gT1", [128, 8])
    gT2_d = din("gT2", [128, 8])
    gB1_d = din("gB1", [128, D])
    gB2_d = din("gB2", [128, D])
    bf_d = din("bf", [8, 1])
    cw_d = din("cw", [128, 3 * NFC])
    cb_d = din("cb", [128, NFC])
    mbo_d = din("mb_own", [128, NBK])
    mbh_d = din("mb_halo", [128, NBK])
    ropeC_d = din("ropeC", [128, NTOK])
    ropeS_d = din("ropeS", [128, NTOK])
    mm_d = din("mmask", [20, 128, 512])
    causal_d = din("causal", [128, 128])
    ident_d = din("ident", [128, 128])
    sel_d = din("sel", [8, 8 * 128])
    hflag_d = din("hflag", [128, 1])
    y_out = nc.dram_tensor("y", [2048, D], F32, kind="ExternalOutput").ap()
    x1s = nc.dram_tensor("x1s", [NQT, D], F32).ap()
    dbg = {}
    if debug:
        dbg["ya"] = nc.dram_tensor("dbg_ya", [128, 4, NQT], BF16, kind="ExternalOutput").ap()
        dbg["yb"] = nc.dram_tensor("dbg_yb", [128, 4, NQT], BF16, kind="ExternalOutput").ap()
        dbg["x1"] = nc.dram_tensor("dbg_x1", [NQT, D], F32, kind="ExternalOutput").ap()

    w_in_v = w_in.rearrange("(c p) n -> p c n", p=128)
    w_up_v = w_up.rearrange("(c p) n -> p c n", p=128)
    w_down_v = w_down.rearrange("(c p) n -> p c n", p=128)
    w_out_v = w_out.rearrange("(c p) n -> p c n", p=128)
    wof_v = w_o_fox.rearrange("(c p) n -> p c n", p=128)
    wod_v = w_o_dil.rearrange("(c p) n -> p c n", p=128)

    out_toks = []
    with ExitStack() as es:
        fw = FW(nc, es)

        uid = [0]

        def T(stack, name, shape, dt):
            uid[0] += 1
            return stack.enter_context(nc.sbuf_tensor("sb%d_%s" % (uid[0], name), shape, dt))

        def P(stack, name, shape, dt=F32):
            uid[0] += 1
            return stack.enter_context(nc.psum_tensor("ps%d_%s" % (uid[0], name), shape, dt))

        def mm(out, lhsT, rhs, start, stop, reads, writes, inc=None):
            return fw.op("pe", lambda: nc.tensor.matmul(out, lhsT, rhs, start=start, stop=stop),
                         reads, writes, inc=(stop if inc is None else inc))

        def act(out, in_, func, reads, writes, **kw):
            return fw.op("act", lambda: nc.scalar.activation(out, in_, func, **kw), reads, writes)

        def loadw(dst, src_v, col0, ncols, nchunks, b):
            for c in range(nchunks):
                fw.dma("pool", dst[:, c, 0:ncols], src_v[:, c, col0:col0 + ncols], writes=[b], par=True)

        identf = T(es, "identf", [128, 128], F32)
        identb = T(es, "identb", [128, 128], BF16)
        causal = T(es, "causal", [128, 128], BF16)
        ones32 = T(es, "ones32", [128, 64], F32)
        gT1 = T(es, "gT1", [128, 8], F32)
        gT2 = T(es, "gT2", [128, 8], F32)
        mbo = T(es, "mbo", [128, NBK], F32)
        mbh = T(es, "mbh", [128, NBK], F32)
        hflag = T(es, "hflag", [128, 1], F32)
        bfT = T(es, "bfT", [8, 1], F32)
        b_const = Buf("const")
        for dst, src in ((identf, ident_d), (gT1, gT1_d), (gT2, gT2_d), (mbo, mbo_d),
                         (mbh, mbh_d), (hflag, hflag_d), (bfT, bf_d)):
            fw.dma("sp", dst[:], src, writes=[b_const])
        fw.dma("pool", identb[:], ident_d, writes=[b_const])
        fw.dma("pool", causal[:], causal_d, writes=[b_const])
        fw.op("dve", lambda: nc.vector.memset(ones32[:], 1.0), writes=[b_const])
        negones = T(es, "negones", [65, 512], F32)
        fw.op("dve", lambda: nc.vector.memset(negones[:], -1.0), writes=[b_const])
        fw.barrier()

        att = ExitStack()
        ybT = T(att, "ybT", [128, 4, NQT], BF16)
        yaT = T(att, "yaT", [128, 4, NQT], BF16)

        class NormRes:
            pass

        def make_norm_res(stack, tr_ps):
            r = NormRes()
            r.xrot = Rot([(T(stack, "xt%d" % i, [128, D], F32), Buf()) for i in range(2)])
            r.junk = T(stack, "junk", [128, D], BF16)
            r.b_junk = Buf()
            r.strot = Rot([(T(stack, "st%d" % i, [128, 4], F32), Buf()) for i in range(2)])
            r.tr = tr_ps
            r.b_tr = Buf()
            return r

        def rstd_from(r, src_ap, src_bufs):
            st, b_st = r.strot.next()
            fw.op("act", lambda: nc.scalar.activation(r.junk[:], src_ap, AF.Square, accum_out=st[:, 0:1]),
                  reads=src_bufs, writes=[r.b_junk, b_st], small=True)
            fw.op("act", lambda: nc.scalar.activation(st[:, 1:2], st[:, 0:1], AF.Sqrt, scale=1.0 / D, bias=EPS),
                  reads=[b_st], writes=[b_st], small=True)
            fw.op("dve", lambda: nc.vector.reciprocal(st[:, 2:3], st[:, 1:2]), reads=[b_st], writes=[b_st], small=True)
            return st, b_st

        def norm_transpose(r, src_rows_ap, src_bufs, hT, b_hT, col0, gT):
            xt, b_xt = r.xrot.next()
            fw.dma("sp", xt[:], src_rows_ap, reads=src_bufs, writes=[b_xt])
            st, b_st = rstd_from(r, xt[:], [b_xt])
            fw.op("dve", lambda: nc.vector.tensor_scalar(xt[:], xt[:], st[:, 2:3], None, ALU.mult),
                  reads=[b_xt, b_st], writes=[b_xt])
            for c in range(8):
                fw.op("pe", lambda c=c: nc.tensor.transpose(r.tr[:, c * 128:(c + 1) * 128], xt[:, c * 128:(c + 1) * 128], identf[:]),
                      reads=[b_xt], writes=[r.b_tr], inc=(c == 7))
            fw.op("dve", lambda: nc.vector.tensor_tensor(
                hT[:, :, col0:col0 + 128], r.tr[:, :].rearrange("p (c t) -> p c t", c=8),
                gT[:, :].unsqueeze(2).to_broadcast([128, 8, 128]), ALU.mult),
                reads=[r.b_tr], writes=[b_hT])
            return xt, b_xt

        def pipelined_chunks(chunks, nr, hrot, gT):
            def prep(ch):
                b0, nb = ch
                hT, b_hT = hrot.next()
                for bi in range(nb):
                    norm_transpose(nr, xk[(b0 + bi) * 128:(b0 + bi + 1) * 128, :], [], hT, b_hT, bi * 128, gT)
                return (b0, nb, hT, b_hT)
            cur = prep(chunks[0])
            for i in range(len(chunks)):
                nxt = prep(chunks[i + 1]) if i + 1 < len(chunks) else None
                yield cur
                cur = nxt

        def proj_tokmajor_V(pjrot, hT, b_hT, nblk, W, b_W, Vt, blk0):
            for bi in range(nblk):
                ps, b_ps = pjrot.next()
                for c in range(8):
                    mm(ps[:, 0:512], hT[:, c, bi * 128:(bi + 1) * 128], W[:, c, 0:512], c == 0, c == 7, [b_hT, b_W], [b_ps])
                fw.op("dve", lambda ps=ps, bi=bi: nc.vector.tensor_copy(
                    Vt[:, blk0 + bi, :, 0:64], ps[:, 0:512].rearrange("p (h d) -> p h d", h=8)),
                    reads=[b_ps], writes=[])

        def attention(kind, q0blk, nqb, QT, b_QT, KT, Vt, yT, res, FT=None, biasK=None, mb=None, MM=None):
            N = nqb * 128
            qtok0 = (q0blk - 16) * 128
            if kind == "fox":
                kbs = list(range(0, q0blk + nqb))
            else:
                kbs = list(range(max(0, q0blk - 16), q0blk + nqb))
            nk = len(kbs)
            L = 4
            tiles = [(h, i, kb) for h in range(8) for i, kb in enumerate(kbs)]
            fq = {}
            ot = {}
            pvbuf = {}
            deferred = []

            def emit_fq(h):
                ps, b_ps = res.pjrot.next()
                mm(ps[:, 0:N], res.sel[:, h * 128:(h + 1) * 128], FT[0:8, q0blk * 128:q0blk * 128 + N], True, True, [], [b_ps])
                fqb, b_fqb = res.fqrot.next()
                act(fqb[:, 0:N], ps[:, 0:N], AF.Copy, [b_ps], [b_fqb])
                fq[h] = (fqb, b_fqb)

            def stage_a(t):
                h, i, kb = tiles[t]
                p, a = h // 2, h % 2
                rs = slice(a * 64, (a + 1) * 64)
                if kind == "fox" and i == 0 and h + 1 < 8:
                    emit_fq(h + 1)
                j = kb - q0blk
                c0 = max(0, j) * 128
                diag = (j >= 0) and kind == "fox"
                S, b_S = res.srot.next()
                mm(S[:, c0:N], KT[:, p, kb * 128:(kb + 1) * 128], QT[a][:, p, c0:N], True, not diag, [b_QT], [b_S])
                if diag:
                    mm(S[:, c0:c0 + 128], identb[:], causal[:], False, True, [], [b_S])
                pt, b_pt = res.ptrot.next()
                if kind == "fox":
                    fqb, b_fqb = fq[h]
                    ssb, b_ssb = res.ssrot.next()
                    fw.op("dve", lambda: nc.vector.tensor_tensor(
                        ssb[:, c0:N], S[:, c0:N], fqb[:, c0:N], ALU.add), reads=[b_S, b_fqb], writes=[b_ssb])
                    act(pt[:, c0:N], ssb[:, c0:N], AF.Exp, [b_ssb], [b_pt], scale=0.125, bias=biasK[:, kb, h:h + 1])
                    pvbuf[t] = (pt, b_pt, c0)
                else:
                    act(pt[:, c0:N], S[:, c0:N], AF.Exp, [b_S], [b_pt], scale=0.125, bias=mb[:, kb:kb + 1])
                    pm, b_pm = res.pmrot.next()
                    rel = q0blk - kb + 3
                    if t % 3 == 2:
                        fw.op("pool", lambda: nc.gpsimd.tensor_tensor(
                            pm[:, c0:N], pt[:, c0:N], MM[:, rel, c0:N], ALU.mult), reads=[b_pt], writes=[b_pm])
                    else:
                        fw.op("dve", lambda: nc.vector.tensor_tensor(
                            pm[:, c0:N], pt[:, c0:N], MM[:, rel, c0:N], ALU.mult), reads=[b_pt], writes=[b_pm])
                    pvbuf[t] = (pm, b_pm, c0)

            def stage_b(t, step):
                h, i, kb = tiles[t]
                p, a = h // 2, h % 2
                rs = slice(a * 64, (a + 1) * 64)
                if i == 0:
                    ot[h] = res.otrot.next()
                oT, b_oT = ot[h]
                pv, b_pv, c0 = pvbuf.pop(t)
                vo = (kb * 8 + h) * 65
                mm(oT[:, c0:N], Vt[:, vo:vo + 128], pv[:, c0:N], i == 0, i == nk - 1, [b_pv], [b_oT])
                if i == nk - 1:
                    rec, b_rec = res.recrot.next()
                    ots, b_ots = res.otsrot.next()
                    act(ots[0:65, 0:N], oT[0:65, 0:N], AF.Copy, [b_oT], [b_ots])
                    fw.op("pool", lambda: nc.gpsimd.tensor_tensor(
                        rec[64:65, 0:N], ots[64:65, 0:N], negones[64:65, 0:N], ALU.pow),
                        reads=[b_ots], writes=[b_rec])

                    def part2():
                        R, b_R = res.pjrot.next()
                        mm(R[0:64, 0:N], ones32[64:65, 0:64], rec[64:65, 0:N], True, True, [b_rec], [b_R])
                        fw.op("dve", lambda: nc.vector.tensor_tensor(
                            yT[rs, p, qtok0:qtok0 + N], R[0:64, 0:N], ots[0:64, 0:N], ALU.mult),
                            reads=[b_R, b_ots], writes=[])
                    deferred.append((step + 6, part2))

            if kind == "fox":
                emit_fq(0)
            nt = len(tiles)
            for step in range(nt + L):
                if step < nt:
                    stage_a(step)
                if step - L >= 0:
                    stage_b(step - L, step)
                while deferred and deferred[0][0] <= step:
                    deferred.pop(0)[1]()
            while deferred:
                deferred.pop(0)[1]()

        class Res:
            pass

        with ExitStack() as ph:
            KdT = T(ph, "KdT", [128, 4, NTOK], BF16)
            Vd_flat = T(ph, "Vd", [128, NBK * 520 + 64], BF16)
            Vd = Vd_flat[:, 0:NBK * 520].rearrange("p (b h d) -> p b h d", b=NBK, h=8)
            fw.op("pool", lambda: nc.gpsimd.memset(Vd_flat[:, NBK * 520:NBK * 520 + 64], 0.0), writes=[])
            MM = T(ph, "MM", [128, 20, 512], BF16)
            b_mm = Buf()
            Wq = T(ph, "Wq", [128, 8, 512], BF16)
            Wqr = T(ph, "Wqr", [128, 8, 512], BF16)
            b_Wq = Buf()
            fw.op("pool", lambda: nc.gpsimd.memset(Vd[:, :, :, 64:65], 1.0), writes=[])
            hrot = Rot([(T(ph, "hT%d" % i, [128, 8, 512], BF16), Buf()) for i in range(2)])
            crot = Rot([(T(ph, "Ct%d" % i, [128, 512], F32), Buf()) for i in range(2)])
            srot_t = Rot([(T(ph, "St%d" % i, [128, 512], F32), Buf()) for i in range(2)])
            t1rot = Rot([(T(ph, "t1_%d" % i, [128, 512], F32), Buf()) for i in range(1)])
            t2rot = Rot([(T(ph, "t2_%d" % i, [128, 512], F32), Buf()) for i in range(1)])

            def rope_proj(pjrot, hT, b_hT, N, W, Wr, b_W, dstT, dcol0, tok0, b_dst_list, dstB=None):
                Ct, b_Ct = crot.next()
                St, b_St = srot_t.next()
                fw.dma("sp", Ct[:, 0:N], ropeC_d[:, tok0:tok0 + N], writes=[b_Ct])
                fw.dma("sp", St[:, 0:N], ropeS_d[:, tok0:tok0 + N], writes=[b_St])
                for p in range(4):
                    psA, b_A = pjrot.next()
                    psB, b_B = pjrot.next()
                    for c in range(8):
                        mm(psA[:, 0:N], W[:, c, p * 128:(p + 1) * 128], hT[:, c, 0:N], c == 0, c == 7, [b_hT, b_W], [b_A])
                    for c in range(8):
                        mm(psB[:, 0:N], Wr[:, c, p * 128:(p + 1) * 128], hT[:, c, 0:N], c == 0, c == 7, [b_hT, b_W], [b_B])
                    t1, b_t1 = t1rot.next()
                    t2, b_t2 = t2rot.next()
                    fw.op("dve", lambda: nc.vector.tensor_tensor(t1[:, 0:N], psA[:, 0:N], Ct[:, 0:N], ALU.mult),
                          reads=[b_A, b_Ct], writes=[b_t1])
                    fw.op("dve", lambda: nc.vector.tensor_tensor(t2[:, 0:N], psB[:, 0:N], St[:, 0:N], ALU.mult),
                          reads=[b_B, b_St], writes=[b_t2])
                    if dstB is None:
                        fw.op("pool", lambda p=p: nc.gpsimd.tensor_tensor(dstT[:, p, dcol0:dcol0 + N], t1[:, 0:N], t2[:, 0:N], ALU.add),
                              reads=[b_t1, b_t2], writes=b_dst_list)
                    else:
                        fw.op("pool", lambda p=p: nc.gpsimd.tensor_tensor(dstT[0:64, p, dcol0:dcol0 + N], t1[0:64, 0:N], t2[0:64, 0:N], ALU.add),
                              reads=[b_t1, b_t2], writes=b_dst_list)
                        fw.op("pool", lambda p=p: nc.gpsimd.tensor_tensor(dstB[64:128, p, dcol0:dcol0 + N], t1[64:128, 0:N], t2[64:128, 0:N], ALU.add),
                              reads=[b_t1, b_t2], writes=b_dst_list)

            def make_rot_w(W, Wr, b_W):
                fw.op("pool", lambda: nc.gpsimd.memset(Wr[:], 0.0), writes=[b_W])
                Wv = W[:, :, :].rearrange("p c (h d) -> p c h d", h=8)
                Wrv = Wr[:, :, :].rearrange("p c (h d) -> p c h d", h=8)
                for c in range(8):
                    fw.op("pool", lambda c=c: nc.gpsimd.tensor_copy(Wrv[:, c, :, 0:8], Wv[:, c, :, 8:16]), reads=[b_W], writes=[b_W])
                    fw.op("pool", lambda c=c: nc.gpsimd.tensor_copy(Wrv[:, c, :, 8:16], Wv[:, c, :, 0:8]), reads=[b_W], writes=[b_W])

            with ExitStack() as sp1:
                Wk = T(sp1, "Wk", [128, 8, 512], BF16)
                Wkr = T(sp1, "Wkr", [128, 8, 512], BF16)
                Wv = T(sp1, "Wv", [128, 8, 512], BF16)
                b_W = Buf()
                loadw(Wk, w_in_v, OFF["kb"], 512, 8, b_W)
                make_rot_w(Wk, Wkr, b_W)
                loadw(Wv, w_in_v, OFF["vb"], 512, 8, b_W)
                loadw(Wq, w_in_v, OFF["qb"], 512, 8, b_Wq)
                make_rot_w(Wq, Wqr, b_Wq)
                for i in range(20):
                    fw.dma("pool", MM[:, i, :], mm_d[i], writes=[b_mm], par=True)
                tr = P(sp1, "tr", [128, 1024])
                pjrot = Rot([(P(sp1, "pj%d" % i, [128, 512]), Buf()) for i in range(6)])
                nr = make_norm_res(sp1, tr)
                for (b0, nb, hT, b_hT) in pipelined_chunks(KCH, nr, hrot, gT1):
                    N = nb * 128
                    rope_proj(pjrot, hT, b_hT, N, Wk, Wkr, b_W, KdT, b0 * 128, b0 * 128, [])
                    proj_tokmajor_V(pjrot, hT, b_hT, nb, Wv, b_W, Vd, b0)
                fw.barrier()
            with ExitStack() as sp2:
                b_W = b_Wq
                tr = P(sp2, "tr", [128, 1024])
                res = Res()
                res.pjrot = Rot([(P(sp2, "pj%d" % i, [128, 512]), Buf()) for i in range(1)])
                res.srot = Rot([(P(sp2, "S%d" % i, [128, 512]), Buf()) for i in range(3)])
                res.otrot = Rot([(P(sp2, "oT%d" % i, [128, 512]), Buf()) for i in range(2)])
                res.ptrot = Rot([(T(sp2, "pt%d" % i, [128, 512], BF16), Buf()) for i in range(4)])
                res.pmrot = Rot([(T(sp2, "pm%d" % i, [128, 512], BF16), Buf()) for i in range(6)])
                rope_rot = Rot(res.pjrot.items + res.srot.items)
                res.recrot = Rot([(T(sp2, "rec%d" % i, [65, 512], F32), Buf()) for i in range(2)])
                res.otsrot = Rot([(T(sp2, "ots%d" % i, [65, 512], F32), Buf()) for i in range(2)])
                QdA = T(sp2, "QdA", [128, 4, 512], BF16)
                QdB = T(sp2, "QdB", [128, 4, 512], BF16)
                b_QdT = Buf()
                fw.op("pool", lambda: nc.gpsimd.memset(QdA[:], 0.0), writes=[b_QdT])
                fw.op("pool", lambda: nc.gpsimd.memset(QdB[:], 0.0), writes=[b_QdT])
                nr = make_norm_res(sp2, tr)
                for (b0, nb, hT, b_hT) in pipelined_chunks(QCH, nr, hrot, gT1):
                    N = nb * 128
                    rope_proj(rope_rot, hT, b_hT, N, Wq, Wqr, b_W, QdA, 0, b0 * 128, [b_QdT], dstB=QdB)
                    attention("dil", b0, nb, (QdA, QdB), b_QdT, KdT, Vd_flat, ybT, res,
                              mb=(mbh if nb == 1 else mbo), MM=MM)
                fw.barrier()

        with ExitStack() as ph:
            KfT = T(ph, "KfT", [128, 4, NTOK], BF16)
            Vf_flat = T(ph, "Vf", [128, NBK * 520 + 64], BF16)
            Vf = Vf_flat[:, 0:NBK * 520].rearrange("p (b h d) -> p b h d", b=NBK, h=8)
            fw.op("pool", lambda: nc.gpsimd.memset(Vf_flat[:, NBK * 520:NBK * 520 + 64], 0.0), writes=[])
            FT = T(ph, "FT", [8, NTOK], F32)
            biasO = T(ph, "biasO", [128, NBK, 8], F32)
            biasH = T(ph, "biasH", [128, NBK, 8], F32)
            sel = T(ph, "sel", [8, 8 * 128], F32)
            negb = T(ph, "negb", [8, 1], F32)
            onesr = T(ph, "onesr", [8, 512], F32)
            b_c2 = Buf()
            fw.dma("sp", sel[:], sel_d, writes=[b_c2])
            fw.op("dve", lambda: nc.vector.tensor_scalar(negb[:], bfT[:], -1.0, None, ALU.mult), writes=[b_c2])
            fw.op("dve", lambda: nc.vector.memset(onesr[:], 1.0), writes=[b_c2])
            fw.op("pool", lambda: nc.gpsimd.memset(Vf[:, :, :, 64:65], 1.0), writes=[])
            hrot = Rot([(T(ph, "hT%d" % i, [128, 8, 512], BF16), Buf()) for i in range(2)])
            WqF = T(ph, "WqF", [128, 8, 512], BF16)
            b_WqF = Buf()
            with ExitStack() as sp1:
                Wk = T(sp1, "Wk", [128, 8, 512], BF16)
                Wv = T(sp1, "Wv", [128, 8, 512], BF16)
                Wf = T(sp1, "Wf", [128, 8, 8], BF16)
                b_W = Buf()
                loadw(Wk, w_in_v, OFF["ka"], 512, 8, b_W)
                loadw(Wv, w_in_v, OFF["va"], 512, 8, b_W)
                loadw(Wf, w_in_v, OFF["fa"], 8, 8, b_W)
                loadw(WqF, w_in_v, OFF["qa"], 512, 8, b_WqF)
                tr = P(sp1, "tr", [128, 1024])
                pjrot = Rot([(P(sp1, "pj%d" % i, [128, 512]), Buf()) for i in range(6)])
                nr = make_norm_res(sp1, tr)
                elrot = Rot([(T(sp1, "el%d" % i, [8, 512], F32), Buf()) for i in range(2)])
                b_FT = Buf()
                b_bias = Buf()
                prev_end = None
                for (b0, nb, hT, b_hT) in pipelined_chunks(KCH, nr, hrot, gT1):
                    N = nb * 128
                    t0 = b0 * 128
                    for p in range(4):
                        ps, b_ps = pjrot.next()
                        for c in range(8):
                            mm(ps[:, 0:N], Wk[:, c, p * 128:(p + 1) * 128], hT[:, c, 0:N], c == 0, c == 7, [b_hT, b_W], [b_ps])
                        act(KfT[:, p, t0:t0 + N], ps[:, 0:N], AF.Copy, [b_ps], [])
                    proj_tokmajor_V(pjrot, hT, b_hT, nb, Wv, b_W, Vf, b0)
                    ps, b_ps = pjrot.next()
                    for c in range(8):
                        mm(ps[0:8, 0:N], Wf[:, c, 0:8], hT[:, c, 0:N], c == 0, c == 7, [b_hT, b_W], [b_ps])
                    el, b_el = elrot.next()
                    act(el[:, 0:N], ps[0:8, 0:N], AF.Exp, [b_ps, b_c2], [b_el], scale=-1.0, bias=negb[:, 0:1])
                    act(el[:, 0:N], el[:, 0:N], AF.Ln, [b_el], [b_el], bias=1.0)
                    init = 0.0 if prev_end is None else FT[:, prev_end - 1:prev_end]
                    fw.op("dve", lambda el=el, init=init, t0=t0, N=N: nc.vector.tensor_tensor_scan(
                        FT[:, t0:t0 + N], onesr[:, 0:N], el[:, 0:N], init, ALU.mult, ALU.subtract),
                        reads=[b_el, b_FT, b_c2], writes=[b_FT], small=True)
                    prev_end = t0 + N
                    for bi in range(nb):
                        blk = b0 + bi
                        ps2, b_ps2 = pjrot.next()
                        fw.op("pe", lambda ps2=ps2, blk=blk: nc.tensor.transpose(ps2[:, 0:8], FT[0:8, blk * 128:(blk + 1) * 128], identf[0:8, 0:8]),
                              reads=[b_FT], writes=[b_ps2])
                        fw.op("dve", lambda ps2=ps2, blk=blk: nc.vector.tensor_scalar(
                            biasO[:, blk, :], ps2[:, 0:8], -1.0, mbo[:, blk:blk + 1], ALU.mult, ALU.add),
                            reads=[b_ps2], writes=[b_bias])
                        fw.op("dve", lambda ps2=ps2, blk=blk: nc.vector.tensor_scalar(
                            biasH[:, blk, :], ps2[:, 0:8], -1.0, mbh[:, blk:blk + 1], ALU.mult, ALU.add),
                            reads=[b_ps2], writes=[b_bias])
                fw.barrier()
            with ExitStack() as sp2:
                Wq = WqF
                b_W = b_WqF
                tr = P(sp2, "tr", [128, 1024])
                res = Res()
                res.sel = sel
                res.pjrot = Rot([(P(sp2, "pj%d" % i, [128, 512]), Buf()) for i in range(1)])
                res.srot = Rot([(P(sp2, "S%d" % i, [128, 512]), Buf()) for i in range(3)])
                res.otrot = Rot([(P(sp2, "oT%d" % i, [128, 512]), Buf()) for i in range(2)])
                res.ptrot = Rot([(T(sp2, "pt%d" % i, [128, 512], BF16), Buf()) for i in range(6)])
                res.ssrot = Rot([(T(sp2, "ss%d" % i, [128, 512], F32), Buf()) for i in range(4)])
                res.fqrot = Rot([(T(sp2, "fq%d" % i, [128, 512], F32), Buf()) for i in range(3)])
                res.recrot = Rot([(T(sp2, "rec%d" % i, [65, 512], F32), Buf()) for i in range(2)])
                res.otsrot = Rot([(T(sp2, "ots%d" % i, [65, 512], F32), Buf()) for i in range(2)])
                QfA = T(sp2, "QfA", [128, 4, 512], BF16)
                QfB = T(sp2, "QfB", [128, 4, 512], BF16)
                b_QfT = Buf()
                fw.op("pool", lambda: nc.gpsimd.memset(QfA[:], 0.0), writes=[b_QfT])
                fw.op("pool", lambda: nc.gpsimd.memset(QfB[:], 0.0), writes=[b_QfT])
                nr = make_norm_res(sp2, tr)
                for (b0, nb, hT, b_hT) in pipelined_chunks(QCH, nr, hrot, gT1):
                    N = nb * 128
                    for p in range(4):
                        ps, b_ps = res.pjrot.next()
                        for c in range(8):
                            mm(ps[:, 0:N], Wq[:, c, p * 128:(p + 1) * 128], hT[:, c, 0:N], c == 0, c == 7, [b_hT, b_W], [b_ps])
                        act(QfA[0:64, p, 0:N], ps[0:64, 0:N], AF.Copy, [b_ps], [b_QfT])
                        act(QfB[64:128, p, 0:N], ps[64:128, 0:N], AF.Copy, [b_ps], [b_QfT])
                    attention("fox", b0, nb, (QfA, QfB), b_QfT, KfT, Vf_flat, yaT, res, FT=FT,
                              biasK=(biasH if nb == 1 else biasO))
                fw.barrier()

        b_x1s = [Buf() for _ in range(17)]
        with ExitStack() as ph:
            Wg = T(ph, "Wg", [128, 8, 2048], BF16)
            wof = T(ph, "wof", [128, 4, D], BF16)
            wod = T(ph, "wod", [128, 4, D], BF16)
            wout = T(ph, "wout", [128, 8, D], BF16)
            gB1 = T(ph, "gB1", [128, D], F32)
            b_W = Buf()
            loadw(Wg, w_in_v, OFF["ga"], 2048, 8, b_W)
            loadw(wof, wof_v, 0, D, 4, b_W)
            loadw(wod, wod_v, 0, D, 4, b_W)
            loadw(wout, w_out_v, 0, D, 8, b_W)
            fw.dma("sp", gB1[:], gB1_d, writes=[b_W], par=True)
            tr = P(ph, "tr", [128, 1024])
            grot = Rot([(P(ph, "g%d" % i, [128, 512]), Buf()) for i in range(4)])
            yps = P(ph, "yps", [128, 1024])
            b_yps = Buf()
            nr = make_norm_res(ph, tr)
            hrotm = Rot([(T(ph, "hTm%d" % i, [128, 8, 512], BF16), Buf()) for i in range(2)])
            mixT = T(ph, "mixT", [128, 8, 512], BF16)
            b_mix = Buf()
            sarot = Rot([(T(ph, "sa%d" % i, [128, 512], F32), Buf()) for i in range(2)])
            sbrot = Rot([(T(ph, "sb%d" % i, [128, 512], F32), Buf()) for i in range(2)])
            tmp = T(ph, "tmpm", [128, D], F32)
            b_tmp = Buf()
            x1rot = Rot([(T(ph, "x1t%d" % i, [128, D], F32), Buf()) for i in range(2)])
            if debug:
                out_toks.append(fw.dma("sp", dbg["ya"], yaT[:, :, :]))
                out_toks.append(fw.dma("sp", dbg["yb"], ybT[:, :, :]))
            for (b0, nb, hTm, b_hTm) in pipelined_chunks(QCH, nr, hrotm, gT1):
                N = nb * 128
                qtok0 = (b0 - 16) * 128
                for fc in range(8):
                    ga, b_ga = grot.next()
                    gb, b_gb = grot.next()
                    yap, b_yap = grot.next()
                    ybp, b_ybp = grot.next()
                    for c in range(8):
                        mm(ga[:, 0:N], Wg[:, c, fc * 128:(fc + 1) * 128], hTm[:, c, 0:N], c == 0, c == 7, [b_hTm, b_W], [b_ga])
                    for c in range(8):
                        mm(gb[:, 0:N], Wg[:, c, 1024 + fc * 128:1024 + (fc + 1) * 128], hTm[:, c, 0:N], c == 0, c == 7, [b_hTm, b_W], [b_gb])
                    for p in range(4):
                        mm(yap[:, 0:N], wof[:, p, fc * 128:(fc + 1) * 128], yaT[:, p, qtok0:qtok0 + N], p == 0, p == 3, [b_W], [b_yap])
                    for p in range(4):
                        mm(ybp[:, 0:N], wod[:, p, fc * 128:(fc + 1) * 128], ybT[:, p, qtok0:qtok0 + N], p == 0, p == 3, [b_W], [b_ybp])
                    sa, b_sa = sarot.next()
                    sb, b_sb = sbrot.next()
                    act(sa[:, 0:N], ga[:, 0:N], AF.Sigmoid, [b_ga], [b_sa])
                    act(sb[:, 0:N], gb[:, 0:N], AF.Sigmoid, [b_gb], [b_sb])
                    fw.op("dve", lambda: nc.vector.tensor_tensor(sa[:, 0:N], yap[:, 0:N], sa[:, 0:N], ALU.mult),
                          reads=[b_yap, b_sa], writes=[b_sa])
                    fw.op("dve", lambda: nc.vector.tensor_tensor(sb[:, 0:N], ybp[:, 0:N], sb[:, 0:N], ALU.mult),
                          reads=[b_ybp, b_sb], writes=[b_sb])
                    fw.op("pool", lambda fc=fc: nc.gpsimd.tensor_tensor(mixT[:, fc, 0:N], sa[:, 0:N], sb[:, 0:N], ALU.add),
                          reads=[b_sa, b_sb], writes=[b_mix])
                for bi in range(nb):
                    blk = b0 + bi
                    for half in range(2):
                        for fc in range(8):
                            mm(yps[:, half * 512:(half + 1) * 512], mixT[:, fc, bi * 128:(bi + 1) * 128],
                               wout[:, fc, half * 512:(half + 1) * 512], fc == 0, fc == 7, [b_mix, b_W], [b_yps])
                    st, b_st = rstd_from(nr, yps[:, :], [b_yps])
                    xt, b_xt = nr.xrot.next()
                    fw.dma("sp", xt[:], xk[blk * 128:(blk + 1) * 128, :], writes=[b_xt])
                    fw.op("dve", lambda st=st: nc.vector.scalar_tensor_tensor(tmp[:], yps[:, :], st[:, 2:3], gB1[:], ALU.mult, ALU.mult),
                          reads=[b_yps, b_st, b_W], writes=[b_tmp])
                    x1t, b_x1t = x1rot.next()
                    fw.op("pool", lambda xt=xt, x1t=x1t: nc.gpsimd.tensor_tensor(x1t[:], tmp[:], xt[:], ALU.add),
                          reads=[b_tmp, b_xt], writes=[b_x1t])
                    fw.dma("sp", x1s[(blk - 16) * 128:(blk - 15) * 128, :], x1t[:], reads=[b_x1t], writes=[b_x1s[blk - 16]])
                    if debug:
                        out_toks.append(fw.dma("sp", dbg["x1"][(blk - 16) * 128:(blk - 15) * 128, :], x1t[:], reads=[b_x1t]))
            fw.barrier()
        att.close()

        with ExitStack() as ph:
            Wup = T(ph, "Wup", [128, 8, 2 * DFF], BF16)
            Wd = T(ph, "Wd", [128, 22, D], BF16)
            gB2 = T(ph, "gB2", [128, D], F32)
            cw = T(ph, "cw", [128, 3 * NFC], F32)
            cb = T(ph, "cb", [128, NFC], F32)
            b_W = Buf()
            for c in range(8):
                for q4 in range(4):
                    fw.dma("pool", Wup[:, c, q4 * 1408:(q4 + 1) * 1408], w_up_v[:, c, q4 * 1408:(q4 + 1) * 1408], writes=[b_W], par=True)
            loadw(Wd, w_down_v, 0, D, 22, b_W)
            fw.dma("sp", gB2[:], gB2_d, writes=[b_W], par=True)
            fw.dma("sp", cw[:], cw_d, writes=[b_W], par=True)
            fw.dma("sp", cb[:], cb_d, writes=[b_W], par=True)
            tr = P(ph, "tr", [128, 1024])
            urot = Rot([(P(ph, "u%d" % i, [128, 512]), Buf()) for i in range(4)])
            yps = P(ph, "yps", [128, 1024])
            b_yps = Buf()
            nr = make_norm_res(ph, tr)
            h2T = T(ph, "h2T", [128, 8, 512], BF16)
            b_h2T = Buf()
            h2h = T(ph, "h2h", [128, 8, 128], BF16)
            b_h2h = Buf()
            mT = T(ph, "mT", [128, 22, 512], BF16)
            b_mT = Buf()
            carry = T(ph, "carry", [128, NFC, 2], F32)
            b_carry = Buf()
            yarot = Rot([(T(ph, "Ya%d" % i, [128, 512], F32), Buf()) for i in range(3)])
            ybrot = Rot([(T(ph, "Yb%d" % i, [128, 512], F32), Buf()) for i in range(3)])
            sqrot = Rot([(T(ph, "sq%d" % i, [128, 512], F32), Buf()) for i in range(2)])
            orot = Rot([(T(ph, "ot%d" % i, [128, D], F32), Buf()) for i in range(1)])
            norm_transpose(nr, x1s[0:128, :], [b_x1s[0]], h2h, b_h2h, 0, gT2)
            for fc in range(NFC):
                ps, b_ps = urot.next()
                for c in range(8):
                    mm(ps[:, 0:2], Wup[:, c, fc * 128:(fc + 1) * 128], h2h[:, c, 126:128], c == 0, c == 7, [b_h2h, b_W], [b_ps])
                fw.op("dve", lambda ps=ps, fc=fc: nc.vector.tensor_scalar(carry[:, fc, :], ps[:, 0:2], hflag[:, 0:1], None, ALU.mult),
                      reads=[b_ps], writes=[b_carry], small=True)
            for ci in range(4):
                for bi in range(4):
                    r0 = (1 + 4 * ci + bi) * 128
                    norm_transpose(nr, x1s[r0:r0 + 128, :], [b_x1s[1 + 4 * ci + bi]], h2T, b_h2T, bi * 128, gT2)
                pend_a = []
                pend_b = []

                def stage1(f):
                    Ys = []
                    for which, fc in ((0, f), (1, 22 + f)):
                        ps, b_ps = urot.next()
                        for c in range(8):
                            mm(ps[:, :], Wup[:, c, fc * 128:(fc + 1) * 128], h2T[:, c, :], c == 0, c == 7, [b_h2T, b_W], [b_ps])
                        Y, b_Y = (yarot if which == 0 else ybrot).next()
                        act(Y[:, :], ps[:, :], AF.Identity, [b_ps, b_W], [b_Y],
                            scale=cw[:, 2 * NFC + fc:2 * NFC + fc + 1], bias=cb[:, fc:fc + 1])
                        w1 = cw[:, NFC + fc:NFC + fc + 1]
                        w0 = cw[:, fc:fc + 1]
                        fw.op("dve", lambda: nc.vector.scalar_tensor_tensor(
                            Y[:, 1:512], ps[:, 0:511], w1, Y[:, 1:512], ALU.mult, ALU.add), reads=[b_ps, b_Y], writes=[b_Y])
                        fw.op("dve", lambda: nc.vector.scalar_tensor_tensor(
                            Y[:, 2:512], ps[:, 0:510], w0, Y[:, 2:512], ALU.mult, ALU.add), reads=[b_ps, b_Y], writes=[b_Y])
                        fw.op("dve", lambda: nc.vector.scalar_tensor_tensor(
                            Y[:, 0:1], carry[:, fc, 1:2], w1, Y[:, 0:1], ALU.mult, ALU.add), reads=[b_carry, b_Y], writes=[b_Y], small=True)
                        fw.op("dve", lambda: nc.vector.scalar_tensor_tensor(
                            Y[:, 0:2], carry[:, fc, 0:2], w0, Y[:, 0:2], ALU.mult, ALU.add), reads=[b_carry, b_Y], writes=[b_Y], small=True)
                        fw.op("dve", lambda: nc.vector.tensor_copy(carry[:, fc, :], ps[:, 510:512]),
                              reads=[b_ps], writes=[b_carry], small=True)
                        Ys.append((Y, b_Y))
                    pend_a.append((f, Ys))

                def stage2a(f, Ys):
                    (Ya, b_Ya), (Yb, b_Yb) = Ys
                    sq, b_sq = sqrot.next()
                    act(sq[:, :], Ya[:, :], AF.Gelu_apprx_tanh, [b_Ya], [b_sq])
                    fw.op("pool", lambda: nc.gpsimd.tensor_tensor(mT[:, f, :], sq[:, :], Yb[:, :], ALU.mult),
                          reads=[b_sq, b_Yb], writes=[b_mT])

                for it in range(22 + 1):
                    if it < 22:
                        stage1(it)
                    if len(pend_a) > 0 and (it >= 1):
                        stage2a(*pend_a.pop(0))
                while pend_a:
                    stage2a(*pend_a.pop(0))
                for bi in range(4):
                    lb = 4 * ci + bi
                    for half in range(2):
                        for f in range(22):
                            mm(yps[:, half * 512:(half + 1) * 512], mT[:, f, bi * 128:(bi + 1) * 128],
                               Wd[:, f, half * 512:(half + 1) * 512], f == 0, f == 21, [b_mT, b_W], [b_yps])
                    st, b_st = rstd_from(nr, yps[:, :], [b_yps])
                    xt, b_xt = nr.xrot.next()
                    fw.dma("sp", xt[:], x1s[(1 + lb) * 128:(2 + lb) * 128, :], reads=[b_x1s[1 + lb]], writes=[b_xt])
                    ot, b_ot = orot.next()
                    fw.op("dve", lambda st=st, ot=ot: nc.vector.scalar_tensor_tensor(ot[:], yps[:, :], st[:, 2:3], gB2[:], ALU.mult, ALU.mult),
                          reads=[b_yps, b_st, b_W], writes=[b_ot])
                    fw.op("pool", lambda xt=xt, ot=ot: nc.gpsimd.tensor_tensor(ot[:], ot[:], xt[:], ALU.add),
                          reads=[b_ot, b_xt], writes=[b_ot])
                    out_toks.append(fw.dma("sp", y_out[lb * 128:(lb + 1) * 128, :], ot[:], reads=[b_ot]))
            for t in out_toks:
                fw._wait("sp", t)
            fw.barrier()
    return nc


def _constants():
    c = {}
    k = np.arange(128)[:, None]
    q = np.arange(512)[None, :]
    mmask = np.zeros((20, 128, 512), np.float32)
    for idx in range(20):
        rel = idx - 3
        dlt = rel * 128 + q - k
        m = ((dlt >= 0) & (dlt <= 128)).astype(np.float32)
        m += ((dlt >= 0) & (dlt <= 512) & (dlt % 4 == 0)).astype(np.float32)
        m += ((dlt >= 0) & (dlt <= 2048) & (dlt % 16 == 0)).astype(np.float32)
        mmask[idx] = m
    c["mmask"] = mmask
    kk = np.arange(128)[:, None]
    qq = np.arange(128)[None, :]
    c["causal"] = np.where(kk <= qq, 0.0, -240000.0).astype(np.float32)
    c["ident"] = np.eye(128, dtype=np.float32)
    sel = np.zeros((8, 8, 128), np.float32)
    for h in range(8):
        sel[h, h, :] = 8.0
    c["sel"] = sel.reshape(8, 8 * 128)
    return c


def _rope_tables(base):
    half = 8
    inv_freq = (np.float32(500000.0) ** (-(np.arange(half, dtype=np.float32) * np.float32(2.0) / np.float32(16.0)))).astype(np.float32)
    pos = np.maximum(np.arange(NTOK) + base, 0).astype(np.float32)
    ang = (pos[:, None] * inv_freq[None, :]).astype(np.float32)
    cos = np.cos(ang.astype(np.float64)).astype(np.float32).T
    sin = np.sin(ang.astype(np.float64)).astype(np.float32).T
    C = np.ones((128, NTOK), np.float32)
    S = np.zeros((128, NTOK), np.float32)
    for a in range(2):
        C[a * 64:a * 64 + 8] = cos
        C[a * 64 + 8:a * 64 + 16] = cos
        S[a * 64:a * 64 + 8] = -sin
        S[a * 64 + 8:a * 64 + 16] = sin
    return C, S


_PROG = {}


def kernel(x, g_pre_mix, w_in, b_forget, w_o_fox, w_o_dil, w_out, g_post_mix,
           g_pre_ffn, w_up, conv_w, conv_b, w_down, g_post_ffn, _debug=False):
    f32 = np.float32
    x = np.asarray(x, f32)
    B, S, _ = x.shape
    consts = _constants()
    shared = {
        "w_in": np.ascontiguousarray(np.asarray(w_in, f32)[0]),
        "w_o_fox": np.ascontiguousarray(np.asarray(w_o_fox, f32)[0]),
        "w_o_dil": np.ascontiguousarray(np.asarray(w_o_dil, f32)[0]),
        "w_out": np.ascontiguousarray(np.asarray(w_out, f32)[0]),
        "w_up": np.ascontiguousarray(np.asarray(w_up, f32)[0]),
        "w_down": np.ascontiguousarray(np.asarray(w_down, f32)[0]),
        "gT1": np.ascontiguousarray(np.asarray(g_pre_mix, f32)[0].reshape(8, 128).T),
        "gT2": np.ascontiguousarray(np.asarray(g_pre_ffn, f32)[0].reshape(8, 128).T),
        "gB1": np.ascontiguousarray(np.broadcast_to(np.asarray(g_post_mix, f32)[0][None, :], (128, D))),
        "gB2": np.ascontiguousarray(np.broadcast_to(np.asarray(g_post_ffn, f32)[0][None, :], (128, D))),
        "bf": np.ascontiguousarray(np.asarray(b_forget, f32)[0].reshape(8, 1)),
        "cw": np.ascontiguousarray(np.asarray(conv_w, f32)[0].reshape(3, NFC, 128).transpose(2, 0, 1).reshape(128, 3 * NFC)),
        "cb": np.ascontiguousarray(np.asarray(conv_b, f32)[0].reshape(NFC, 128).T),
    }
    shared.update(consts)
    in_maps = []
    for core in range(8):
        b, h = core // 2, core % 2
        base = 2048 * h - 2176
        xkc = np.zeros((NTOK, D), f32)
        lo = max(0, -base)
        xkc[lo:] = x[b, base + lo:base + NTOK]
        tok = np.arange(NTOK) + base
        valid = tok >= 0
        mbo = np.where(valid, 0.0, -30000.0).astype(f32)
        halo_valid = valid.copy()
        halo_valid[16 * 128:17 * 128] = True
        mbh = np.where(halo_valid, 0.0, -30000.0).astype(f32)
        C, Sn = _rope_tables(base)
        m = dict(shared)
        m["xk"] = xkc
        m["mb_own"] = np.ascontiguousarray(mbo.reshape(NBK, 128).T)
        m["mb_halo"] = np.ascontiguousarray(mbh.reshape(NBK, 128).T)
        m["ropeC"] = C
        m["ropeS"] = Sn
        m["hflag"] = np.full((128, 1), float(h), f32)
        in_maps.append(m)
    key = bool(_debug)
    if key not in _PROG:
        _PROG[key] = build_program(debug=key)
    nc = _PROG[key]
    res = run_bass_kernel_spmd(nc, in_maps, core_ids=list(range(8)))
    out = np.zeros((B, S, D), f32)
    for core in range(8):
        b, h = core // 2, core % 2
        out[b, 2048 * h:2048 * (h + 1)] = res.results[core]["y"]
    if _debug:
        return out, res.results
    return out
```

```python
import numpy as np
from contextlib import ExitStack
import concourse.bass as bass
import concourse.mybir as mybir
from concourse.bass_utils import run_bass_kernel_spmd

F32 = mybir.dt.float32
BF16 = mybir.dt.bfloat16
AF = mybir.ActivationFunctionType
ALU = mybir.AluOpType
NDSEM = 40

D = 1024
NBK = 33
NTOK = NBK * 128
KCH = [(0, 4), (4, 4), (8, 4), (12, 4), (16, 1), (17, 4), (21, 4), (25, 4), (29, 4)]
QCH = KCH[4:]
NQT = 17 * 128
OFF = dict(qa=0, ka=512, va=1024, fa=1536, qb=1544, kb=2056, vb=2568, ga=3080, gb=4104)
DFF = 2816
NFC = 44
EPS = 1e-6


class Buf:
    __slots__ = ("name", "w", "r", "wsmall", "wl")

    def __init__(self, name=""):
        self.name = name
        self.w = None
        self.r = {}
        self.wsmall = False
        self.wl = []


SKIP_SAME_ENGINE_RAW = True


class Rot:
    def __init__(self, items):
        self.items = items
        self.i = 0

    def next(self):
        it = self.items[self.i]
        self.i = (self.i + 1) % len(self.items)
        return it


class FW:
    ENG = ("pe", "act", "dve", "pool", "sp")

    def __init__(self, nc, es):
        self.nc = nc
        self.e = {"pe": nc.tensor, "act": nc.scalar, "dve": nc.vector,
                  "pool": nc.gpsimd, "sp": nc.sync}
        self.sem = {k: es.enter_context(nc.semaphore("s_" + k)) for k in self.ENG}
        self.cnt = {k: 0 for k in self.ENG}
        self.waited = {}
        self.dsems = [es.enter_context(nc.semaphore("d%d" % i)) for i in range(NDSEM)]
        self.dcnt = [0] * NDSEM
        self.dnext = 0

    def _wait(self, eng, tok):
        kind, key, val = tok
        wk = (eng, kind, key)
        if self.waited.get(wk, 0) >= val:
            return
        sem = self.sem[key] if kind == "e" else self.dsems[key]
        self.e[eng].wait_ge(sem, val)
        self.waited[wk] = val

    def _deps(self, eng, reads, writes, par=False):
        for b in reads:
            for tok in b.wl:
                self._wait(eng, tok)
            if b.w is not None:
                if (SKIP_SAME_ENGINE_RAW and b.w[0] == "e" and b.w[1] == eng and not b.wsmall
                        and eng != "pool"):
                    continue
                self._wait(eng, b.w)
        for b in writes:
            if par:
                continue
            for tok in b.wl:
                self._wait(eng, tok)
            if b.w is not None and not (b.w[0] == "e" and b.w[1] == eng):
                self._wait(eng, b.w)
            for tok in b.r.values():
                if tok[0] == "e" and tok[1] == eng:
                    continue
                self._wait(eng, tok)

    def op(self, eng, fn, reads=(), writes=(), inc=True, small=False):
        self._deps(eng, reads, writes)
        ins = fn()
        if inc:
            self.cnt[eng] += 1
            ins.then_inc(self.sem[eng], 1)
            tok = ("e", eng, self.cnt[eng])
        else:
            tok = ("e", eng, self.cnt[eng] + 1)
        for b in reads:
            b.r[("e", eng)] = tok
        for b in writes:
            b.w = tok
            b.wl = []
            b.r = {}
            b.wsmall = small
        return tok

    def dma(self, q, out_ap, in_ap, reads=(), writes=(), par=False):
        self._deps(q, reads, writes, par=par)
        if par:
            for b in writes:
                for tok in b.r.values():
                    self._wait(q, tok)
        i = self.dnext
        self.dnext = (i + 1) % NDSEM
        if self.dcnt[i] > 0:
            self._wait(q, ("d", i, self.dcnt[i]))
        self.dcnt[i] += 16
        self.e[q].dma_start(out=out_ap, in_=in_ap).then_inc(self.dsems[i], 16)
        tok = ("d", i, self.dcnt[i])
        for b in reads:
            b.r[("d", i)] = tok
        for b in writes:
            if par:
                b.wl.append(tok)
            else:
                b.w = tok
                b.wl = []
            b.r = {}
        return tok

    def barrier(self):
        import os
        if os.environ.get("KDEBUG"):
            print("barrier: sbuf remaining", self.nc.sbuf_bytes_remaining, "cnt", dict(self.cnt))
        for k in ("pe", "act", "dve", "pool"):
            if self.cnt[k] > 0:
                self._wait("sp", ("e", k, self.cnt[k]))
        for i in range(NDSEM):
            if self.dcnt[i] > 0:
                self._wait("sp", ("d", i, self.dcnt[i]))
        self.cnt["sp"] += 1
        self.e["sp"].nop().then_inc(self.sem["sp"], 1)
        for k in ("pe", "act", "dve", "pool"):
            self._wait(k, ("e", "sp", self.cnt["sp"]))


def build_program(debug=False):
    nc = bass.Bass("TRN2", target_bir_lowering=False)

    def din(name, shape):
        return nc.dram_tensor(name, shape, F32, kind="ExternalInput").ap()

    xk = din("xk", [NTOK, D])
    w_in = din("w_in", [D, 5128])
    w_o_fox = din("w_o_fox", [512, D])
    w_o_dil = din("w_o_dil", [512, D])
    w_out = din("w_out", [D, D])
    w_up = din("w_up", [D, 2 * DFF])
    w_down = din("w_down", [DFF, D])
    gT1_d = din("gT1", [128, 8])
    gT2_d = din("gT2", [128, 8])
    gB1_d = din("gB1", [128, D])
    gB2_d = din("gB2", [128, D])
    bf_d = din("bf", [8, 1])
    cw_d = din("cw", [128, 3 * NFC])
    cb_d = din("cb", [128, NFC])
    mbo_d = din("mb_own", [128, NBK])
    mbh_d = din("mb_halo", [128, NBK])
    ropeC_d = din("ropeC", [128, NTOK])
    ropeS_d = din("ropeS", [128, NTOK])
    mm_d = din("mmask", [20, 128, 512])
    causal_d = din("causal", [128, 128])
    ident_d = din("ident", [128, 128])
    sel_d = din("sel", [8, 8 * 128])
    hflag_d = din("hflag", [128, 1])
    y_out = nc.dram_tensor("y", [2048, D], F32, kind="ExternalOutput").ap()
    x1s = nc.dram_tensor("x1s", [NQT, D], F32).ap()
    dbg = {}
    if debug:
        dbg["ya"] = nc.dram_tensor("dbg_ya", [128, 4, NQT], BF16, kind="ExternalOutput").ap()
        dbg["yb"] = nc.dram_tensor("dbg_yb", [128, 4, NQT], BF16, kind="ExternalOutput").ap()
        dbg["x1"] = nc.dram_tensor("dbg_x1", [NQT, D], F32, kind="ExternalOutput").ap()

    w_in_v = w_in.rearrange("(c p) n -> p c n", p=128)
    w_up_v = w_up.rearrange("(c p) n -> p c n", p=128)
    w_down_v = w_down.rearrange("(c p) n -> p c n", p=128)
    w_out_v = w_out.rearrange("(c p) n -> p c n", p=128)
    wof_v = w_o_fox.rearrange("(c p) n -> p c n", p=128)
    wod_v = w_o_dil.rearrange("(c p) n -> p c n", p=128)

    out_toks = []
    with ExitStack() as es:
        fw = FW(nc, es)

        uid = [0]

        def T(stack, name, shape, dt):
            uid[0] += 1
            return stack.enter_context(nc.sbuf_tensor("sb%d_%s" % (uid[0], name), shape, dt))

        def P(stack, name, shape, dt=F32):
            uid[0] += 1
            return stack.enter_context(nc.psum_tensor("ps%d_%s" % (uid[0], name), shape, dt))

        def mm(out, lhsT, rhs, start, stop, reads, writes, inc=None):
            return fw.op("pe", lambda: nc.tensor.matmul(out, lhsT, rhs, start=start, stop=stop),
                         reads, writes, inc=(stop if inc is None else inc))

        def act(out, in_, func, reads, writes, **kw):
            return fw.op("act", lambda: nc.scalar.activation(out, in_, func, **kw), reads, writes)

        def loadw(dst, src_v, col0, ncols, nchunks, b):
            for c in range(nchunks):
                fw.dma("pool", dst[:, c, 0:ncols], src_v[:, c, col0:col0 + ncols], writes=[b], par=True)

        identf = T(es, "identf", [128, 128], F32)
        identb = T(es, "identb", [128, 128], BF16)
        causal = T(es, "causal", [128, 128], BF16)
        ones32 = T(es, "ones32", [128, 64], F32)
        gT1 = T(es, "gT1", [128, 8], F32)
        gT2 = T(es, "gT2", [128, 8], F32)
        mbo = T(es, "mbo", [128, NBK], F32)
        mbh = T(es, "mbh", [128, NBK], F32)
        hflag = T(es, "hflag", [128, 1], F32)
        bfT = T(es, "bfT", [8, 1], F32)
        b_const = Buf("const")
        for dst, src in ((identf, ident_d), (gT1, gT1_d), (gT2, gT2_d), (mbo, mbo_d),
                         (mbh, mbh_d), (hflag, hflag_d), (bfT, bf_d)):
            fw.dma("sp", dst[:], src, writes=[b_const])
        fw.dma("pool", identb[:], ident_d, writes=[b_const])
        fw.dma("pool", causal[:], causal_d, writes=[b_const])
        fw.op("dve", lambda: nc.vector.memset(ones32[:], 1.0), writes=[b_const])

        fw.barrier()

        att = ExitStack()
        ybT = T(att, "ybT", [128, 4, NQT], BF16)
        yaT = T(att, "yaT", [128, 4, NQT], BF16)

        class NormRes:
            pass

        def make_norm_res(stack, tr_ps):
            r = NormRes()
            r.xrot = Rot([(T(stack, "xt%d" % i, [128, D], F32), Buf()) for i in range(2)])
            r.junk = T(stack, "junk", [128, D], BF16)
            r.b_junk = Buf()
            r.strot = Rot([(T(stack, "st%d" % i, [128, 4], F32), Buf()) for i in range(2)])
            r.tr = tr_ps
            r.b_tr = Buf()
            return r

        def rstd_from(r, src_ap, src_bufs):
            st, b_st = r.strot.next()
            fw.op("act", lambda: nc.scalar.activation(r.junk[:], src_ap, AF.Square, accum_out=st[:, 0:1]),
                  reads=src_bufs, writes=[r.b_junk, b_st], small=True)
            fw.op("act", lambda: nc.scalar.activation(st[:, 1:2], st[:, 0:1], AF.Sqrt, scale=1.0 / D, bias=EPS),
                  reads=[b_st], writes=[b_st], small=True)
            fw.op("dve", lambda: nc.vector.reciprocal(st[:, 2:3], st[:, 1:2]), reads=[b_st], writes=[b_st], small=True)
            return st, b_st

        def norm_transpose(r, src_rows_ap, src_bufs, hT, b_hT, col0, gT):
            xt, b_xt = r.xrot.next()
            fw.dma("sp", xt[:], src_rows_ap, reads=src_bufs, writes=[b_xt])
            st, b_st = rstd_from(r, xt[:], [b_xt])
            fw.op("dve", lambda: nc.vector.tensor_scalar(xt[:], xt[:], st[:, 2:3], None, ALU.mult),
                  reads=[b_xt, b_st], writes=[b_xt])
            for c in range(8):
                fw.op("pe", lambda c=c: nc.tensor.transpose(r.tr[:, c * 128:(c + 1) * 128], xt[:, c * 128:(c + 1) * 128], identf[:]),
                      reads=[b_xt], writes=[r.b_tr], inc=(c == 7))
            fw.op("dve", lambda: nc.vector.tensor_tensor(
                hT[:, :, col0:col0 + 128], r.tr[:, :].rearrange("p (c t) -> p c t", c=8),
                gT[:, :].unsqueeze(2).to_broadcast([128, 8, 128]), ALU.mult),
                reads=[r.b_tr], writes=[b_hT])
            return xt, b_xt

        def pipelined_chunks(chunks, nr, hrot, gT):
            def prep(ch):
                b0, nb = ch
                hT, b_hT = hrot.next()
                for bi in range(nb):
                    norm_transpose(nr, xk[(b0 + bi) * 128:(b0 + bi + 1) * 128, :], [], hT, b_hT, bi * 128, gT)
                return (b0, nb, hT, b_hT)
            cur = prep(chunks[0])
            for i in range(len(chunks)):
                nxt = prep(chunks[i + 1]) if i + 1 < len(chunks) else None
                yield cur
                cur = nxt

        def proj_tokmajor_V(pjrot, hT, b_hT, nblk, W, b_W, Vt, blk0):
            for bi in range(nblk):
                ps, b_ps = pjrot.next()
                for c in range(8):
                    mm(ps[:, 0:512], hT[:, c, bi * 128:(bi + 1) * 128], W[:, c, 0:512], c == 0, c == 7, [b_hT, b_W], [b_ps])
                fw.op("dve", lambda ps=ps, bi=bi: nc.vector.tensor_copy(
                    Vt[:, blk0 + bi, :, 0:64], ps[:, 0:512].rearrange("p (h d) -> p h d", h=8)),
                    reads=[b_ps], writes=[])

        def attention(kind, q0blk, nqb, QT, b_QT, KT, Vt, yT, res, FT=None, biasK=None, mb=None, MM=None):
            N = nqb * 128
            qtok0 = (q0blk - 16) * 128
            if kind == "fox":
                kbs = list(range(0, q0blk + nqb))
            else:
                kbs = list(range(max(0, q0blk - 16), q0blk + nqb))
            nk = len(kbs)
            L = 4
            tiles = [(h, i, kb) for h in range(8) for i, kb in enumerate(kbs)]
            fq = {}
            ot = {}
            pvbuf = {}
            deferred = []

            def emit_fq(h):
                ps, b_ps = res.pjrot.next()
                mm(ps[:, 0:N], res.sel[:, h * 128:(h + 1) * 128], FT[0:8, q0blk * 128:q0blk * 128 + N], True, True, [], [b_ps])
                fqb, b_fqb = res.fqrot.next()
                act(fqb[:, 0:N], ps[:, 0:N], AF.Copy, [b_ps], [b_fqb])
                fq[h] = (fqb, b_fqb)

            def stage_a(t):
                h, i, kb = tiles[t]
                p, a = h // 2, h % 2
                rs = slice(a * 64, (a + 1) * 64)
                if kind == "fox" and i == 0 and h + 1 < 8:
                    emit_fq(h + 1)
                j = kb - q0blk
                c0 = max(0, j) * 128
                diag = (j >= 0) and kind == "fox"
                S, b_S = res.srot.next()
                mm(S[:, c0:N], KT[:, p, kb * 128:(kb + 1) * 128], QT[a][:, p, c0:N], True, not diag, [b_QT], [b_S])
                if diag:
                    mm(S[:, c0:c0 + 128], identb[:], causal[:], False, True, [], [b_S])
                pt, b_pt = res.ptrot.next()
                if kind == "fox":
                    fqb, b_fqb = fq[h]
                    ssb, b_ssb = res.ssrot.next()
                    fw.op("dve", lambda: nc.vector.tensor_tensor(
                        ssb[:, c0:N], S[:, c0:N], fqb[:, c0:N], ALU.add), reads=[b_S, b_fqb], writes=[b_ssb])
                    act(pt[:, c0:N], ssb[:, c0:N], AF.Exp, [b_ssb], [b_pt], scale=0.125, bias=biasK[:, kb, h:h + 1])
                    pvbuf[t] = (pt, b_pt, c0)
                else:
                    act(pt[:, c0:N], S[:, c0:N], AF.Exp, [b_S], [b_pt], scale=0.125, bias=mb[:, kb:kb + 1])
                    pm, b_pm = res.pmrot.next()
                    rel = q0blk - kb + 3
                    if t % 3 == 2:
                        fw.op("pool", lambda: nc.gpsimd.tensor_tensor(
                            pm[:, c0:N], pt[:, c0:N], MM[:, rel, c0:N], ALU.mult), reads=[b_pt], writes=[b_pm])
                    else:
                        fw.op("dve", lambda: nc.vector.tensor_tensor(
                            pm[:, c0:N], pt[:, c0:N], MM[:, rel, c0:N], ALU.mult), reads=[b_pt], writes=[b_pm])
                    pvbuf[t] = (pm, b_pm, c0)

            def stage_b(t, step):
                h, i, kb = tiles[t]
                p, a = h // 2, h % 2
                rs = slice(a * 64, (a + 1) * 64)
                if i == 0:
                    ot[h] = res.otrot.next()
                oT, b_oT = ot[h]
                pv, b_pv, c0 = pvbuf.pop(t)
                vo = (kb * 8 + h) * 65
                mm(oT[:, c0:N], Vt[:, vo:vo + 128], pv[:, c0:N], i == 0, i == nk - 1, [b_pv], [b_oT])
                if i == nk - 1:
                    rec, b_rec = res.recrot.next()
                    ots, b_ots = res.otsrot.next()
                    act(ots[0:65, 0:N], oT[0:65, 0:N], AF.Copy, [b_oT], [b_ots])
                    if kind == "fox":
                        act(rec[64:65, 0:N], ots[64:65, 0:N], AF.Ln, [b_ots], [b_rec])
                        act(rec[64:65, 0:N], rec[64:65, 0:N], AF.Exp, [b_rec], [b_rec], scale=-1.0)
                    else:
                        fw.op("dve", lambda: nc.vector.reciprocal(rec[64:65, 0:N], ots[64:65, 0:N]),
                              reads=[b_ots], writes=[b_rec])

                    def part2():
                        R, b_R = res.pjrot.next()
                        mm(R[0:64, 0:N], ones32[64:65, 0:64], rec[64:65, 0:N], True, True, [b_rec], [b_R])
                        fw.op("dve", lambda: nc.vector.tensor_tensor(
                            yT[rs, p, qtok0:qtok0 + N], R[0:64, 0:N], ots[0:64, 0:N], ALU.mult),
                            reads=[b_R, b_ots], writes=[])
                    deferred.append((step + 5, part2))

            if kind == "fox":
                emit_fq(0)
            nt = len(tiles)
            for step in range(nt + L):
                if step < nt:
                    stage_a(step)
                if step - L >= 0:
                    stage_b(step - L, step)
                while deferred and deferred[0][0] <= step:
                    deferred.pop(0)[1]()
            while deferred:
                deferred.pop(0)[1]()

        class Res:
            pass

        with ExitStack() as ph:
            KdT = T(ph, "KdT", [128, 4, NTOK], BF16)
            Vd_flat = T(ph, "Vd", [128, NBK * 520 + 64], BF16)
            Vd = Vd_flat[:, 0:NBK * 520].rearrange("p (b h d) -> p b h d", b=NBK, h=8)
            fw.op("pool", lambda: nc.gpsimd.memset(Vd_flat[:, NBK * 520:NBK * 520 + 64], 0.0), writes=[])
            MM = T(ph, "MM", [128, 20, 512], BF16)
            b_mm = Buf()
            Wq = T(ph, "Wq", [128, 8, 512], BF16)
            Wqr = T(ph, "Wqr", [128, 8, 512], BF16)
            b_Wq = Buf()
            fw.op("pool", lambda: nc.gpsimd.memset(Vd[:, :, :, 64:65], 1.0), writes=[])
            hrot = Rot([(T(ph, "hT%d" % i, [128, 8, 512], BF16), Buf()) for i in range(2)])
            crot = Rot([(T(ph, "Ct%d" % i, [128, 512], F32), Buf()) for i in range(2)])
            srot_t = Rot([(T(ph, "St%d" % i, [128, 512], F32), Buf()) for i in range(2)])
            t1rot = Rot([(T(ph, "t1_%d" % i, [128, 512], F32), Buf()) for i in range(1)])
            t2rot = Rot([(T(ph, "t2_%d" % i, [128, 512], F32), Buf()) for i in range(1)])

            def rope_proj(pjrot, hT, b_hT, N, W, Wr, b_W, dstT, dcol0, tok0, b_dst_list, dstB=None):
                Ct, b_Ct = crot.next()
                St, b_St = srot_t.next()
                fw.dma("sp", Ct[:, 0:N], ropeC_d[:, tok0:tok0 + N], writes=[b_Ct])
                fw.dma("sp", St[:, 0:N], ropeS_d[:, tok0:tok0 + N], writes=[b_St])
                for p in range(4):
                    psA, b_A = pjrot.next()
                    psB, b_B = pjrot.next()
                    for c in range(8):
                        mm(psA[:, 0:N], W[:, c, p * 128:(p + 1) * 128], hT[:, c, 0:N], c == 0, c == 7, [b_hT, b_W], [b_A])
                    for c in range(8):
                        mm(psB[:, 0:N], Wr[:, c, p * 128:(p + 1) * 128], hT[:, c, 0:N], c == 0, c == 7, [b_hT, b_W], [b_B])
                    t1, b_t1 = t1rot.next()
                    t2, b_t2 = t2rot.next()
                    fw.op("dve", lambda: nc.vector.tensor_tensor(t1[:, 0:N], psA[:, 0:N], Ct[:, 0:N], ALU.mult),
                          reads=[b_A, b_Ct], writes=[b_t1])
                    fw.op("dve", lambda: nc.vector.tensor_tensor(t2[:, 0:N], psB[:, 0:N], St[:, 0:N], ALU.mult),
                          reads=[b_B, b_St], writes=[b_t2])
                    if dstB is None:
                        fw.op("pool", lambda p=p: nc.gpsimd.tensor_tensor(dstT[:, p, dcol0:dcol0 + N], t1[:, 0:N], t2[:, 0:N], ALU.add),
                              reads=[b_t1, b_t2], writes=b_dst_list)
                    else:
                        fw.op("pool", lambda p=p: nc.gpsimd.tensor_tensor(dstT[0:64, p, dcol0:dcol0 + N], t1[0:64, 0:N], t2[0:64, 0:N], ALU.add),
                              reads=[b_t1, b_t2], writes=b_dst_list)
                        fw.op("pool", lambda p=p: nc.gpsimd.tensor_tensor(dstB[64:128, p, dcol0:dcol0 + N], t1[64:128, 0:N], t2[64:128, 0:N], ALU.add),
                              reads=[b_t1, b_t2], writes=b_dst_list)

            def make_rot_w(W, Wr, b_W):
                fw.op("pool", lambda: nc.gpsimd.memset(Wr[:], 0.0), writes=[b_W])
                Wv = W[:, :, :].rearrange("p c (h d) -> p c h d", h=8)
                Wrv = Wr[:, :, :].rearrange("p c (h d) -> p c h d", h=8)
                for c in range(8):
                    fw.op("pool", lambda c=c: nc.gpsimd.tensor_copy(Wrv[:, c, :, 0:8], Wv[:, c, :, 8:16]), reads=[b_W], writes=[b_W])
                    fw.op("pool", lambda c=c: nc.gpsimd.tensor_copy(Wrv[:, c, :, 8:16], Wv[:, c, :, 0:8]), reads=[b_W], writes=[b_W])

            with ExitStack() as sp1:
                Wk = T(sp1, "Wk", [128, 8, 512], BF16)
                Wkr = T(sp1, "Wkr", [128, 8, 512], BF16)
                Wv = T(sp1, "Wv", [128, 8, 512], BF16)
                b_W = Buf()
                loadw(Wk, w_in_v, OFF["kb"], 512, 8, b_W)
                make_rot_w(Wk, Wkr, b_W)
                loadw(Wv, w_in_v, OFF["vb"], 512, 8, b_W)
                loadw(Wq, w_in_v, OFF["qb"], 512, 8, b_Wq)
                make_rot_w(Wq, Wqr, b_Wq)
                for i in range(20):
                    fw.dma("pool", MM[:, i, :], mm_d[i], writes=[b_mm], par=True)
                tr = P(sp1, "tr", [128, 1024])
                pjrot = Rot([(P(sp1, "pj%d" % i, [128, 512]), Buf()) for i in range(6)])
                nr = make_norm_res(sp1, tr)
                for (b0, nb, hT, b_hT) in pipelined_chunks(KCH, nr, hrot, gT1):
                    N = nb * 128
                    rope_proj(pjrot, hT, b_hT, N, Wk, Wkr, b_W, KdT, b0 * 128, b0 * 128, [])
                    proj_tokmajor_V(pjrot, hT, b_hT, nb, Wv, b_W, Vd, b0)
                fw.barrier()
            with ExitStack() as sp2:
                b_W = b_Wq
                tr = P(sp2, "tr", [128, 1024])
                res = Res()
                res.pjrot = Rot([(P(sp2, "pj%d" % i, [128, 512]), Buf()) for i in range(1)])
                res.srot = Rot([(P(sp2, "S%d" % i, [128, 512]), Buf()) for i in range(3)])
                res.otrot = Rot([(P(sp2, "oT%d" % i, [128, 512]), Buf()) for i in range(2)])
                res.ptrot = Rot([(T(sp2, "pt%d" % i, [128, 512], BF16), Buf()) for i in range(4)])
                res.pmrot = Rot([(T(sp2, "pm%d" % i, [128, 512], BF16), Buf()) for i in range(6)])
                rope_rot = Rot(res.pjrot.items + res.srot.items)
                res.recrot = Rot([(T(sp2, "rec%d" % i, [65, 512], F32), Buf()) for i in range(2)])
                res.otsrot = Rot([(T(sp2, "ots%d" % i, [65, 512], F32), Buf()) for i in range(2)])
                QdA = T(sp2, "QdA", [128, 4, 512], BF16)
                QdB = T(sp2, "QdB", [128, 4, 512], BF16)
                b_QdT = Buf()
                fw.op("pool", lambda: nc.gpsimd.memset(QdA[:], 0.0), writes=[b_QdT])
                fw.op("pool", lambda: nc.gpsimd.memset(QdB[:], 0.0), writes=[b_QdT])
                nr = make_norm_res(sp2, tr)
                for (b0, nb, hT, b_hT) in pipelined_chunks(QCH, nr, hrot, gT1):
                    N = nb * 128
                    rope_proj(rope_rot, hT, b_hT, N, Wq, Wqr, b_W, QdA, 0, b0 * 128, [b_QdT], dstB=QdB)
                    attention("dil", b0, nb, (QdA, QdB), b_QdT, KdT, Vd_flat, ybT, res,
                              mb=(mbh if nb == 1 else mbo), MM=MM)
                fw.barrier()

        with ExitStack() as ph:
            KfT = T(ph, "KfT", [128, 4, NTOK], BF16)
            Vf_flat = T(ph, "Vf", [128, NBK * 520 + 64], BF16)
            Vf = Vf_flat[:, 0:NBK * 520].rearrange("p (b h d) -> p b h d", b=NBK, h=8)
            fw.op("pool", lambda: nc.gpsimd.memset(Vf_flat[:, NBK * 520:NBK * 520 + 64], 0.0), writes=[])
            FT = T(ph, "FT", [8, NTOK], F32)
            biasO = T(ph, "biasO", [128, NBK, 8], F32)
            biasH = T(ph, "biasH", [128, NBK, 8], F32)
            sel = T(ph, "sel", [8, 8 * 128], F32)
            negb = T(ph, "negb", [8, 1], F32)
            onesr = T(ph, "onesr", [8, 512], F32)
            b_c2 = Buf()
            fw.dma("sp", sel[:], sel_d, writes=[b_c2])
            fw.op("dve", lambda: nc.vector.tensor_scalar(negb[:], bfT[:], -1.0, None, ALU.mult), writes=[b_c2])
            fw.op("dve", lambda: nc.vector.memset(onesr[:], 1.0), writes=[b_c2])
            fw.op("pool", lambda: nc.gpsimd.memset(Vf[:, :, :, 64:65], 1.0), writes=[])
            hrot = Rot([(T(ph, "hT%d" % i, [128, 8, 512], BF16), Buf()) for i in range(2)])
            WqF = T(ph, "WqF", [128, 8, 512], BF16)
            b_WqF = Buf()
            with ExitStack() as sp1:
                Wk = T(sp1, "Wk", [128, 8, 512], BF16)
                Wv = T(sp1, "Wv", [128, 8, 512], BF16)
                Wf = T(sp1, "Wf", [128, 8, 8], BF16)
                b_W = Buf()
                loadw(Wk, w_in_v, OFF["ka"], 512, 8, b_W)
                loadw(Wv, w_in_v, OFF["va"], 512, 8, b_W)
                loadw(Wf, w_in_v, OFF["fa"], 8, 8, b_W)
                loadw(WqF, w_in_v, OFF["qa"], 512, 8, b_WqF)
                tr = P(sp1, "tr", [128, 1024])
                pjrot = Rot([(P(sp1, "pj%d" % i, [128, 512]), Buf()) for i in range(6)])
                nr = make_norm_res(sp1, tr)
                elrot = Rot([(T(sp1, "el%d" % i, [8, 512], F32), Buf()) for i in range(2)])
                b_FT = Buf()
                b_bias = Buf()
                prev_end = None
                for (b0, nb, hT, b_hT) in pipelined_chunks(KCH, nr, hrot, gT1):
                    N = nb * 128
                    t0 = b0 * 128
                    for p in range(4):
                        ps, b_ps = pjrot.next()
                        for c in range(8):
                            mm(ps[:, 0:N], Wk[:, c, p * 128:(p + 1) * 128], hT[:, c, 0:N], c == 0, c == 7, [b_hT, b_W], [b_ps])
                        act(KfT[:, p, t0:t0 + N], ps[:, 0:N], AF.Copy, [b_ps], [])
                    proj_tokmajor_V(pjrot, hT, b_hT, nb, Wv, b_W, Vf, b0)
                    ps, b_ps = pjrot.next()
                    for c in range(8):
                        mm(ps[0:8, 0:N], Wf[:, c, 0:8], hT[:, c, 0:N], c == 0, c == 7, [b_hT, b_W], [b_ps])
                    el, b_el = elrot.next()
                    act(el[:, 0:N], ps[0:8, 0:N], AF.Exp, [b_ps, b_c2], [b_el], scale=-1.0, bias=negb[:, 0:1])
                    act(el[:, 0:N], el[:, 0:N], AF.Ln, [b_el], [b_el], bias=1.0)
                    init = 0.0 if prev_end is None else FT[:, prev_end - 1:prev_end]
                    fw.op("dve", lambda el=el, init=init, t0=t0, N=N: nc.vector.tensor_tensor_scan(
                        FT[:, t0:t0 + N], onesr[:, 0:N], el[:, 0:N], init, ALU.mult, ALU.subtract),
                        reads=[b_el, b_FT, b_c2], writes=[b_FT], small=True)
                    prev_end = t0 + N
                    for bi in range(nb):
                        blk = b0 + bi
                        ps2, b_ps2 = pjrot.next()
                        fw.op("pe", lambda ps2=ps2, blk=blk: nc.tensor.transpose(ps2[:, 0:8], FT[0:8, blk * 128:(blk + 1) * 128], identf[0:8, 0:8]),
                              reads=[b_FT], writes=[b_ps2])
                        fw.op("dve", lambda ps2=ps2, blk=blk: nc.vector.tensor_scalar(
                            biasO[:, blk, :], ps2[:, 0:8], -1.0, mbo[:, blk:blk + 1], ALU.mult, ALU.add),
                            reads=[b_ps2], writes=[b_bias])
                        fw.op("dve", lambda ps2=ps2, blk=blk: nc.vector.tensor_scalar(
                            biasH[:, blk, :], ps2[:, 0:8], -1.0, mbh[:, blk:blk + 1], ALU.mult, ALU.add),
                            reads=[b_ps2], writes=[b_bias])
                fw.barrier()
            with ExitStack() as sp2:
                Wq = WqF
                b_W = b_WqF
                tr = P(sp2, "tr", [128, 1024])
                res = Res()
                res.sel = sel
                res.pjrot = Rot([(P(sp2, "pj%d" % i, [128, 512]), Buf()) for i in range(1)])
                res.srot = Rot([(P(sp2, "S%d" % i, [128, 512]), Buf()) for i in range(3)])
                res.otrot = Rot([(P(sp2, "oT%d" % i, [128, 512]), Buf()) for i in range(2)])
                res.ptrot = Rot([(T(sp2, "pt%d" % i, [128, 512], BF16), Buf()) for i in range(6)])
                res.ssrot = Rot([(T(sp2, "ss%d" % i, [128, 512], F32), Buf()) for i in range(4)])
                res.fqrot = Rot([(T(sp2, "fq%d" % i, [128, 512], F32), Buf()) for i in range(3)])
                res.recrot = Rot([(T(sp2, "rec%d" % i, [65, 512], F32), Buf()) for i in range(2)])
                res.otsrot = Rot([(T(sp2, "ots%d" % i, [65, 512], F32), Buf()) for i in range(2)])
                QfA = T(sp2, "QfA", [128, 4, 512], BF16)
                QfB = T(sp2, "QfB", [128, 4, 512], BF16)
                b_QfT = Buf()
                fw.op("pool", lambda: nc.gpsimd.memset(QfA[:], 0.0), writes=[b_QfT])
                fw.op("pool", lambda: nc.gpsimd.memset(QfB[:], 0.0), writes=[b_QfT])
                nr = make_norm_res(sp2, tr)
                for (b0, nb, hT, b_hT) in pipelined_chunks(QCH, nr, hrot, gT1):
                    N = nb * 128
                    for p in range(4):
                        ps, b_ps = res.pjrot.next()
                        for c in range(8):
                            mm(ps[:, 0:N], Wq[:, c, p * 128:(p + 1) * 128], hT[:, c, 0:N], c == 0, c == 7, [b_hT, b_W], [b_ps])
                        act(QfA[0:64, p, 0:N], ps[0:64, 0:N], AF.Copy, [b_ps], [b_QfT])
                        act(QfB[64:128, p, 0:N], ps[64:128, 0:N], AF.Copy, [b_ps], [b_QfT])
                    attention("fox", b0, nb, (QfA, QfB), b_QfT, KfT, Vf_flat, yaT, res, FT=FT,
                              biasK=(biasH if nb == 1 else biasO))
                fw.barrier()

        b_x1s = [Buf() for _ in range(17)]
        with ExitStack() as ph:
            Wg = T(ph, "Wg", [128, 8, 2048], BF16)
            wof = T(ph, "wof", [128, 4, D], BF16)
            wod = T(ph, "wod", [128, 4, D], BF16)
            wout = T(ph, "wout", [128, 8, D], BF16)
            gB1 = T(ph, "gB1", [128, D], F32)
            b_W = Buf()
            loadw(Wg, w_in_v, OFF["ga"], 2048, 8, b_W)
            loadw(wof, wof_v, 0, D, 4, b_W)
            loadw(wod, wod_v, 0, D, 4, b_W)
            loadw(wout, w_out_v, 0, D, 8, b_W)
            fw.dma("sp", gB1[:], gB1_d, writes=[b_W], par=True)
            tr = P(ph, "tr", [128, 1024])
            grot = Rot([(P(ph, "g%d" % i, [128, 512]), Buf()) for i in range(4)])
            yps = P(ph, "yps", [128, 1024])
            b_yps = Buf()
            nr = make_norm_res(ph, tr)
            hrotm = Rot([(T(ph, "hTm%d" % i, [128, 8, 512], BF16), Buf()) for i in range(2)])
            mixT = T(ph, "mixT", [128, 8, 512], BF16)
            b_mix = Buf()
            sarot = Rot([(T(ph, "sa%d" % i, [128, 512], F32), Buf()) for i in range(2)])
            sbrot = Rot([(T(ph, "sb%d" % i, [128, 512], F32), Buf()) for i in range(2)])
            tmp = T(ph, "tmpm", [128, D], F32)
            b_tmp = Buf()
            x1rot = Rot([(T(ph, "x1t%d" % i, [128, D], F32), Buf()) for i in range(2)])
            if debug:
                out_toks.append(fw.dma("sp", dbg["ya"], yaT[:, :, :]))
                out_toks.append(fw.dma("sp", dbg["yb"], ybT[:, :, :]))
            for (b0, nb, hTm, b_hTm) in pipelined_chunks(QCH, nr, hrotm, gT1):
                N = nb * 128
                qtok0 = (b0 - 16) * 128
                for fc in range(8):
                    ga, b_ga = grot.next()
                    gb, b_gb = grot.next()
                    yap, b_yap = grot.next()
                    ybp, b_ybp = grot.next()
                    for c in range(8):
                        mm(ga[:, 0:N], Wg[:, c, fc * 128:(fc + 1) * 128], hTm[:, c, 0:N], c == 0, c == 7, [b_hTm, b_W], [b_ga])
                    for c in range(8):
                        mm(gb[:, 0:N], Wg[:, c, 1024 + fc * 128:1024 + (fc + 1) * 128], hTm[:, c, 0:N], c == 0, c == 7, [b_hTm, b_W], [b_gb])
                    for p in range(4):
                        mm(yap[:, 0:N], wof[:, p, fc * 128:(fc + 1) * 128], yaT[:, p, qtok0:qtok0 + N], p == 0, p == 3, [b_W], [b_yap])
                    for p in range(4):
                        mm(ybp[:, 0:N], wod[:, p, fc * 128:(fc + 1) * 128], ybT[:, p, qtok0:qtok0 + N], p == 0, p == 3, [b_W], [b_ybp])
                    sa, b_sa = sarot.next()
                    sb, b_sb = sbrot.next()
                    act(sa[:, 0:N], ga[:, 0:N], AF.Sigmoid, [b_ga], [b_sa])
                    act(sb[:, 0:N], gb[:, 0:N], AF.Sigmoid, [b_gb], [b_sb])
                    fw.op("dve", lambda: nc.vector.tensor_tensor(sa[:, 0:N], yap[:, 0:N], sa[:, 0:N], ALU.mult),
                          reads=[b_yap, b_sa], writes=[b_sa])
                    fw.op("dve", lambda: nc.vector.tensor_tensor(sb[:, 0:N], ybp[:, 0:N], sb[:, 0:N], ALU.mult),
                          reads=[b_ybp, b_sb], writes=[b_sb])
                    fw.op("pool", lambda fc=fc: nc.gpsimd.tensor_tensor(mixT[:, fc, 0:N], sa[:, 0:N], sb[:, 0:N], ALU.add),
                          reads=[b_sa, b_sb], writes=[b_mix])
                for bi in range(nb):
                    blk = b0 + bi
                    for half in range(2):
                        for fc in range(8):
                            mm(yps[:, half * 512:(half + 1) * 512], mixT[:, fc, bi * 128:(bi + 1) * 128],
                               wout[:, fc, half * 512:(half + 1) * 512], fc == 0, fc == 7, [b_mix, b_W], [b_yps])
                    st, b_st = rstd_from(nr, yps[:, :], [b_yps])
                    xt, b_xt = nr.xrot.next()
                    fw.dma("sp", xt[:], xk[blk * 128:(blk + 1) * 128, :], writes=[b_xt])
                    fw.op("dve", lambda st=st: nc.vector.scalar_tensor_tensor(tmp[:], yps[:, :], st[:, 2:3], gB1[:], ALU.mult, ALU.mult),
                          reads=[b_yps, b_st, b_W], writes=[b_tmp])
                    x1t, b_x1t = x1rot.next()
                    fw.op("pool", lambda xt=xt, x1t=x1t: nc.gpsimd.tensor_tensor(x1t[:], tmp[:], xt[:], ALU.add),
                          reads=[b_tmp, b_xt], writes=[b_x1t])
                    fw.dma("sp", x1s[(blk - 16) * 128:(blk - 15) * 128, :], x1t[:], reads=[b_x1t], writes=[b_x1s[blk - 16]])
                    if debug:
                        out_toks.append(fw.dma("sp", dbg["x1"][(blk - 16) * 128:(blk - 15) * 128, :], x1t[:], reads=[b_x1t]))
            fw.barrier()
        att.close()

        with ExitStack() as ph:
            Wup = T(ph, "Wup", [128, 8, 2 * DFF], BF16)
            Wd = T(ph, "Wd", [128, 22, D], BF16)
            gB2 = T(ph, "gB2", [128, D], F32)
            cw = T(ph, "cw", [128, 3 * NFC], F32)
            cb = T(ph, "cb", [128, NFC], F32)
            b_W = Buf()
            for c in range(8):
                for q4 in range(4):
                    fw.dma("pool", Wup[:, c, q4 * 1408:(q4 + 1) * 1408], w_up_v[:, c, q4 * 1408:(q4 + 1) * 1408], writes=[b_W], par=True)
            loadw(Wd, w_down_v, 0, D, 22, b_W)
            fw.dma("sp", gB2[:], gB2_d, writes=[b_W], par=True)
            fw.dma("sp", cw[:], cw_d, writes=[b_W], par=True)
            fw.dma("sp", cb[:], cb_d, writes=[b_W], par=True)
            tr = P(ph, "tr", [128, 1024])
            urot = Rot([(P(ph, "u%d" % i, [128, 512]), Buf()) for i in range(4)])
            yps = P(ph, "yps", [128, 1024])
            b_yps = Buf()
            nr = make_norm_res(ph, tr)
            h2T = T(ph, "h2T", [128, 8, 512], BF16)
            b_h2T = Buf()
            h2h = T(ph, "h2h", [128, 8, 128], BF16)
            b_h2h = Buf()
            mT = T(ph, "mT", [128, 22, 512], BF16)
            b_mT = Buf()
            carry = T(ph, "carry", [128, NFC, 2], F32)
            b_carry = Buf()
            yarot = Rot([(T(ph, "Ya%d" % i, [128, 512], F32), Buf()) for i in range(3)])
            ybrot = Rot([(T(ph, "Yb%d" % i, [128, 512], F32), Buf()) for i in range(3)])
            sqrot = Rot([(T(ph, "sq%d" % i, [128, 512], F32), Buf()) for i in range(2)])
            orot = Rot([(T(ph, "ot%d" % i, [128, D], F32), Buf()) for i in range(1)])
            norm_transpose(nr, x1s[0:128, :], [b_x1s[0]], h2h, b_h2h, 0, gT2)
            for fc in range(NFC):
                ps, b_ps = urot.next()
                for c in range(8):
                    mm(ps[:, 0:2], Wup[:, c, fc * 128:(fc + 1) * 128], h2h[:, c, 126:128], c == 0, c == 7, [b_h2h, b_W], [b_ps])
                fw.op("dve", lambda ps=ps, fc=fc: nc.vector.tensor_scalar(carry[:, fc, :], ps[:, 0:2], hflag[:, 0:1], None, ALU.mult),
                      reads=[b_ps], writes=[b_carry], small=True)
            for ci in range(4):
                for bi in range(4):
                    r0 = (1 + 4 * ci + bi) * 128
                    norm_transpose(nr, x1s[r0:r0 + 128, :], [b_x1s[1 + 4 * ci + bi]], h2T, b_h2T, bi * 128, gT2)
                pend_a = []
                pend_b = []

                def stage1(f):
                    Ys = []
                    for which, fc in ((0, f), (1, 22 + f)):
                        ps, b_ps = urot.next()
                        for c in range(8):
                            mm(ps[:, :], Wup[:, c, fc * 128:(fc + 1) * 128], h2T[:, c, :], c == 0, c == 7, [b_h2T, b_W], [b_ps])
                        Y, b_Y = (yarot if which == 0 else ybrot).next()
                        act(Y[:, :], ps[:, :], AF.Identity, [b_ps, b_W], [b_Y],
                            scale=cw[:, 2 * NFC + fc:2 * NFC + fc + 1], bias=cb[:, fc:fc + 1])
                        w1 = cw[:, NFC + fc:NFC + fc + 1]
                        w0 = cw[:, fc:fc + 1]
                        fw.op("dve", lambda: nc.vector.scalar_tensor_tensor(
                            Y[:, 1:512], ps[:, 0:511], w1, Y[:, 1:512], ALU.mult, ALU.add), reads=[b_ps, b_Y], writes=[b_Y])
                        fw.op("dve", lambda: nc.vector.scalar_tensor_tensor(
                            Y[:, 2:512], ps[:, 0:510], w0, Y[:, 2:512], ALU.mult, ALU.add), reads=[b_ps, b_Y], writes=[b_Y])
                        fw.op("dve", lambda: nc.vector.scalar_tensor_tensor(
                            Y[:, 0:1], carry[:, fc, 1:2], w1, Y[:, 0:1], ALU.mult, ALU.add), reads=[b_carry, b_Y], writes=[b_Y], small=True)
                        fw.op("dve", lambda: nc.vector.scalar_tensor_tensor(
                            Y[:, 0:2], carry[:, fc, 0:2], w0, Y[:, 0:2], ALU.mult, ALU.add), reads=[b_carry, b_Y], writes=[b_Y], small=True)
                        fw.op("dve", lambda: nc.vector.tensor_copy(carry[:, fc, :], ps[:, 510:512]),
                              reads=[b_ps], writes=[b_carry], small=True)
                        Ys.append((Y, b_Y))
                    pend_a.append((f, Ys))

                def stage2a(f, Ys):
                    (Ya, b_Ya), (Yb, b_Yb) = Ys
                    sq, b_sq = sqrot.next()
                    act(sq[:, :], Ya[:, :], AF.Gelu_apprx_tanh, [b_Ya], [b_sq])
                    fw.op("pool", lambda: nc.gpsimd.tensor_tensor(mT[:, f, :], sq[:, :], Yb[:, :], ALU.mult),
                          reads=[b_sq, b_Yb], writes=[b_mT])

                for it in range(22 + 1):
                    if it < 22:
                        stage1(it)
                    if len(pend_a) > 0 and (it >= 1):
                        stage2a(*pend_a.pop(0))
                while pend_a:
                    stage2a(*pend_a.pop(0))
                for bi in range(4):
                    lb = 4 * ci + bi
                    for half in range(2):
                        for f in range(22):
                            mm(yps[:, half * 512:(half + 1) * 512], mT[:, f, bi * 128:(bi + 1) * 128],
                               Wd[:, f, half * 512:(half + 1) * 512], f == 0, f == 21, [b_mT, b_W], [b_yps])
                    st, b_st = rstd_from(nr, yps[:, :], [b_yps])
                    xt, b_xt = nr.xrot.next()
                    fw.dma("sp", xt[:], x1s[(1 + lb) * 128:(2 + lb) * 128, :], reads=[b_x1s[1 + lb]], writes=[b_xt])
                    ot, b_ot = orot.next()
                    fw.op("dve", lambda st=st, ot=ot: nc.vector.scalar_tensor_tensor(ot[:], yps[:, :], st[:, 2:3], gB2[:], ALU.mult, ALU.mult),
                          reads=[b_yps, b_st, b_W], writes=[b_ot])
                    fw.op("pool", lambda xt=xt, ot=ot: nc.gpsimd.tensor_tensor(ot[:], ot[:], xt[:], ALU.add),
                          reads=[b_ot, b_xt], writes=[b_ot])
                    out_toks.append(fw.dma("sp", y_out[lb * 128:(lb + 1) * 128, :], ot[:], reads=[b_ot]))
            for t in out_toks:
                fw._wait("sp", t)
            fw.barrier()
    return nc


def _constants():
    c = {}
    k = np.arange(128)[:, None]
    q = np.arange(512)[None, :]
    mmask = np.zeros((20, 128, 512), np.float32)
    for idx in range(20):
        rel = idx - 3
        dlt = rel * 128 + q - k
        m = ((dlt >= 0) & (dlt <= 128)).astype(np.float32)
        m += ((dlt >= 0) & (dlt <= 512) & (dlt % 4 == 0)).astype(np.float32)
        m += ((dlt >= 0) & (dlt <= 2048) & (dlt % 16 == 0)).astype(np.float32)
        mmask[idx] = m
    c["mmask"] = mmask
    kk = np.arange(128)[:, None]
    qq = np.arange(128)[None, :]
    c["causal"] = np.where(kk <= qq, 0.0, -240000.0).astype(np.float32)
    c["ident"] = np.eye(128, dtype=np.float32)
    sel = np.zeros((8, 8, 128), np.float32)
    for h in range(8):
        sel[h, h, :] = 8.0
    c["sel"] = sel.reshape(8, 8 * 128)
    return c


def _rope_tables(base):
    half = 8
    inv_freq = (np.float32(500000.0) ** (-(np.arange(half, dtype=np.float32) * np.float32(2.0) / np.float32(16.0)))).astype(np.float32)
    pos = np.maximum(np.arange(NTOK) + base, 0).astype(np.float32)
    ang = (pos[:, None] * inv_freq[None, :]).astype(np.float32)
    cos = np.cos(ang.astype(np.float64)).astype(np.float32).T
    sin = np.sin(ang.astype(np.float64)).astype(np.float32).T
    C = np.ones((128, NTOK), np.float32)
    S = np.zeros((128, NTOK), np.float32)
    for a in range(2):
        C[a * 64:a * 64 + 8] = cos
        C[a * 64 + 8:a * 64 + 16] = cos
        S[a * 64:a * 64 + 8] = -sin
        S[a * 64 + 8:a * 64 + 16] = sin
    return C, S


_PROG = {}


def kernel(x, g_pre_mix, w_in, b_forget, w_o_fox, w_o_dil, w_out, g_post_mix,
           g_pre_ffn, w_up, conv_w, conv_b, w_down, g_post_ffn, _debug=False):
    f32 = np.float32
    x = np.asarray(x, f32)
    B, S, _ = x.shape
    consts = _constants()
    shared = {
        "w_in": np.ascontiguousarray(np.asarray(w_in, f32)[0]),
        "w_o_fox": np.ascontiguousarray(np.asarray(w_o_fox, f32)[0]),
        "w_o_dil": np.ascontiguousarray(np.asarray(w_o_dil, f32)[0]),
        "w_out": np.ascontiguousarray(np.asarray(w_out, f32)[0]),
        "w_up": np.ascontiguousarray(np.asarray(w_up, f32)[0]),
        "w_down": np.ascontiguousarray(np.asarray(w_down, f32)[0]),
        "gT1": np.ascontiguousarray(np.asarray(g_pre_mix, f32)[0].reshape(8, 128).T),
        "gT2": np.ascontiguousarray(np.asarray(g_pre_ffn, f32)[0].reshape(8, 128).T),
        "gB1": np.ascontiguousarray(np.broadcast_to(np.asarray(g_post_mix, f32)[0][None, :], (128, D))),
        "gB2": np.ascontiguousarray(np.broadcast_to(np.asarray(g_post_ffn, f32)[0][None, :], (128, D))),
        "bf": np.ascontiguousarray(np.asarray(b_forget, f32)[0].reshape(8, 1)),
        "cw": np.ascontiguousarray(np.asarray(conv_w, f32)[0].reshape(3, NFC, 128).transpose(2, 0, 1).reshape(128, 3 * NFC)),
        "cb": np.ascontiguousarray(np.asarray(conv_b, f32)[0].reshape(NFC, 128).T),
    }
    shared.update(consts)
    in_maps = []
    for core in range(8):
        b, h = core // 2, core % 2
        base = 2048 * h - 2176
        xkc = np.zeros((NTOK, D), f32)
        lo = max(0, -base)
        xkc[lo:] = x[b, base + lo:base + NTOK]
        tok = np.arange(NTOK) + base
        valid = tok >= 0
        mbo = np.where(valid, 0.0, -30000.0).astype(f32)
        halo_valid = valid.copy()
        halo_valid[16 * 128:17 * 128] = True
        mbh = np.where(halo_valid, 0.0, -30000.0).astype(f32)
        C, Sn = _rope_tables(base)
        m = dict(shared)
        m["xk"] = xkc
        m["mb_own"] = np.ascontiguousarray(mbo.reshape(NBK, 128).T)
        m["mb_halo"] = np.ascontiguousarray(mbh.reshape(NBK, 128).T)
        m["ropeC"] = C
        m["ropeS"] = Sn
        m["hflag"] = np.full((128, 1), float(h), f32)
        in_maps.append(m)
    key = bool(_debug)
    if key not in _PROG:
        _PROG[key] = build_program(debug=key)
    nc = _PROG[key]
    res = run_bass_kernel_spmd(nc, in_maps, core_ids=list(range(8)))
    out = np.zeros((B, S, D), f32)
    for core in range(8):
        b, h = core // 2, core % 2
        out[b, 2048 * h:2048 * (h + 1)] = res.results[core]["y"]
    if _debug:
        return out, res.results
    return out
```

```python
import numpy as np
from contextlib import ExitStack
import concourse.bass as bass
import concourse.mybir as mybir
from concourse.bass_utils import run_bass_kernel_spmd

F32 = mybir.dt.float32
BF16 = mybir.dt.bfloat16
AF = mybir.ActivationFunctionType
ALU = mybir.AluOpType
NDSEM = 40

D = 1024
NBK = 33
NTOK = NBK * 128
KCH = [(0, 4), (4, 4), (8, 4), (12, 4), (16, 1), (17, 4), (21, 4), (25, 4), (29, 4)]
QCH = KCH[4:]
NQT = 17 * 128
OFF = dict(qa=0, ka=512, va=1024, fa=1536, qb=1544, kb=2056, vb=2568, ga=3080, gb=4104)
DFF = 2816
NFC = 44
EPS = 1e-6


class Buf:
    __slots__ = ("name", "w", "r", "wsmall", "wl")

    def __init__(self, name=""):
        self.name = name
        self.w = None
        self.r = {}
        self.wsmall = False
        self.wl = []


SKIP_SAME_ENGINE_RAW = True


class Rot:
    def __init__(self, items):
        self.items = items
        self.i = 0

    def next(self):
        it = self.items[self.i]
        self.i = (self.i + 1) % len(self.items)
        return it


class FW:
    ENG = ("pe", "act", "dve", "pool", "sp")

    def __init__(self, nc, es):
        self.nc = nc
        self.e = {"pe": nc.tensor, "act": nc.scalar, "dve": nc.vector,
                  "pool": nc.gpsimd, "sp": nc.sync}
        self.sem = {k: es.enter_context(nc.semaphore("s_" + k)) for k in self.ENG}
        self.cnt = {k: 0 for k in self.ENG}
        self.waited = {}
        self.dsems = [es.enter_context(nc.semaphore("d%d" % i)) for i in range(NDSEM)]
        self.dcnt = [0] * NDSEM
        self.dnext = 0

    def _wait(self, eng, tok):
        kind, key, val = tok
        wk = (eng, kind, key)
        if self.waited.get(wk, 0) >= val:
            return
        sem = self.sem[key] if kind == "e" else self.dsems[key]
        self.e[eng].wait_ge(sem, val)
        self.waited[wk] = val

    def _deps(self, eng, reads, writes, par=False):
        for b in reads:
            for tok in b.wl:
                self._wait(eng, tok)
            if b.w is not None:
                if (SKIP_SAME_ENGINE_RAW and b.w[0] == "e" and b.w[1] == eng and not b.wsmall
                        and eng != "pool"):
                    continue
                self._wait(eng, b.w)
        for b in writes:
            if par:
                continue
            for tok in b.wl:
                self._wait(eng, tok)
            if b.w is not None and not (b.w[0] == "e" and b.w[1] == eng):
                self._wait(eng, b.w)
            for tok in b.r.values():
                if tok[0] == "e" and tok[1] == eng:
                    continue
                self._wait(eng, tok)

    def op(self, eng, fn, reads=(), writes=(), inc=True, small=False):
        self._deps(eng, reads, writes)
        ins = fn()
        if inc:
            self.cnt[eng] += 1
            ins.then_inc(self.sem[eng], 1)
            tok = ("e", eng, self.cnt[eng])
        else:
            tok = ("e", eng, self.cnt[eng] + 1)
        for b in reads:
            b.r[("e", eng)] = tok
        for b in writes:
            b.w = tok
            b.wl = []
            b.r = {}
            b.wsmall = small
        return tok

    def dma(self, q, out_ap, in_ap, reads=(), writes=(), par=False):
        self._deps(q, reads, writes, par=par)
        if par:
            for b in writes:
                for tok in b.r.values():
                    self._wait(q, tok)
        i = self.dnext
        self.dnext = (i + 1) % NDSEM
        if self.dcnt[i] > 0:
            self._wait(q, ("d", i, self.dcnt[i]))
        self.dcnt[i] += 16
        self.e[q].dma_start(out=out_ap, in_=in_ap).then_inc(self.dsems[i], 16)
        tok = ("d", i, self.dcnt[i])
        for b in reads:
            b.r[("d", i)] = tok
        for b in writes:
            if par:
                b.wl.append(tok)
            else:
                b.w = tok
                b.wl = []
            b.r = {}
        return tok

    def barrier(self):
        import os
        if os.environ.get("KDEBUG"):
            print("barrier: sbuf remaining", self.nc.sbuf_bytes_remaining, "cnt", dict(self.cnt))
        for k in ("pe", "act", "dve", "pool"):
            if self.cnt[k] > 0:
                self._wait("sp", ("e", k, self.cnt[k]))
        for i in range(NDSEM):
            if self.dcnt[i] > 0:
                self._wait("sp", ("d", i, self.dcnt[i]))
        self.cnt["sp"] += 1
        self.e["sp"].nop().then_inc(self.sem["sp"], 1)
        for k in ("pe", "act", "dve", "pool"):
            self._wait(k, ("e", "sp", self.cnt["sp"]))


def build_program(debug=False):
    nc = bass.Bass("TRN2", target_bir_lowering=False)

    def din(name, shape):
        return nc.dram_tensor(name, shape, F32, kind="ExternalInput").ap()

    xk = din("xk", [NTOK, D])
    w_in = din("w_in", [D, 5128])
    w_o_fox = din("w_o_fox", [512, D])
    w_o_dil = din("w_o_dil", [512, D])
    w_out = din("w_out", [D, D])
    w_up = din("w_up", [D, 2 * DFF])
    w_down = din("w_down", [DFF, D])
    gT1_d = din("gT1", [128, 8])
    gT2_d = din("gT2", [128, 8])
    gB1_d = din("gB1", [128, D])
    gB2_d = din("gB2", [128, D])
    bf_d = din("bf", [8, 1])
    cw_d = din("cw", [128, 3 * NFC])
    cb_d = din("cb", [128, NFC])
    mbo_d = din("mb_own", [128, NBK])
    mbh_d = din("mb_halo", [128, NBK])
    ropeC_d = din("ropeC", [128, NTOK])
    ropeS_d = din("ropeS", [128, NTOK])
    mm_d = din("mmask", [20, 128, 512])
    causal_d = din("causal", [128, 128])
    ident_d = din("ident", [128, 128])
    sel_d = din("sel", [8, 8 * 128])
    hflag_d = din("hflag", [128, 1])
    y_out = nc.dram_tensor("y", [2048, D], F32, kind="ExternalOutput").ap()
    x1s = nc.dram_tensor("x1s", [NQT, D], F32).ap()
    dbg = {}
    if debug:
        dbg["ya"] = nc.dram_tensor("dbg_ya", [128, 4, NQT], BF16, kind="ExternalOutput").ap()
        dbg["yb"] = nc.dram_tensor("dbg_yb", [128, 4, NQT], BF16, kind="ExternalOutput").ap()
        dbg["x1"] = nc.dram_tensor("dbg_x1", [NQT, D], F32, kind="ExternalOutput").ap()

    w_in_v = w_in.rearrange("(c p) n -> p c n", p=128)
    w_up_v = w_up.rearrange("(c p) n -> p c n", p=128)
    w_down_v = w_down.rearrange("(c p) n -> p c n", p=128)
    w_out_v = w_out.rearrange("(c p) n -> p c n", p=128)
    wof_v = w_o_fox.rearrange("(c p) n -> p c n", p=128)
    wod_v = w_o_dil.rearrange("(c p) n -> p c n", p=128)

    out_toks = []
    with ExitStack() as es:
        fw = FW(nc, es)

        uid = [0]

        def T(stack, name, shape, dt):
            uid[0] += 1
            return stack.enter_context(nc.sbuf_tensor("sb%d_%s" % (uid[0], name), shape, dt))

        def P(stack, name, shape, dt=F32):
            uid[0] += 1
            return stack.enter_context(nc.psum_tensor("ps%d_%s" % (uid[0], name), shape, dt))

        def mm(out, lhsT, rhs, start, stop, reads, writes, inc=None):
            return fw.op("pe", lambda: nc.tensor.matmul(out, lhsT, rhs, start=start, stop=stop),
                         reads, writes, inc=(stop if inc is None else inc))

        def act(out, in_, func, reads, writes, **kw):
            return fw.op("act", lambda: nc.scalar.activation(out, in_, func, **kw), reads, writes)

        def loadw(dst, src_v, col0, ncols, nchunks, b):
            for c in range(nchunks):
                fw.dma("pool", dst[:, c, 0:ncols], src_v[:, c, col0:col0 + ncols], writes=[b], par=True)

        identf = T(es, "identf", [128, 128], F32)
        identb = T(es, "identb", [128, 128], BF16)
        causal = T(es, "causal", [128, 128], BF16)
        ones32 = T(es, "ones32", [128, 64], F32)
        gT1 = T(es, "gT1", [128, 8], F32)
        gT2 = T(es, "gT2", [128, 8], F32)
        mbo = T(es, "mbo", [128, NBK], F32)
        mbh = T(es, "mbh", [128, NBK], F32)
        hflag = T(es, "hflag", [128, 1], F32)
        bfT = T(es, "bfT", [8, 1], F32)
        b_const = Buf("const")
        for dst, src in ((identf, ident_d), (gT1, gT1_d), (gT2, gT2_d), (mbo, mbo_d),
                         (mbh, mbh_d), (hflag, hflag_d), (bfT, bf_d)):
            fw.dma("sp", dst[:], src, writes=[b_const])
        fw.dma("pool", identb[:], ident_d, writes=[b_const])
        fw.dma("pool", causal[:], causal_d, writes=[b_const])
        fw.op("dve", lambda: nc.vector.memset(ones32[:], 1.0), writes=[b_const])

        fw.barrier()

        att = ExitStack()
        ybT = T(att, "ybT", [128, 4, NQT], BF16)
        yaT = T(att, "yaT", [128, 4, NQT], BF16)

        class NormRes:
            pass

        def make_norm_res(stack, tr_ps):
            r = NormRes()
            r.xrot = Rot([(T(stack, "xt%d" % i, [128, D], F32), Buf()) for i in range(2)])
            r.junk = T(stack, "junk", [128, D], BF16)
            r.b_junk = Buf()
            r.strot = Rot([(T(stack, "st%d" % i, [128, 4], F32), Buf()) for i in range(2)])
            r.tr = tr_ps
            r.b_tr = Buf()
            return r

        def rstd_from(r, src_ap, src_bufs):
            st, b_st = r.strot.next()
            fw.op("act", lambda: nc.scalar.activation(r.junk[:], src_ap, AF.Square, accum_out=st[:, 0:1]),
                  reads=src_bufs, writes=[r.b_junk, b_st], small=True)
            fw.op("act", lambda: nc.scalar.activation(st[:, 1:2], st[:, 0:1], AF.Sqrt, scale=1.0 / D, bias=EPS),
                  reads=[b_st], writes=[b_st], small=True)
            fw.op("dve", lambda: nc.vector.reciprocal(st[:, 2:3], st[:, 1:2]), reads=[b_st], writes=[b_st], small=True)
            return st, b_st

        def norm_transpose(r, src_rows_ap, src_bufs, hT, b_hT, col0, gT):
            xt, b_xt = r.xrot.next()
            fw.dma("sp", xt[:], src_rows_ap, reads=src_bufs, writes=[b_xt])
            st, b_st = rstd_from(r, xt[:], [b_xt])
            fw.op("dve", lambda: nc.vector.tensor_scalar(xt[:], xt[:], st[:, 2:3], None, ALU.mult),
                  reads=[b_xt, b_st], writes=[b_xt])
            for c in range(8):
                fw.op("pe", lambda c=c: nc.tensor.transpose(r.tr[:, c * 128:(c + 1) * 128], xt[:, c * 128:(c + 1) * 128], identf[:]),
                      reads=[b_xt], writes=[r.b_tr], inc=(c == 7))
            fw.op("dve", lambda: nc.vector.tensor_tensor(
                hT[:, :, col0:col0 + 128], r.tr[:, :].rearrange("p (c t) -> p c t", c=8),
                gT[:, :].unsqueeze(2).to_broadcast([128, 8, 128]), ALU.mult),
                reads=[r.b_tr], writes=[b_hT])
            return xt, b_xt

        def pipelined_chunks(chunks, nr, hrot, gT):
            def prep(ch):
                b0, nb = ch
                hT, b_hT = hrot.next()
                for bi in range(nb):
                    norm_transpose(nr, xk[(b0 + bi) * 128:(b0 + bi + 1) * 128, :], [], hT, b_hT, bi * 128, gT)
                return (b0, nb, hT, b_hT)
            cur = prep(chunks[0])
            for i in range(len(chunks)):
                nxt = prep(chunks[i + 1]) if i + 1 < len(chunks) else None
                yield cur
                cur = nxt

        def proj_tokmajor_V(pjrot, hT, b_hT, nblk, W, b_W, Vt, blk0):
            for bi in range(nblk):
                ps, b_ps = pjrot.next()
                for c in range(8):
                    mm(ps[:, 0:512], hT[:, c, bi * 128:(bi + 1) * 128], W[:, c, 0:512], c == 0, c == 7, [b_hT, b_W], [b_ps])
                fw.op("dve", lambda ps=ps, bi=bi: nc.vector.tensor_copy(
                    Vt[:, blk0 + bi, :, 0:64], ps[:, 0:512].rearrange("p (h d) -> p h d", h=8)),
                    reads=[b_ps], writes=[])

        def attention(kind, q0blk, nqb, QT, b_QT, KT, Vt, yT, res, FT=None, biasK=None, mb=None, MM=None):
            N = nqb * 128
            qtok0 = (q0blk - 16) * 128
            if kind == "fox":
                kbs = list(range(0, q0blk + nqb))
            else:
                kbs = list(range(max(0, q0blk - 16), q0blk + nqb))
            nk = len(kbs)
            L = 4
            tiles = [(h, i, kb) for h in range(8) for i, kb in enumerate(kbs)]
            fq = {}
            ot = {}
            pvbuf = {}
            deferred = []

            def emit_fq(h):
                ps, b_ps = res.pjrot.next()
                mm(ps[:, 0:N], res.sel[:, h * 128:(h + 1) * 128], FT[0:8, q0blk * 128:q0blk * 128 + N], True, True, [], [b_ps])
                fqb, b_fqb = res.fqrot.next()
                act(fqb[:, 0:N], ps[:, 0:N], AF.Copy, [b_ps], [b_fqb])
                fq[h] = (fqb, b_fqb)

            def stage_a(t):
                h, i, kb = tiles[t]
                p, a = h // 2, h % 2
                rs = slice(a * 64, (a + 1) * 64)
                if kind == "fox" and i == 0 and h + 1 < 8:
                    emit_fq(h + 1)
                j = kb - q0blk
                c0 = max(0, j) * 128
                diag = (j >= 0) and kind == "fox"
                S, b_S = res.srot.next()
                mm(S[:, c0:N], KT[:, p, kb * 128:(kb + 1) * 128], QT[a][:, p, c0:N], True, not diag, [b_QT], [b_S])
                if diag:
                    mm(S[:, c0:c0 + 128], identb[:], causal[:], False, True, [], [b_S])
                pt, b_pt = res.ptrot.next()
                if kind == "fox":
                    fqb, b_fqb = fq[h]
                    ssb, b_ssb = res.ssrot.next()
                    fw.op("dve", lambda: nc.vector.tensor_tensor(
                        ssb[:, c0:N], S[:, c0:N], fqb[:, c0:N], ALU.add), reads=[b_S, b_fqb], writes=[b_ssb])
                    act(pt[:, c0:N], ssb[:, c0:N], AF.Exp, [b_ssb], [b_pt], scale=0.125, bias=biasK[:, kb, h:h + 1])
                    pvbuf[t] = (pt, b_pt, c0)
                else:
                    act(pt[:, c0:N], S[:, c0:N], AF.Exp, [b_S], [b_pt], scale=0.125, bias=mb[:, kb:kb + 1])
                    pm, b_pm = res.pmrot.next()
                    rel = q0blk - kb + 3
                    if t % 3 == 2:
                        fw.op("pool", lambda: nc.gpsimd.tensor_tensor(
                            pm[:, c0:N], pt[:, c0:N], MM[:, rel, c0:N], ALU.mult), reads=[b_pt], writes=[b_pm])
                    else:
                        fw.op("dve", lambda: nc.vector.tensor_tensor(
                            pm[:, c0:N], pt[:, c0:N], MM[:, rel, c0:N], ALU.mult), reads=[b_pt], writes=[b_pm])
                    pvbuf[t] = (pm, b_pm, c0)

            def stage_b(t, step):
                h, i, kb = tiles[t]
                p, a = h // 2, h % 2
                rs = slice(a * 64, (a + 1) * 64)
                if i == 0:
                    ot[h] = res.otrot.next()
                oT, b_oT = ot[h]
                pv, b_pv, c0 = pvbuf.pop(t)
                vo = (kb * 8 + h) * 65
                mm(oT[:, c0:N], Vt[:, vo:vo + 128], pv[:, c0:N], i == 0, i == nk - 1, [b_pv], [b_oT])
                if i == nk - 1:
                    rec, b_rec = res.recrot.next()
                    ots, b_ots = res.otsrot.next()
                    act(ots[0:65, 0:N], oT[0:65, 0:N], AF.Copy, [b_oT], [b_ots])
                    if kind == "fox":
                        act(rec[64:65, 0:N], ots[64:65, 0:N], AF.Ln, [b_ots], [b_rec])
                        act(rec[64:65, 0:N], rec[64:65, 0:N], AF.Exp, [b_rec], [b_rec], scale=-1.0)
                    else:
                        fw.op("dve", lambda: nc.vector.reciprocal(rec[64:65, 0:N], ots[64:65, 0:N]),
                              reads=[b_ots], writes=[b_rec])

                    def part2():
                        R, b_R = res.pjrot.next()
                        mm(R[0:64, 0:N], ones32[64:65, 0:64], rec[64:65, 0:N], True, True, [b_rec], [b_R])
                        fw.op("dve", lambda: nc.vector.tensor_tensor(
                            yT[rs, p, qtok0:qtok0 + N], R[0:64, 0:N], ots[0:64, 0:N], ALU.mult),
                            reads=[b_R, b_ots], writes=[])
                    deferred.append((step + 5, part2))

            if kind == "fox":
                emit_fq(0)
            nt = len(tiles)
            for step in range(nt + L):
                if step < nt:
                    stage_a(step)
                if step - L >= 0:
                    stage_b(step - L, step)
                while deferred and deferred[0][0] <= step:
                    deferred.pop(0)[1]()
            while deferred:
                deferred.pop(0)[1]()

        class Res:
            pass

        with ExitStack() as ph:
            KdT = T(ph, "KdT", [128, 4, NTOK], BF16)
            Vd_flat = T(ph, "Vd", [128, NBK * 520 + 64], BF16)
            Vd = Vd_flat[:, 0:NBK * 520].rearrange("p (b h d) -> p b h d", b=NBK, h=8)
            fw.op("pool", lambda: nc.gpsimd.memset(Vd_flat[:, NBK * 520:NBK * 520 + 64], 0.0), writes=[])
            MM = T(ph, "MM", [128, 20, 512], BF16)
            b_mm = Buf()
            Wq = T(ph, "Wq", [128, 8, 512], BF16)
            Wqr = T(ph, "Wqr", [128, 8, 512], BF16)
            b_Wq = Buf()
            fw.op("pool", lambda: nc.gpsimd.memset(Vd[:, :, :, 64:65], 1.0), writes=[])
            hrot = Rot([(T(ph, "hT%d" % i, [128, 8, 512], BF16), Buf()) for i in range(2)])
            crot = Rot([(T(ph, "Ct%d" % i, [128, 512], F32), Buf()) for i in range(2)])
            srot_t = Rot([(T(ph, "St%d" % i, [128, 512], F32), Buf()) for i in range(2)])
            t1rot = Rot([(T(ph, "t1_%d" % i, [128, 512], F32), Buf()) for i in range(1)])
            t2rot = Rot([(T(ph, "t2_%d" % i, [128, 512], F32), Buf()) for i in range(1)])

            def rope_proj(pjrot, hT, b_hT, N, W, Wr, b_W, dstT, dcol0, tok0, b_dst_list, dstB=None):
                Ct, b_Ct = crot.next()
                St, b_St = srot_t.next()
                fw.dma("sp", Ct[:, 0:N], ropeC_d[:, tok0:tok0 + N], writes=[b_Ct])
                fw.dma("sp", St[:, 0:N], ropeS_d[:, tok0:tok0 + N], writes=[b_St])
                for p in range(4):
                    psA, b_A = pjrot.next()
                    psB, b_B = pjrot.next()
                    for c in range(8):
                        mm(psA[:, 0:N], W[:, c, p * 128:(p + 1) * 128], hT[:, c, 0:N], c == 0, c == 7, [b_hT, b_W], [b_A])
                    for c in range(8):
                        mm(psB[:, 0:N], Wr[:, c, p * 128:(p + 1) * 128], hT[:, c, 0:N], c == 0, c == 7, [b_hT, b_W], [b_B])
                    t1, b_t1 = t1rot.next()
                    t2, b_t2 = t2rot.next()
                    fw.op("dve", lambda: nc.vector.tensor_tensor(t1[:, 0:N], psA[:, 0:N], Ct[:, 0:N], ALU.mult),
                          reads=[b_A, b_Ct], writes=[b_t1])
                    fw.op("dve", lambda: nc.vector.tensor_tensor(t2[:, 0:N], psB[:, 0:N], St[:, 0:N], ALU.mult),
                          reads=[b_B, b_St], writes=[b_t2])
                    if dstB is None:
                        fw.op("pool", lambda p=p: nc.gpsimd.tensor_tensor(dstT[:, p, dcol0:dcol0 + N], t1[:, 0:N], t2[:, 0:N], ALU.add),
                              reads=[b_t1, b_t2], writes=b_dst_list)
                    else:
                        fw.op("pool", lambda p=p: nc.gpsimd.tensor_tensor(dstT[0:64, p, dcol0:dcol0 + N], t1[0:64, 0:N], t2[0:64, 0:N], ALU.add),
                              reads=[b_t1, b_t2], writes=b_dst_list)
                        fw.op("pool", lambda p=p: nc.gpsimd.tensor_tensor(dstB[64:128, p, dcol0:dcol0 + N], t1[64:128, 0:N], t2[64:128, 0:N], ALU.add),
                              reads=[b_t1, b_t2], writes=b_dst_list)

            def make_rot_w(W, Wr, b_W):
                fw.op("pool", lambda: nc.gpsimd.memset(Wr[:], 0.0), writes=[b_W])
                Wv = W[:, :, :].rearrange("p c (h d) -> p c h d", h=8)
                Wrv = Wr[:, :, :].rearrange("p c (h d) -> p c h d", h=8)
                for c in range(8):
                    fw.op("pool", lambda c=c: nc.gpsimd.tensor_copy(Wrv[:, c, :, 0:8], Wv[:, c, :, 8:16]), reads=[b_W], writes=[b_W])
                    fw.op("pool", lambda c=c: nc.gpsimd.tensor_copy(Wrv[:, c, :, 8:16], Wv[:, c, :, 0:8]), reads=[b_W], writes=[b_W])

            with ExitStack() as sp1:
                Wk = T(sp1, "Wk", [128, 8, 512], BF16)
                Wkr = T(sp1, "Wkr", [128, 8, 512], BF16)
                Wv = T(sp1, "Wv", [128, 8, 512], BF16)
                b_W = Buf()
                loadw(Wk, w_in_v, OFF["kb"], 512, 8, b_W)
                make_rot_w(Wk, Wkr, b_W)
                loadw(Wv, w_in_v, OFF["vb"], 512, 8, b_W)
                loadw(Wq, w_in_v, OFF["qb"], 512, 8, b_Wq)
                make_rot_w(Wq, Wqr, b_Wq)
                for i in range(20):
                    fw.dma("pool", MM[:, i, :], mm_d[i], writes=[b_mm], par=True)
                tr = P(sp1, "tr", [128, 1024])
                pjrot = Rot([(P(sp1, "pj%d" % i, [128, 512]), Buf()) for i in range(6)])
                nr = make_norm_res(sp1, tr)
                for (b0, nb, hT, b_hT) in pipelined_chunks(KCH, nr, hrot, gT1):
                    N = nb * 128
                    rope_proj(pjrot, hT, b_hT, N, Wk, Wkr, b_W, KdT, b0 * 128, b0 * 128, [])
                    proj_tokmajor_V(pjrot, hT, b_hT, nb, Wv, b_W, Vd, b0)
                fw.barrier()
            with ExitStack() as sp2:
                b_W = b_Wq
                tr = P(sp2, "tr", [128, 1024])
                res = Res()
                res.pjrot = Rot([(P(sp2, "pj%d" % i, [128, 512]), Buf()) for i in range(1)])
                res.srot = Rot([(P(sp2, "S%d" % i, [128, 512]), Buf()) for i in range(3)])
                res.otrot = Rot([(P(sp2, "oT%d" % i, [128, 512]), Buf()) for i in range(2)])
                res.ptrot = Rot([(T(sp2, "pt%d" % i, [128, 512], BF16), Buf()) for i in range(4)])
                res.pmrot = Rot([(T(sp2, "pm%d" % i, [128, 512], BF16), Buf()) for i in range(6)])
                rope_rot = Rot(res.pjrot.items + res.srot.items)
                res.recrot = Rot([(T(sp2, "rec%d" % i, [65, 512], F32), Buf()) for i in range(2)])
                res.otsrot = Rot([(T(sp2, "ots%d" % i, [65, 512], F32), Buf()) for i in range(2)])
                QdA = T(sp2, "QdA", [128, 4, 512], BF16)
                QdB = T(sp2, "QdB", [128, 4, 512], BF16)
                b_QdT = Buf()
                fw.op("pool", lambda: nc.gpsimd.memset(QdA[:], 0.0), writes=[b_QdT])
                fw.op("pool", lambda: nc.gpsimd.memset(QdB[:], 0.0), writes=[b_QdT])
                nr = make_norm_res(sp2, tr)
                for (b0, nb, hT, b_hT) in pipelined_chunks(QCH, nr, hrot, gT1):
                    N = nb * 128
                    rope_proj(rope_rot, hT, b_hT, N, Wq, Wqr, b_W, QdA, 0, b0 * 128, [b_QdT], dstB=QdB)
                    attention("dil", b0, nb, (QdA, QdB), b_QdT, KdT, Vd_flat, ybT, res,
                              mb=(mbh if nb == 1 else mbo), MM=MM)
                fw.barrier()

        with ExitStack() as ph:
            KfT = T(ph, "KfT", [128, 4, NTOK], BF16)
            Vf_flat = T(ph, "Vf", [128, NBK * 520 + 64], BF16)
            Vf = Vf_flat[:, 0:NBK * 520].rearrange("p (b h d) -> p b h d", b=NBK, h=8)
            fw.op("pool", lambda: nc.gpsimd.memset(Vf_flat[:, NBK * 520:NBK * 520 + 64], 0.0), writes=[])
            FT = T(ph, "FT", [8, NTOK], F32)
            biasO = T(ph, "biasO", [128, NBK, 8], F32)
            biasH = T(ph, "biasH", [128, NBK, 8], F32)
            sel = T(ph, "sel", [8, 8 * 128], F32)
            negb = T(ph, "negb", [8, 1], F32)
            onesr = T(ph, "onesr", [8, 512], F32)
            b_c2 = Buf()
            fw.dma("sp", sel[:], sel_d, writes=[b_c2])
            fw.op("dve", lambda: nc.vector.tensor_scalar(negb[:], bfT[:], -1.0, None, ALU.mult), writes=[b_c2])
            fw.op("dve", lambda: nc.vector.memset(onesr[:], 1.0), writes=[b_c2])
            fw.op("pool", lambda: nc.gpsimd.memset(Vf[:, :, :, 64:65], 1.0), writes=[])
            hrot = Rot([(T(ph, "hT%d" % i, [128, 8, 512], BF16), Buf()) for i in range(2)])
            WqF = T(ph, "WqF", [128, 8, 512], BF16)
            b_WqF = Buf()
            with ExitStack() as sp1:
                Wk = T(sp1, "Wk", [128, 8, 512], BF16)
                Wv = T(sp1, "Wv", [128, 8, 512], BF16)
                Wf = T(sp1, "Wf", [128, 8, 8], BF16)
                b_W = Buf()
                loadw(Wk, w_in_v, OFF["ka"], 512, 8, b_W)
                loadw(Wv, w_in_v, OFF["va"], 512, 8, b_W)
                loadw(Wf, w_in_v, OFF["fa"], 8, 8, b_W)
                loadw(WqF, w_in_v, OFF["qa"], 512, 8, b_WqF)
                tr = P(sp1, "tr", [128, 1024])
                pjrot = Rot([(P(sp1, "pj%d" % i, [128, 512]), Buf()) for i in range(6)])
                nr = make_norm_res(sp1, tr)
                elrot = Rot([(T(sp1, "el%d" % i, [8, 512], F32), Buf()) for i in range(2)])
                b_FT = Buf()
                b_bias = Buf()
                prev_end = None
                for (b0, nb, hT, b_hT) in pipelined_chunks(KCH, nr, hrot, gT1):
                    N = nb * 128
                    t0 = b0 * 128
                    for p in range(4):
                        ps, b_ps = pjrot.next()
                        for c in range(8):
                            mm(ps[:, 0:N], Wk[:, c, p * 128:(p + 1) * 128], hT[:, c, 0:N], c == 0, c == 7, [b_hT, b_W], [b_ps])
                        act(KfT[:, p, t0:t0 + N], ps[:, 0:N], AF.Copy, [b_ps], [])
                    proj_tokmajor_V(pjrot, hT, b_hT, nb, Wv, b_W, Vf, b0)
                    ps, b_ps = pjrot.next()
                    for c in range(8):
                        mm(ps[0:8, 0:N], Wf[:, c, 0:8], hT[:, c, 0:N], c == 0, c == 7, [b_hT, b_W], [b_ps])
                    el, b_el = elrot.next()
                    act(el[:, 0:N], ps[0:8, 0:N], AF.Exp, [b_ps, b_c2], [b_el], scale=-1.0, bias=negb[:, 0:1])
                    act(el[:, 0:N], el[:, 0:N], AF.Ln, [b_el], [b_el], bias=1.0)
                    init = 0.0 if prev_end is None else FT[:, prev_end - 1:prev_end]
                    fw.op("dve", lambda el=el, init=init, t0=t0, N=N: nc.vector.tensor_tensor_scan(
                        FT[:, t0:t0 + N], onesr[:, 0:N], el[:, 0:N], init, ALU.mult, ALU.subtract),
                        reads=[b_el, b_FT, b_c2], writes=[b_FT], small=True)
                    prev_end = t0 + N
                    for bi in range(nb):
                        blk = b0 + bi
                        ps2, b_ps2 = pjrot.next()
                        fw.op("pe", lambda ps2=ps2, blk=blk: nc.tensor.transpose(ps2[:, 0:8], FT[0:8, blk * 128:(blk + 1) * 128], identf[0:8, 0:8]),
                              reads=[b_FT], writes=[b_ps2])
                        fw.op("dve", lambda ps2=ps2, blk=blk: nc.vector.tensor_scalar(
                            biasO[:, blk, :], ps2[:, 0:8], -1.0, mbo[:, blk:blk + 1], ALU.mult, ALU.add),
                            reads=[b_ps2], writes=[b_bias])
                        fw.op("dve", lambda ps2=ps2, blk=blk: nc.vector.tensor_scalar(
                            biasH[:, blk, :], ps2[:, 0:8], -1.0, mbh[:, blk:blk + 1], ALU.mult, ALU.add),
                            reads=[b_ps2], writes=[b_bias])
                fw.barrier()
            with ExitStack() as sp2:
                Wq = WqF
                b_W = b_WqF
                tr = P(sp2, "tr", [128, 1024])
                res = Res()
                res.sel = sel
                res.pjrot = Rot([(P(sp2, "pj%d" % i, [128, 512]), Buf()) for i in range(1)])
                res.srot = Rot([(P(sp2, "S%d" % i, [128, 512]), Buf()) for i in range(3)])
                res.otrot = Rot([(P(sp2, "oT%d" % i, [128, 512]), Buf()) for i in range(2)])
                res.ptrot = Rot([(T(sp2, "pt%d" % i, [128, 512], BF16), Buf()) for i in range(6)])
                res.ssrot = Rot([(T(sp2, "ss%d" % i, [128, 512], F32), Buf()) for i in range(4)])
                res.fqrot = Rot([(T(sp2, "fq%d" % i, [128, 512], F32), Buf()) for i in range(3)])
                res.recrot = Rot([(T(sp2, "rec%d" % i, [65, 512], F32), Buf()) for i in range(2)])
                res.otsrot = Rot([(T(sp2, "ots%d" % i, [65, 512], F32), Buf()) for i in range(2)])
                QfA = T(sp2, "QfA", [128, 4, 512], BF16)
                QfB = T(sp2, "QfB", [128, 4, 512], BF16)
                b_QfT = Buf()
                fw.op("pool", lambda: nc.gpsimd.memset(QfA[:], 0.0), writes=[b_QfT])
                fw.op("pool", lambda: nc.gpsimd.memset(QfB[:], 0.0), writes=[b_QfT])
                nr = make_norm_res(sp2, tr)
                for (b0, nb, hT, b_hT) in pipelined_chunks(QCH, nr, hrot, gT1):
                    N = nb * 128
                    for p in range(4):
                        ps, b_ps = res.pjrot.next()
                        for c in range(8):
                            mm(ps[:, 0:N], Wq[:, c, p * 128:(p + 1) * 128], hT[:, c, 0:N], c == 0, c == 7, [b_hT, b_W], [b_ps])
                        act(QfA[0:64, p, 0:N], ps[0:64, 0:N], AF.Copy, [b_ps], [b_QfT])
                        act(QfB[64:128, p, 0:N], ps[64:128, 0:N], AF.Copy, [b_ps], [b_QfT])
                    attention("fox", b0, nb, (QfA, QfB), b_QfT, KfT, Vf_flat, yaT, res, FT=FT,
                              biasK=(biasH if nb == 1 else biasO))
                fw.barrier()

        b_x1s = [Buf() for _ in range(17)]
        with ExitStack() as ph:
            Wg = T(ph, "Wg", [128, 8, 2048], BF16)
            wof = T(ph, "wof", [128, 4, D], BF16)
            wod = T(ph, "wod", [128, 4, D], BF16)
            wout = T(ph, "wout", [128, 8, D], BF16)
            gB1 = T(ph, "gB1", [128, D], F32)
            b_W = Buf()
            loadw(Wg, w_in_v, OFF["ga"], 2048, 8, b_W)
            loadw(wof, wof_v, 0, D, 4, b_W)
            loadw(wod, wod_v, 0, D, 4, b_W)
            loadw(wout, w_out_v, 0, D, 8, b_W)
            fw.dma("sp", gB1[:], gB1_d, writes=[b_W], par=True)
            tr = P(ph, "tr", [128, 1024])
            grot = Rot([(P(ph, "g%d" % i, [128, 512]), Buf()) for i in range(4)])
            yps = P(ph, "yps", [128, 1024])
            b_yps = Buf()
            nr = make_norm_res(ph, tr)
            hrotm = Rot([(T(ph, "hTm%d" % i, [128, 8, 512], BF16), Buf()) for i in range(2)])
            mixT = T(ph, "mixT", [128, 8, 512], BF16)
            b_mix = Buf()
            sarot = Rot([(T(ph, "sa%d" % i, [128, 512], F32), Buf()) for i in range(2)])
            sbrot = Rot([(T(ph, "sb%d" % i, [128, 512], F32), Buf()) for i in range(2)])
            tmp = T(ph, "tmpm", [128, D], F32)
            b_tmp = Buf()
            x1rot = Rot([(T(ph, "x1t%d" % i, [128, D], F32), Buf()) for i in range(2)])
            if debug:
                out_toks.append(fw.dma("sp", dbg["ya"], yaT[:, :, :]))
                out_toks.append(fw.dma("sp", dbg["yb"], ybT[:, :, :]))
            for (b0, nb, hTm, b_hTm) in pipelined_chunks(QCH, nr, hrotm, gT1):
                N = nb * 128
                qtok0 = (b0 - 16) * 128
                for fc in range(8):
                    ga, b_ga = grot.next()
                    gb, b_gb = grot.next()
                    yap, b_yap = grot.next()
                    ybp, b_ybp = grot.next()
                    for c in range(8):
                        mm(ga[:, 0:N], Wg[:, c, fc * 128:(fc + 1) * 128], hTm[:, c, 0:N], c == 0, c == 7, [b_hTm, b_W], [b_ga])
                    for c in range(8):
                        mm(gb[:, 0:N], Wg[:, c, 1024 + fc * 128:1024 + (fc + 1) * 128], hTm[:, c, 0:N], c == 0, c == 7, [b_hTm, b_W], [b_gb])
                    for p in range(4):
                        mm(yap[:, 0:N], wof[:, p, fc * 128:(fc + 1) * 128], yaT[:, p, qtok0:qtok0 + N], p == 0, p == 3, [b_W], [b_yap])
                    for p in range(4):
                        mm(ybp[:, 0:N], wod[:, p, fc * 128:(fc + 1) * 128], ybT[:, p, qtok0:qtok0 + N], p == 0, p == 3, [b_W], [b_ybp])
                    sa, b_sa = sarot.next()
                    sb, b_sb = sbrot.next()
                    act(sa[:, 0:N], ga[:, 0:N], AF.Sigmoid, [b_ga], [b_sa])
                    act(sb[:, 0:N], gb[:, 0:N], AF.Sigmoid, [b_gb], [b_sb])
                    fw.op("dve", lambda: nc.vector.tensor_tensor(sa[:, 0:N], yap[:, 0:N], sa[:, 0:N], ALU.mult),
                          reads=[b_yap, b_sa], writes=[b_sa])
                    fw.op("dve", lambda: nc.vector.tensor_tensor(sb[:, 0:N], ybp[:, 0:N], sb[:, 0:N], ALU.mult),
                          reads=[b_ybp, b_sb], writes=[b_sb])
                    fw.op("pool", lambda fc=fc: nc.gpsimd.tensor_tensor(mixT[:, fc, 0:N], sa[:, 0:N], sb[:, 0:N], ALU.add),
                          reads=[b_sa, b_sb], writes=[b_mix])
                for bi in range(nb):
                    blk = b0 + bi
                    for half in range(2):
                        for fc in range(8):
                            mm(yps[:, half * 512:(half + 1) * 512], mixT[:, fc, bi * 128:(bi + 1) * 128],
                               wout[:, fc, half * 512:(half + 1) * 512], fc == 0, fc == 7, [b_mix, b_W], [b_yps])
                    st, b_st = rstd_from(nr, yps[:, :], [b_yps])
                    xt, b_xt = nr.xrot.next()
                    fw.dma("sp", xt[:], xk[blk * 128:(blk + 1) * 128, :], writes=[b_xt])
                    fw.op("dve", lambda st=st: nc.vector.scalar_tensor_tensor(tmp[:], yps[:, :], st[:, 2:3], gB1[:], ALU.mult, ALU.mult),
                          reads=[b_yps, b_st, b_W], writes=[b_tmp])
                    x1t, b_x1t = x1rot.next()
                    fw.op("pool", lambda xt=xt, x1t=x1t: nc.gpsimd.tensor_tensor(x1t[:], tmp[:], xt[:], ALU.add),
                          reads=[b_tmp, b_xt], writes=[b_x1t])
                    fw.dma("sp", x1s[(blk - 16) * 128:(blk - 15) * 128, :], x1t[:], reads=[b_x1t], writes=[b_x1s[blk - 16]])
                    if debug:
                        out_toks.append(fw.dma("sp", dbg["x1"][(blk - 16) * 128:(blk - 15) * 128, :], x1t[:], reads=[b_x1t]))
            fw.barrier()
        att.close()

        with ExitStack() as ph:
            Wup = T(ph, "Wup", [128, 8, 2 * DFF], BF16)
            Wd = T(ph, "Wd", [128, 22, D], BF16)
            gB2 = T(ph, "gB2", [128, D], F32)
            cw = T(ph, "cw", [128, 3 * NFC], F32)
            cb = T(ph, "cb", [128, NFC], F32)
            b_W = Buf()
            for c in range(8):
                for q4 in range(4):
                    fw.dma("pool", Wup[:, c, q4 * 1408:(q4 + 1) * 1408], w_up_v[:, c, q4 * 1408:(q4 + 1) * 1408], writes=[b_W], par=True)
            loadw(Wd, w_down_v, 0, D, 22, b_W)
            fw.dma("sp", gB2[:], gB2_d, writes=[b_W], par=True)
            fw.dma("sp", cw[:], cw_d, writes=[b_W], par=True)
            fw.dma("sp", cb[:], cb_d, writes=[b_W], par=True)
            tr = P(ph, "tr", [128, 1024])
            urot = Rot([(P(ph, "u%d" % i, [128, 512]), Buf()) for i in range(4)])
            yps = P(ph, "yps", [128, 1024])
            b_yps = Buf()
            nr = make_norm_res(ph, tr)
            h2rot = Rot([(T(ph, "h2T%d" % i, [128, 8, 512], BF16), Buf()) for i in range(2)])
            h2h = T(ph, "h2h", [128, 8, 128], BF16)
            b_h2h = Buf()
            mT = T(ph, "mT", [128, 22, 512], BF16)
            b_mT = Buf()
            carry = T(ph, "carry", [128, NFC, 2], F32)
            b_carry = Buf()
            yarot = Rot([(T(ph, "Ya%d" % i, [128, 512], F32), Buf()) for i in range(2)])
            ybrot = Rot([(T(ph, "Yb%d" % i, [128, 512], F32), Buf()) for i in range(2)])
            sqrot = Rot([(T(ph, "sq%d" % i, [128, 512], F32), Buf()) for i in range(2)])
            orot = Rot([(T(ph, "ot%d" % i, [128, D], F32), Buf()) for i in range(1)])
            norm_transpose(nr, x1s[0:128, :], [b_x1s[0]], h2h, b_h2h, 0, gT2)
            for fc in range(NFC):
                ps, b_ps = urot.next()
                for c in range(8):
                    mm(ps[:, 0:2], Wup[:, c, fc * 128:(fc + 1) * 128], h2h[:, c, 126:128], c == 0, c == 7, [b_h2h, b_W], [b_ps])
                fw.op("dve", lambda ps=ps, fc=fc: nc.vector.tensor_scalar(carry[:, fc, :], ps[:, 0:2], hflag[:, 0:1], None, ALU.mult),
                      reads=[b_ps], writes=[b_carry], small=True)
            def prep_ffn(ci):
                h2T_, b_h2T_ = h2rot.next()
                for bi in range(4):
                    r0 = (1 + 4 * ci + bi) * 128
                    norm_transpose(nr, x1s[r0:r0 + 128, :], [b_x1s[1 + 4 * ci + bi]], h2T_, b_h2T_, bi * 128, gT2)
                return h2T_, b_h2T_

            cur_h2 = prep_ffn(0)
            for ci in range(4):
                h2T, b_h2T = cur_h2
                pend_a = []
                pend_b = []

                def stage1(f):
                    Ys = []
                    for which, fc in ((0, f), (1, 22 + f)):
                        ps, b_ps = urot.next()
                        for c in range(8):
                            mm(ps[:, :], Wup[:, c, fc * 128:(fc + 1) * 128], h2T[:, c, :], c == 0, c == 7, [b_h2T, b_W], [b_ps])
                        Y, b_Y = (yarot if which == 0 else ybrot).next()
                        act(Y[:, :], ps[:, :], AF.Identity, [b_ps, b_W], [b_Y],
                            scale=cw[:, 2 * NFC + fc:2 * NFC + fc + 1], bias=cb[:, fc:fc + 1])
                        w1 = cw[:, NFC + fc:NFC + fc + 1]
                        w0 = cw[:, fc:fc + 1]
                        fw.op("dve", lambda: nc.vector.scalar_tensor_tensor(
                            Y[:, 1:512], ps[:, 0:511], w1, Y[:, 1:512], ALU.mult, ALU.add), reads=[b_ps, b_Y], writes=[b_Y])
                        fw.op("dve", lambda: nc.vector.scalar_tensor_tensor(
                            Y[:, 2:512], ps[:, 0:510], w0, Y[:, 2:512], ALU.mult, ALU.add), reads=[b_ps, b_Y], writes=[b_Y])
                        fw.op("dve", lambda: nc.vector.scalar_tensor_tensor(
                            Y[:, 0:1], carry[:, fc, 1:2], w1, Y[:, 0:1], ALU.mult, ALU.add), reads=[b_carry, b_Y], writes=[b_Y], small=True)
                        fw.op("dve", lambda: nc.vector.scalar_tensor_tensor(
                            Y[:, 0:2], carry[:, fc, 0:2], w0, Y[:, 0:2], ALU.mult, ALU.add), reads=[b_carry, b_Y], writes=[b_Y], small=True)
                        fw.op("dve", lambda: nc.vector.tensor_copy(carry[:, fc, :], ps[:, 510:512]),
                              reads=[b_ps], writes=[b_carry], small=True)
                        Ys.append((Y, b_Y))
                    pend_a.append((f, Ys))

                def stage2a(f, Ys):
                    (Ya, b_Ya), (Yb, b_Yb) = Ys
                    sq, b_sq = sqrot.next()
                    act(sq[:, :], Ya[:, :], AF.Gelu_apprx_tanh, [b_Ya], [b_sq])
                    fw.op("pool", lambda: nc.gpsimd.tensor_tensor(mT[:, f, :], sq[:, :], Yb[:, :], ALU.mult),
                          reads=[b_sq, b_Yb], writes=[b_mT])

                for it in range(22 + 1):
                    if it < 22:
                        stage1(it)
                    if len(pend_a) > 0 and (it >= 1):
                        stage2a(*pend_a.pop(0))
                while pend_a:
                    stage2a(*pend_a.pop(0))
                if ci + 1 < 4:
                    cur_h2 = prep_ffn(ci + 1)
                for bi in range(4):
                    lb = 4 * ci + bi
                    for half in range(2):
                        for f in range(22):
                            mm(yps[:, half * 512:(half + 1) * 512], mT[:, f, bi * 128:(bi + 1) * 128],
                               Wd[:, f, half * 512:(half + 1) * 512], f == 0, f == 21, [b_mT, b_W], [b_yps])
                    st, b_st = rstd_from(nr, yps[:, :], [b_yps])
                    xt, b_xt = nr.xrot.next()
                    fw.dma("sp", xt[:], x1s[(1 + lb) * 128:(2 + lb) * 128, :], reads=[b_x1s[1 + lb]], writes=[b_xt])
                    ot, b_ot = orot.next()
                    fw.op("dve", lambda st=st, ot=ot: nc.vector.scalar_tensor_tensor(ot[:], yps[:, :], st[:, 2:3], gB2[:], ALU.mult, ALU.mult),
                          reads=[b_yps, b_st, b_W], writes=[b_ot])
                    fw.op("pool", lambda xt=xt, ot=ot: nc.gpsimd.tensor_tensor(ot[:], ot[:], xt[:], ALU.add),
                          reads=[b_ot, b_xt], writes=[b_ot])
                    out_toks.append(fw.dma("sp", y_out[lb * 128:(lb + 1) * 128, :], ot[:], reads=[b_ot]))
            for t in out_toks:
                fw._wait("sp", t)
            fw.barrier()
    return nc


def _constants():
    c = {}
    k = np.arange(128)[:, None]
    q = np.arange(512)[None, :]
    mmask = np.zeros((20, 128, 512), np.float32)
    for idx in range(20):
        rel = idx - 3
        dlt = rel * 128 + q - k
        m = ((dlt >= 0) & (dlt <= 128)).astype(np.float32)
        m += ((dlt >= 0) & (dlt <= 512) & (dlt % 4 == 0)).astype(np.float32)
        m += ((dlt >= 0) & (dlt <= 2048) & (dlt % 16 == 0)).astype(np.float32)
        mmask[idx] = m
    c["mmask"] = mmask
    kk = np.arange(128)[:, None]
    qq = np.arange(128)[None, :]
    c["causal"] = np.where(kk <= qq, 0.0, -240000.0).astype(np.float32)
    c["ident"] = np.eye(128, dtype=np.float32)
    sel = np.zeros((8, 8, 128), np.float32)
    for h in range(8):
        sel[h, h, :] = 8.0
    c["sel"] = sel.reshape(8, 8 * 128)
    return c


def _rope_tables(base):
    half = 8
    inv_freq = (np.float32(500000.0) ** (-(np.arange(half, dtype=np.float32) * np.float32(2.0) / np.float32(16.0)))).astype(np.float32)
    pos = np.maximum(np.arange(NTOK) + base, 0).astype(np.float32)
    ang = (pos[:, None] * inv_freq[None, :]).astype(np.float32)
    cos = np.cos(ang.astype(np.float64)).astype(np.float32).T
    sin = np.sin(ang.astype(np.float64)).astype(np.float32).T
    C = np.ones((128, NTOK), np.float32)
    S = np.zeros((128, NTOK), np.float32)
    for a in range(2):
        C[a * 64:a * 64 + 8] = cos
        C[a * 64 + 8:a * 64 + 16] = cos
        S[a * 64:a * 64 + 8] = -sin
        S[a * 64 + 8:a * 64 + 16] = sin
    return C, S


_PROG = {}


def kernel(x, g_pre_mix, w_in, b_forget, w_o_fox, w_o_dil, w_out, g_post_mix,
           g_pre_ffn, w_up, conv_w, conv_b, w_down, g_post_ffn, _debug=False):
    f32 = np.float32
    x = np.asarray(x, f32)
    B, S, _ = x.shape
    consts = _constants()
    shared = {
        "w_in": np.ascontiguousarray(np.asarray(w_in, f32)[0]),
        "w_o_fox": np.ascontiguousarray(np.asarray(w_o_fox, f32)[0]),
        "w_o_dil": np.ascontiguousarray(np.asarray(w_o_dil, f32)[0]),
        "w_out": np.ascontiguousarray(np.asarray(w_out, f32)[0]),
        "w_up": np.ascontiguousarray(np.asarray(w_up, f32)[0]),
        "w_down": np.ascontiguousarray(np.asarray(w_down, f32)[0]),
        "gT1": np.ascontiguousarray(np.asarray(g_pre_mix, f32)[0].reshape(8, 128).T),
        "gT2": np.ascontiguousarray(np.asarray(g_pre_ffn, f32)[0].reshape(8, 128).T),
        "gB1": np.ascontiguousarray(np.broadcast_to(np.asarray(g_post_mix, f32)[0][None, :], (128, D))),
        "gB2": np.ascontiguousarray(np.broadcast_to(np.asarray(g_post_ffn, f32)[0][None, :], (128, D))),
        "bf": np.ascontiguousarray(np.asarray(b_forget, f32)[0].reshape(8, 1)),
        "cw": np.ascontiguousarray(np.asarray(conv_w, f32)[0].reshape(3, NFC, 128).transpose(2, 0, 1).reshape(128, 3 * NFC)),
        "cb": np.ascontiguousarray(np.asarray(conv_b, f32)[0].reshape(NFC, 128).T),
    }
    shared.update(consts)
    in_maps = []
    for core in range(8):
        b, h = core // 2, core % 2
        base = 2048 * h - 2176
        xkc = np.zeros((NTOK, D), f32)
        lo = max(0, -base)
        xkc[lo:] = x[b, base + lo:base + NTOK]
        tok = np.arange(NTOK) + base
        valid = tok >= 0
        mbo = np.where(valid, 0.0, -30000.0).astype(f32)
        halo_valid = valid.copy()
        halo_valid[16 * 128:17 * 128] = True
        mbh = np.where(halo_valid, 0.0, -30000.0).astype(f32)
        C, Sn = _rope_tables(base)
        m = dict(shared)
        m["xk"] = xkc
        m["mb_own"] = np.ascontiguousarray(mbo.reshape(NBK, 128).T)
        m["mb_halo"] = np.ascontiguousarray(mbh.reshape(NBK, 128).T)
        m["ropeC"] = C
        m["ropeS"] = Sn
        m["hflag"] = np.full((128, 1), float(h), f32)
        in_maps.append(m)
    key = bool(_debug)
    if key not in _PROG:
        _PROG[key] = build_program(debug=key)
    nc = _PROG[key]
    res = run_bass_kernel_spmd(nc, in_maps, core_ids=list(range(8)))
    out = np.zeros((B, S, D), f32)
    for core in range(8):
        b, h = core // 2, core % 2
        out[b, 2048 * h:2048 * (h + 1)] = res.results[core]["y"]
    if _debug:
        return out, res.results
    return out
```

```python
import numpy as np
from contextlib import ExitStack
import concourse.bass as bass
import concourse.mybir as mybir
from concourse.bass_utils import run_bass_kernel_spmd

F32 = mybir.dt.float32
BF16 = mybir.dt.bfloat16
AF = mybir.ActivationFunctionType
ALU = mybir.AluOpType
NDSEM = 40

D = 1024
NBK = 33
NTOK = NBK * 128
KCH = [(0, 4), (4, 4), (8, 4), (12, 4), (16, 1), (17, 4), (21, 4), (25, 4), (29, 4)]
QCH = KCH[4:]
NQT = 17 * 128
OFF = dict(qa=0, ka=512, va=1024, fa=1536, qb=1544, kb=2056, vb=2568, ga=3080, gb=4104)
DFF = 2816
NFC = 44
EPS = 1e-6


class Buf:
    __slots__ = ("name", "w", "r", "wsmall", "wl")

    def __init__(self, name=""):
        self.name = name
        self.w = None
        self.r = {}
        self.wsmall = False
        self.wl = []


SKIP_SAME_ENGINE_RAW = True


class Rot:
    def __init__(self, items):
        self.items = items
        self.i = 0

    def next(self):
        it = self.items[self.i]
        self.i = (self.i + 1) % len(self.items)
        return it


class FW:
    ENG = ("pe", "act", "dve", "pool", "sp")

    def __init__(self, nc, es):
        self.nc = nc
        self.e = {"pe": nc.tensor, "act": nc.scalar, "dve": nc.vector,
                  "pool": nc.gpsimd, "sp": nc.sync}
        self.sem = {k: es.enter_context(nc.semaphore("s_" + k)) for k in self.ENG}
        self.cnt = {k: 0 for k in self.ENG}
        self.waited = {}
        self.dsems = [es.enter_context(nc.semaphore("d%d" % i)) for i in range(NDSEM)]
        self.dcnt = [0] * NDSEM
        self.dnext = 0

    def _wait(self, eng, tok):
        kind, key, val = tok
        wk = (eng, kind, key)
        if self.waited.get(wk, 0) >= val:
            return
        sem = self.sem[key] if kind == "e" else self.dsems[key]
        self.e[eng].wait_ge(sem, val)
        self.waited[wk] = val

    def _deps(self, eng, reads, writes, par=False):
        for b in reads:
            for tok in b.wl:
                self._wait(eng, tok)
            if b.w is not None:
                if (SKIP_SAME_ENGINE_RAW and b.w[0] == "e" and b.w[1] == eng and not b.wsmall
                        and eng != "pool"):
                    continue
                self._wait(eng, b.w)
        for b in writes:
            if par:
                continue
            for tok in b.wl:
                self._wait(eng, tok)
            if b.w is not None and not (b.w[0] == "e" and b.w[1] == eng):
                self._wait(eng, b.w)
            for tok in b.r.values():
                if tok[0] == "e" and tok[1] == eng:
                    continue
                self._wait(eng, tok)

    def op(self, eng, fn, reads=(), writes=(), inc=True, small=False):
        self._deps(eng, reads, writes)
        ins = fn()
        if inc:
            self.cnt[eng] += 1
            ins.then_inc(self.sem[eng], 1)
            tok = ("e", eng, self.cnt[eng])
        else:
            tok = ("e", eng, self.cnt[eng] + 1)
        for b in reads:
            b.r[("e", eng)] = tok
        for b in writes:
            b.w = tok
            b.wl = []
            b.r = {}
            b.wsmall = small
        return tok

    def dma(self, q, out_ap, in_ap, reads=(), writes=(), par=False):
        self._deps(q, reads, writes, par=par)
        if par:
            for b in writes:
                for tok in b.r.values():
                    self._wait(q, tok)
        i = self.dnext
        self.dnext = (i + 1) % NDSEM
        if self.dcnt[i] > 0:
            self._wait(q, ("d", i, self.dcnt[i]))
        self.dcnt[i] += 16
        self.e[q].dma_start(out=out_ap, in_=in_ap).then_inc(self.dsems[i], 16)
        tok = ("d", i, self.dcnt[i])
        for b in reads:
            b.r[("d", i)] = tok
        for b in writes:
            if par:
                b.wl.append(tok)
            else:
                b.w = tok
                b.wl = []
            b.r = {}
        return tok

    def barrier(self):
        import os
        if os.environ.get("KDEBUG"):
            print("barrier: sbuf remaining", self.nc.sbuf_bytes_remaining, "cnt", dict(self.cnt))
        for k in ("pe", "act", "dve", "pool"):
            if self.cnt[k] > 0:
                self._wait("sp", ("e", k, self.cnt[k]))
        for i in range(NDSEM):
            if self.dcnt[i] > 0:
                self._wait("sp", ("d", i, self.dcnt[i]))
        self.cnt["sp"] += 1
        self.e["sp"].nop().then_inc(self.sem["sp"], 1)
        for k in ("pe", "act", "dve", "pool"):
            self._wait(k, ("e", "sp", self.cnt["sp"]))


def build_program(debug=False):
    nc = bass.Bass("TRN2", target_bir_lowering=False)

    def din(name, shape):
        return nc.dram_tensor(name, shape, F32, kind="ExternalInput").ap()

    xk = din("xk", [NTOK, D])
    w_in = din("w_in", [D, 5128])
    w_o_fox = din("w_o_fox", [512, D])
    w_o_dil = din("w_o_dil", [512, D])
    w_out = din("w_out", [D, D])
    w_up = din("w_up", [D, 2 * DFF])
    w_down = din("w_down", [DFF, D])
    gT1_d = din("gT1", [128, 8])
    gT2_d = din("gT2", [128, 8])
    gB1_d = din("gB1", [128, D])
    gB2_d = din("gB2", [128, D])
    bf_d = din("bf", [8, 1])
    cw_d = din("cw", [128, 3 * NFC])
    cb_d = din("cb", [128, NFC])
    mbo_d = din("mb_own", [128, NBK])
    mbh_d = din("mb_halo", [128, NBK])
    ropeC_d = din("ropeC", [128, NTOK])
    ropeS_d = din("ropeS", [128, NTOK])
    mm_d = din("mmask", [20, 128, 512])
    causal_d = din("causal", [128, 128])
    ident_d = din("ident", [128, 128])
    sel_d = din("sel", [8, 8 * 128])
    hflag_d = din("hflag", [128, 1])
    y_out = nc.dram_tensor("y", [2048, D], F32, kind="ExternalOutput").ap()
    x1s = nc.dram_tensor("x1s", [NQT, D], F32).ap()
    dbg = {}
    if debug:
        dbg["ya"] = nc.dram_tensor("dbg_ya", [128, 4, NQT], BF16, kind="ExternalOutput").ap()
        dbg["yb"] = nc.dram_tensor("dbg_yb", [128, 4, NQT], BF16, kind="ExternalOutput").ap()
        dbg["x1"] = nc.dram_tensor("dbg_x1", [NQT, D], F32, kind="ExternalOutput").ap()

    w_in_v = w_in.rearrange("(c p) n -> p c n", p=128)
    w_up_v = w_up.rearrange("(c p) n -> p c n", p=128)
    w_down_v = w_down.rearrange("(c p) n -> p c n", p=128)
    w_out_v = w_out.rearrange("(c p) n -> p c n", p=128)
    wof_v = w_o_fox.rearrange("(c p) n -> p c n", p=128)
    wod_v = w_o_dil.rearrange("(c p) n -> p c n", p=128)

    out_toks = []
    with ExitStack() as es:
        fw = FW(nc, es)

        uid = [0]

        def T(stack, name, shape, dt):
            uid[0] += 1
            return stack.enter_context(nc.sbuf_tensor("sb%d_%s" % (uid[0], name), shape, dt))

        def P(stack, name, shape, dt=F32):
            uid[0] += 1
            return stack.enter_context(nc.psum_tensor("ps%d_%s" % (uid[0], name), shape, dt))

        def mm(out, lhsT, rhs, start, stop, reads, writes, inc=None):
            return fw.op("pe", lambda: nc.tensor.matmul(out, lhsT, rhs, start=start, stop=stop),
                         reads, writes, inc=(stop if inc is None else inc))

        def act(out, in_, func, reads, writes, **kw):
            return fw.op("act", lambda: nc.scalar.activation(out, in_, func, **kw), reads, writes)

        def loadw(dst, src_v, col0, ncols, nchunks, b):
            for c in range(nchunks):
                fw.dma("pool", dst[:, c, 0:ncols], src_v[:, c, col0:col0 + ncols], writes=[b], par=True)

        identf = T(es, "identf", [128, 128], F32)
        identb = T(es, "identb", [128, 128], BF16)
        causal = T(es, "causal", [128, 128], BF16)
        ones32 = T(es, "ones32", [128, 64], F32)
        gT1 = T(es, "gT1", [128, 8], F32)
        gT2 = T(es, "gT2", [128, 8], F32)
        mbo = T(es, "mbo", [128, NBK], F32)
        mbh = T(es, "mbh", [128, NBK], F32)
        hflag = T(es, "hflag", [128, 1], F32)
        bfT = T(es, "bfT", [8, 1], F32)
        b_const = Buf("const")
        for dst, src in ((identf, ident_d), (gT1, gT1_d), (gT2, gT2_d), (mbo, mbo_d),
                         (mbh, mbh_d), (hflag, hflag_d), (bfT, bf_d)):
            fw.dma("sp", dst[:], src, writes=[b_const])
        fw.dma("pool", identb[:], ident_d, writes=[b_const])
        fw.dma("pool", causal[:], causal_d, writes=[b_const])
        fw.op("dve", lambda: nc.vector.memset(ones32[:], 1.0), writes=[b_const])

        fw.barrier()

        att = ExitStack()
        ybT = T(att, "ybT", [128, 4, NQT], BF16)
        yaT = T(att, "yaT", [128, 4, NQT], BF16)

        class NormRes:
            pass

        def make_norm_res(stack, tr_ps):
            r = NormRes()
            r.xrot = Rot([(T(stack, "xt%d" % i, [128, D], F32), Buf()) for i in range(2)])
            r.junk = T(stack, "junk", [128, D], BF16)
            r.b_junk = Buf()
            r.strot = Rot([(T(stack, "st%d" % i, [128, 4], F32), Buf()) for i in range(2)])
            r.tr = tr_ps
            r.b_tr = Buf()
            return r

        def rstd_from(r, src_ap, src_bufs):
            st, b_st = r.strot.next()
            fw.op("act", lambda: nc.scalar.activation(r.junk[:], src_ap, AF.Square, accum_out=st[:, 0:1]),
                  reads=src_bufs, writes=[r.b_junk, b_st], small=True)
            fw.op("act", lambda: nc.scalar.activation(st[:, 1:2], st[:, 0:1], AF.Sqrt, scale=1.0 / D, bias=EPS),
                  reads=[b_st], writes=[b_st], small=True)
            fw.op("dve", lambda: nc.vector.reciprocal(st[:, 2:3], st[:, 1:2]), reads=[b_st], writes=[b_st], small=True)
            return st, b_st

        def norm_transpose(r, src_rows_ap, src_bufs, hT, b_hT, col0, gT):
            xt, b_xt = r.xrot.next()
            fw.dma("sp", xt[:], src_rows_ap, reads=src_bufs, writes=[b_xt])
            st, b_st = rstd_from(r, xt[:], [b_xt])
            fw.op("dve", lambda: nc.vector.tensor_scalar(xt[:], xt[:], st[:, 2:3], None, ALU.mult),
                  reads=[b_xt, b_st], writes=[b_xt])
            for c in range(8):
                fw.op("pe", lambda c=c: nc.tensor.transpose(r.tr[:, c * 128:(c + 1) * 128], xt[:, c * 128:(c + 1) * 128], identf[:]),
                      reads=[b_xt], writes=[r.b_tr], inc=(c == 7))
            fw.op("dve", lambda: nc.vector.tensor_tensor(
                hT[:, :, col0:col0 + 128], r.tr[:, :].rearrange("p (c t) -> p c t", c=8),
                gT[:, :].unsqueeze(2).to_broadcast([128, 8, 128]), ALU.mult),
                reads=[r.b_tr], writes=[b_hT])
            return xt, b_xt

        def pipelined_chunks(chunks, nr, hrot, gT):
            def prep(ch):
                b0, nb = ch
                hT, b_hT = hrot.next()
                for bi in range(nb):
                    norm_transpose(nr, xk[(b0 + bi) * 128:(b0 + bi + 1) * 128, :], [], hT, b_hT, bi * 128, gT)
                return (b0, nb, hT, b_hT)
            cur = prep(chunks[0])
            for i in range(len(chunks)):
                nxt = prep(chunks[i + 1]) if i + 1 < len(chunks) else None
                yield cur
                cur = nxt

        def proj_tokmajor_V(pjrot, hT, b_hT, nblk, W, b_W, Vt, blk0):
            for bi in range(nblk):
                ps, b_ps = pjrot.next()
                for c in range(8):
                    mm(ps[:, 0:512], hT[:, c, bi * 128:(bi + 1) * 128], W[:, c, 0:512], c == 0, c == 7, [b_hT, b_W], [b_ps])
                fw.op("dve", lambda ps=ps, bi=bi: nc.vector.tensor_copy(
                    Vt[:, blk0 + bi, :, 0:64], ps[:, 0:512].rearrange("p (h d) -> p h d", h=8)),
                    reads=[b_ps], writes=[])

        def attention(kind, q0blk, nqb, QT, b_QT, KT, Vt, yT, res, FT=None, biasK=None, mb=None, MM=None):
            N = nqb * 128
            qtok0 = (q0blk - 16) * 128
            if kind == "fox":
                kbs = list(range(0, q0blk + nqb))
            else:
                kbs = list(range(max(0, q0blk - 16), q0blk + nqb))
            nk = len(kbs)
            L = 4
            tiles = [(h, i, kb) for h in range(8) for i, kb in enumerate(kbs)]
            fq = {}
            ot = {}
            pvbuf = {}
            deferred = []

            def emit_fq(h):
                ps, b_ps = res.pjrot.next()
                mm(ps[:, 0:N], res.sel[:, h * 128:(h + 1) * 128], FT[0:8, q0blk * 128:q0blk * 128 + N], True, True, [], [b_ps])
                fqb, b_fqb = res.fqrot.next()
                act(fqb[:, 0:N], ps[:, 0:N], AF.Copy, [b_ps], [b_fqb])
                fq[h] = (fqb, b_fqb)

            def stage_a(t):
                h, i, kb = tiles[t]
                p, a = h // 2, h % 2
                rs = slice(a * 64, (a + 1) * 64)
                if kind == "fox" and i == 0 and h + 1 < 8:
                    emit_fq(h + 1)
                j = kb - q0blk
                c0 = max(0, j) * 128
                diag = (j >= 0) and kind == "fox"
                S, b_S = res.srot.next()
                mm(S[:, c0:N], KT[:, p, kb * 128:(kb + 1) * 128], QT[a][:, p, c0:N], True, not diag, [b_QT], [b_S])
                if diag:
                    mm(S[:, c0:c0 + 128], identb[:], causal[:], False, True, [], [b_S])
                pt, b_pt = res.ptrot.next()
                if kind == "fox":
                    fqb, b_fqb = fq[h]
                    ssb, b_ssb = res.ssrot.next()
                    fw.op("dve", lambda: nc.vector.tensor_tensor(
                        ssb[:, c0:N], S[:, c0:N], fqb[:, c0:N], ALU.add), reads=[b_S, b_fqb], writes=[b_ssb])
                    act(pt[:, c0:N], ssb[:, c0:N], AF.Exp, [b_ssb], [b_pt], scale=0.125, bias=biasK[:, kb, h:h + 1])
                    pvbuf[t] = (pt, b_pt, c0)
                else:
                    act(pt[:, c0:N], S[:, c0:N], AF.Exp, [b_S], [b_pt], scale=0.125, bias=mb[:, kb:kb + 1])
                    pm, b_pm = res.pmrot.next()
                    rel = q0blk - kb + 3
                    if t % 3 == 2:
                        fw.op("pool", lambda: nc.gpsimd.tensor_tensor(
                            pm[:, c0:N], pt[:, c0:N], MM[:, rel, c0:N], ALU.mult), reads=[b_pt], writes=[b_pm])
                    else:
                        fw.op("dve", lambda: nc.vector.tensor_tensor(
                            pm[:, c0:N], pt[:, c0:N], MM[:, rel, c0:N], ALU.mult), reads=[b_pt], writes=[b_pm])
                    pvbuf[t] = (pm, b_pm, c0)

            def stage_b(t, step):
                h, i, kb = tiles[t]
                p, a = h // 2, h % 2
                rs = slice(a * 64, (a + 1) * 64)
                if i == 0:
                    ot[h] = res.otrot.next()
                oT, b_oT = ot[h]
                pv, b_pv, c0 = pvbuf.pop(t)
                vo = (kb * 8 + h) * 65
                mm(oT[:, c0:N], Vt[:, vo:vo + 128], pv[:, c0:N], i == 0, i == nk - 1, [b_pv], [b_oT])
                if i == nk - 1:
                    rec, b_rec = res.recrot.next()
                    ots, b_ots = res.otsrot.next()
                    act(ots[0:65, 0:N], oT[0:65, 0:N], AF.Copy, [b_oT], [b_ots])
                    if kind == "fox":
                        act(rec[64:65, 0:N], ots[64:65, 0:N], AF.Ln, [b_ots], [b_rec])
                        act(rec[64:65, 0:N], rec[64:65, 0:N], AF.Exp, [b_rec], [b_rec], scale=-1.0)
                    else:
                        fw.op("dve", lambda: nc.vector.reciprocal(rec[64:65, 0:N], ots[64:65, 0:N]),
                              reads=[b_ots], writes=[b_rec])

                    def part2():
                        R, b_R = res.pjrot.next()
                        mm(R[0:64, 0:N], ones32[64:65, 0:64], rec[64:65, 0:N], True, True, [b_rec], [b_R])
                        fw.op("dve", lambda: nc.vector.tensor_tensor(
                            yT[rs, p, qtok0:qtok0 + N], R[0:64, 0:N], ots[0:64, 0:N], ALU.mult),
                            reads=[b_R, b_ots], writes=[])
                    deferred.append((step + 5, part2))

            if kind == "fox":
                emit_fq(0)
            nt = len(tiles)
            for step in range(nt + L):
                if step < nt:
                    stage_a(step)
                if step - L >= 0:
                    stage_b(step - L, step)
                while deferred and deferred[0][0] <= step:
                    deferred.pop(0)[1]()
            while deferred:
                deferred.pop(0)[1]()

        class Res:
            pass

        with ExitStack() as ph:
            KdT = T(ph, "KdT", [128, 4, NTOK], BF16)
            Vd_flat = T(ph, "Vd", [128, NBK * 520 + 64], BF16)
            Vd = Vd_flat[:, 0:NBK * 520].rearrange("p (b h d) -> p b h d", b=NBK, h=8)
            fw.op("pool", lambda: nc.gpsimd.memset(Vd_flat[:, NBK * 520:NBK * 520 + 64], 0.0), writes=[])
            MM = T(ph, "MM", [128, 20, 512], BF16)
            b_mm = Buf()
            Wq = T(ph, "Wq", [128, 8, 512], BF16)
            Wqr = T(ph, "Wqr", [128, 8, 512], BF16)
            b_Wq = Buf()
            fw.op("pool", lambda: nc.gpsimd.memset(Vd[:, :, :, 64:65], 1.0), writes=[])
            hrot = Rot([(T(ph, "hT%d" % i, [128, 8, 512], BF16), Buf()) for i in range(2)])
            crot = Rot([(T(ph, "Ct%d" % i, [128, 512], F32), Buf()) for i in range(2)])
            srot_t = Rot([(T(ph, "St%d" % i, [128, 512], F32), Buf()) for i in range(2)])
            t1rot = Rot([(T(ph, "t1_%d" % i, [128, 512], F32), Buf()) for i in range(1)])
            t2rot = Rot([(T(ph, "t2_%d" % i, [128, 512], F32), Buf()) for i in range(1)])

            def rope_proj(pjrot, hT, b_hT, N, W, Wr, b_W, dstT, dcol0, tok0, b_dst_list, dstB=None):
                Ct, b_Ct = crot.next()
                St, b_St = srot_t.next()
                fw.dma("sp", Ct[:, 0:N], ropeC_d[:, tok0:tok0 + N], writes=[b_Ct])
                fw.dma("sp", St[:, 0:N], ropeS_d[:, tok0:tok0 + N], writes=[b_St])
                for p in range(4):
                    psA, b_A = pjrot.next()
                    psB, b_B = pjrot.next()
                    for c in range(8):
                        mm(psA[:, 0:N], W[:, c, p * 128:(p + 1) * 128], hT[:, c, 0:N], c == 0, c == 7, [b_hT, b_W], [b_A])
                    for c in range(8):
                        mm(psB[:, 0:N], Wr[:, c, p * 128:(p + 1) * 128], hT[:, c, 0:N], c == 0, c == 7, [b_hT, b_W], [b_B])
                    t1, b_t1 = t1rot.next()
                    t2, b_t2 = t2rot.next()
                    fw.op("dve", lambda: nc.vector.tensor_tensor(t1[:, 0:N], psA[:, 0:N], Ct[:, 0:N], ALU.mult),
                          reads=[b_A, b_Ct], writes=[b_t1])
                    fw.op("dve", lambda: nc.vector.tensor_tensor(t2[:, 0:N], psB[:, 0:N], St[:, 0:N], ALU.mult),
                          reads=[b_B, b_St], writes=[b_t2])
                    if dstB is None:
                        fw.op("pool", lambda p=p: nc.gpsimd.tensor_tensor(dstT[:, p, dcol0:dcol0 + N], t1[:, 0:N], t2[:, 0:N], ALU.add),
                              reads=[b_t1, b_t2], writes=b_dst_list)
                    else:
                        fw.op("pool", lambda p=p: nc.gpsimd.tensor_tensor(dstT[0:64, p, dcol0:dcol0 + N], t1[0:64, 0:N], t2[0:64, 0:N], ALU.add),
                              reads=[b_t1, b_t2], writes=b_dst_list)
                        fw.op("pool", lambda p=p: nc.gpsimd.tensor_tensor(dstB[64:128, p, dcol0:dcol0 + N], t1[64:128, 0:N], t2[64:128, 0:N], ALU.add),
                              reads=[b_t1, b_t2], writes=b_dst_list)

            def make_rot_w(W, Wr, b_W):
                fw.op("pool", lambda: nc.gpsimd.memset(Wr[:], 0.0), writes=[b_W])
                Wv = W[:, :, :].rearrange("p c (h d) -> p c h d", h=8)
                Wrv = Wr[:, :, :].rearrange("p c (h d) -> p c h d", h=8)
                for c in range(8):
                    fw.op("pool", lambda c=c: nc.gpsimd.tensor_copy(Wrv[:, c, :, 0:8], Wv[:, c, :, 8:16]), reads=[b_W], writes=[b_W])
                    fw.op("pool", lambda c=c: nc.gpsimd.tensor_copy(Wrv[:, c, :, 8:16], Wv[:, c, :, 0:8]), reads=[b_W], writes=[b_W])

            with ExitStack() as sp1:
                Wk = T(sp1, "Wk", [128, 8, 512], BF16)
                Wkr = T(sp1, "Wkr", [128, 8, 512], BF16)
                Wv = T(sp1, "Wv", [128, 8, 512], BF16)
                b_W = Buf()
                loadw(Wk, w_in_v, OFF["kb"], 512, 8, b_W)
                make_rot_w(Wk, Wkr, b_W)
                loadw(Wv, w_in_v, OFF["vb"], 512, 8, b_W)
                loadw(Wq, w_in_v, OFF["qb"], 512, 8, b_Wq)
                make_rot_w(Wq, Wqr, b_Wq)
                for i in range(20):
                    fw.dma("pool", MM[:, i, :], mm_d[i], writes=[b_mm], par=True)
                tr = P(sp1, "tr", [128, 1024])
                pjrot = Rot([(P(sp1, "pj%d" % i, [128, 512]), Buf()) for i in range(6)])
                nr = make_norm_res(sp1, tr)
                for (b0, nb, hT, b_hT) in pipelined_chunks(KCH, nr, hrot, gT1):
                    N = nb * 128
                    rope_proj(pjrot, hT, b_hT, N, Wk, Wkr, b_W, KdT, b0 * 128, b0 * 128, [])
                    proj_tokmajor_V(pjrot, hT, b_hT, nb, Wv, b_W, Vd, b0)
                fw.barrier()
            with ExitStack() as sp2:
                b_W = b_Wq
                tr = P(sp2, "tr", [128, 1024])
                res = Res()
                res.pjrot = Rot([(P(sp2, "pj%d" % i, [128, 512]), Buf()) for i in range(1)])
                res.srot = Rot([(P(sp2, "S%d" % i, [128, 512]), Buf()) for i in range(3)])
                res.otrot = Rot([(P(sp2, "oT%d" % i, [128, 512]), Buf()) for i in range(2)])
                res.ptrot = Rot([(T(sp2, "pt%d" % i, [128, 512], BF16), Buf()) for i in range(4)])
                res.pmrot = Rot([(T(sp2, "pm%d" % i, [128, 512], BF16), Buf()) for i in range(6)])
                rope_rot = Rot(res.pjrot.items + res.srot.items)
                res.recrot = Rot([(T(sp2, "rec%d" % i, [65, 512], F32), Buf()) for i in range(2)])
                res.otsrot = Rot([(T(sp2, "ots%d" % i, [65, 512], F32), Buf()) for i in range(2)])
                QdA = T(sp2, "QdA", [128, 4, 512], BF16)
                QdB = T(sp2, "QdB", [128, 4, 512], BF16)
                b_QdT = Buf()
                fw.op("pool", lambda: nc.gpsimd.memset(QdA[:], 0.0), writes=[b_QdT])
                fw.op("pool", lambda: nc.gpsimd.memset(QdB[:], 0.0), writes=[b_QdT])
                nr = make_norm_res(sp2, tr)
                for (b0, nb, hT, b_hT) in pipelined_chunks(QCH, nr, hrot, gT1):
                    N = nb * 128
                    rope_proj(rope_rot, hT, b_hT, N, Wq, Wqr, b_W, QdA, 0, b0 * 128, [b_QdT], dstB=QdB)
                    attention("dil", b0, nb, (QdA, QdB), b_QdT, KdT, Vd_flat, ybT, res,
                              mb=(mbh if nb == 1 else mbo), MM=MM)
                fw.barrier()

        with ExitStack() as ph:
            KfT = T(ph, "KfT", [128, 4, NTOK], BF16)
            Vf_flat = T(ph, "Vf", [128, NBK * 520 + 64], BF16)
            Vf = Vf_flat[:, 0:NBK * 520].rearrange("p (b h d) -> p b h d", b=NBK, h=8)
            fw.op("pool", lambda: nc.gpsimd.memset(Vf_flat[:, NBK * 520:NBK * 520 + 64], 0.0), writes=[])
            FT = T(ph, "FT", [8, NTOK], F32)
            biasO = T(ph, "biasO", [128, NBK, 8], F32)
            biasH = T(ph, "biasH", [128, NBK, 8], F32)
            sel = T(ph, "sel", [8, 8 * 128], F32)
            negb = T(ph, "negb", [8, 1], F32)
            onesr = T(ph, "onesr", [8, 512], F32)
            b_c2 = Buf()
            fw.dma("sp", sel[:], sel_d, writes=[b_c2])
            fw.op("dve", lambda: nc.vector.tensor_scalar(negb[:], bfT[:], -1.0, None, ALU.mult), writes=[b_c2])
            fw.op("dve", lambda: nc.vector.memset(onesr[:], 1.0), writes=[b_c2])
            fw.op("pool", lambda: nc.gpsimd.memset(Vf[:, :, :, 64:65], 1.0), writes=[])
            hrot = Rot([(T(ph, "hT%d" % i, [128, 8, 512], BF16), Buf()) for i in range(2)])
            WqF = T(ph, "WqF", [128, 8, 512], BF16)
            b_WqF = Buf()
            with ExitStack() as sp1:
                Wk = T(sp1, "Wk", [128, 8, 512], BF16)
                Wv = T(sp1, "Wv", [128, 8, 512], BF16)
                Wf = T(sp1, "Wf", [128, 8, 8], BF16)
                b_W = Buf()
                loadw(Wk, w_in_v, OFF["ka"], 512, 8, b_W)
                loadw(Wv, w_in_v, OFF["va"], 512, 8, b_W)
                loadw(Wf, w_in_v, OFF["fa"], 8, 8, b_W)
                loadw(WqF, w_in_v, OFF["qa"], 512, 8, b_WqF)
                tr = P(sp1, "tr", [128, 1024])
                pjrot = Rot([(P(sp1, "pj%d" % i, [128, 512]), Buf()) for i in range(6)])
                nr = make_norm_res(sp1, tr)
                elrot = Rot([(T(sp1, "el%d" % i, [8, 512], F32), Buf()) for i in range(2)])
                b_FT = Buf()
                b_bias = Buf()
                prev_end = None
                for (b0, nb, hT, b_hT) in pipelined_chunks(KCH, nr, hrot, gT1):
                    N = nb * 128
                    t0 = b0 * 128
                    for p in range(4):
                        ps, b_ps = pjrot.next()
                        for c in range(8):
                            mm(ps[:, 0:N], Wk[:, c, p * 128:(p + 1) * 128], hT[:, c, 0:N], c == 0, c == 7, [b_hT, b_W], [b_ps])
                        act(KfT[:, p, t0:t0 + N], ps[:, 0:N], AF.Copy, [b_ps], [])
                    proj_tokmajor_V(pjrot, hT, b_hT, nb, Wv, b_W, Vf, b0)
                    ps, b_ps = pjrot.next()
                    for c in range(8):
                        mm(ps[0:8, 0:N], Wf[:, c, 0:8], hT[:, c, 0:N], c == 0, c == 7, [b_hT, b_W], [b_ps])
                    el, b_el = elrot.next()
                    act(el[:, 0:N], ps[0:8, 0:N], AF.Exp, [b_ps, b_c2], [b_el], scale=-1.0, bias=negb[:, 0:1])
                    act(el[:, 0:N], el[:, 0:N], AF.Ln, [b_el], [b_el], bias=1.0)
                    init = 0.0 if prev_end is None else FT[:, prev_end - 1:prev_end]
                    fw.op("dve", lambda el=el, init=init, t0=t0, N=N: nc.vector.tensor_tensor_scan(
                        FT[:, t0:t0 + N], onesr[:, 0:N], el[:, 0:N], init, ALU.mult, ALU.subtract),
                        reads=[b_el, b_FT, b_c2], writes=[b_FT], small=True)
                    prev_end = t0 + N
                    for bi in range(nb):
                        blk = b0 + bi
                        ps2, b_ps2 = pjrot.next()
                        fw.op("pe", lambda ps2=ps2, blk=blk: nc.tensor.transpose(ps2[:, 0:8], FT[0:8, blk * 128:(blk + 1) * 128], identf[0:8, 0:8]),
                              reads=[b_FT], writes=[b_ps2])
                        fw.op("dve", lambda ps2=ps2, blk=blk: nc.vector.tensor_scalar(
                            biasO[:, blk, :], ps2[:, 0:8], -1.0, mbo[:, blk:blk + 1], ALU.mult, ALU.add),
                            reads=[b_ps2], writes=[b_bias])
                        fw.op("dve", lambda ps2=ps2, blk=blk: nc.vector.tensor_scalar(
                            biasH[:, blk, :], ps2[:, 0:8], -1.0, mbh[:, blk:blk + 1], ALU.mult, ALU.add),
                            reads=[b_ps2], writes=[b_bias])
                fw.barrier()
            with ExitStack() as sp2:
                Wq = WqF
                b_W = b_WqF
                tr = P(sp2, "tr", [128, 1024])
                res = Res()
                res.sel = sel
                res.pjrot = Rot([(P(sp2, "pj%d" % i, [128, 512]), Buf()) for i in range(1)])
                res.srot = Rot([(P(sp2, "S%d" % i, [128, 512]), Buf()) for i in range(3)])
                res.otrot = Rot([(P(sp2, "oT%d" % i, [128, 512]), Buf()) for i in range(2)])
                res.ptrot = Rot([(T(sp2, "pt%d" % i, [128, 512], BF16), Buf()) for i in range(6)])
                res.ssrot = Rot([(T(sp2, "ss%d" % i, [128, 512], F32), Buf()) for i in range(4)])
                res.fqrot = Rot([(T(sp2, "fq%d" % i, [128, 512], F32), Buf()) for i in range(3)])
                res.recrot = Rot([(T(sp2, "rec%d" % i, [65, 512], F32), Buf()) for i in range(2)])
                res.otsrot = Rot([(T(sp2, "ots%d" % i, [65, 512], F32), Buf()) for i in range(2)])
                QfA = T(sp2, "QfA", [128, 4, 512], BF16)
                QfB = T(sp2, "QfB", [128, 4, 512], BF16)
                b_QfT = Buf()
                fw.op("pool", lambda: nc.gpsimd.memset(QfA[:], 0.0), writes=[b_QfT])
                fw.op("pool", lambda: nc.gpsimd.memset(QfB[:], 0.0), writes=[b_QfT])
                nr = make_norm_res(sp2, tr)
                for (b0, nb, hT, b_hT) in pipelined_chunks(QCH, nr, hrot, gT1):
                    N = nb * 128
                    for p in range(4):
                        ps, b_ps = res.pjrot.next()
                        for c in range(8):
                            mm(ps[:, 0:N], Wq[:, c, p * 128:(p + 1) * 128], hT[:, c, 0:N], c == 0, c == 7, [b_hT, b_W], [b_ps])
                        act(QfA[0:64, p, 0:N], ps[0:64, 0:N], AF.Copy, [b_ps], [b_QfT])
                        act(QfB[64:128, p, 0:N], ps[64:128, 0:N], AF.Copy, [b_ps], [b_QfT])
                    attention("fox", b0, nb, (QfA, QfB), b_QfT, KfT, Vf_flat, yaT, res, FT=FT,
                              biasK=(biasH if nb == 1 else biasO))
                fw.barrier()

        b_x1s = [Buf() for _ in range(17)]
        with ExitStack() as ph:
            Wg = T(ph, "Wg", [128, 8, 2048], BF16)
            wof = T(ph, "wof", [128, 4, D], BF16)
            wod = T(ph, "wod", [128, 4, D], BF16)
            wout = T(ph, "wout", [128, 8, D], BF16)
            gB1 = T(ph, "gB1", [128, D], F32)
            b_W = Buf()
            loadw(Wg, w_in_v, OFF["ga"], 2048, 8, b_W)
            loadw(wof, wof_v, 0, D, 4, b_W)
            loadw(wod, wod_v, 0, D, 4, b_W)
            loadw(wout, w_out_v, 0, D, 8, b_W)
            fw.dma("sp", gB1[:], gB1_d, writes=[b_W], par=True)
            tr = P(ph, "tr", [128, 1024])
            g01 = P(ph, "g01", [128, 1024])
            g23 = P(ph, "g23", [128, 1024])
            gb_ = [Buf() for _ in range(4)]
            grot = Rot([(g01[:, 0:512], gb_[0]), (g01[:, 512:1024], gb_[1]),
                        (g23[:, 0:512], gb_[2]), (g23[:, 512:1024], gb_[3])])
            yps0 = P(ph, "yps", [128, 1024])
            yrot = Rot([(yps0, [Buf()]), (g01, [gb_[0], gb_[1]])])
            nr = make_norm_res(ph, tr)
            hrotm = Rot([(T(ph, "hTm%d" % i, [128, 8, 512], BF16), Buf()) for i in range(2)])
            mixT = T(ph, "mixT", [128, 8, 512], BF16)
            b_mix = Buf()
            sarot = Rot([(T(ph, "sa%d" % i, [128, 512], F32), Buf()) for i in range(2)])
            sbrot = Rot([(T(ph, "sb%d" % i, [128, 512], F32), Buf()) for i in range(2)])
            tmp = T(ph, "tmpm", [128, D], F32)
            b_tmp = Buf()
            x1rot = Rot([(T(ph, "x1t%d" % i, [128, D], F32), Buf()) for i in range(2)])
            if debug:
                out_toks.append(fw.dma("sp", dbg["ya"], yaT[:, :, :]))
                out_toks.append(fw.dma("sp", dbg["yb"], ybT[:, :, :]))
            for (b0, nb, hTm, b_hTm) in pipelined_chunks(QCH, nr, hrotm, gT1):
                N = nb * 128
                qtok0 = (b0 - 16) * 128
                for fc in range(8):
                    ga, b_ga = grot.next()
                    gb, b_gb = grot.next()
                    yap, b_yap = grot.next()
                    ybp, b_ybp = grot.next()
                    for c in range(8):
                        mm(ga[:, 0:N], Wg[:, c, fc * 128:(fc + 1) * 128], hTm[:, c, 0:N], c == 0, c == 7, [b_hTm, b_W], [b_ga])
                    for c in range(8):
                        mm(gb[:, 0:N], Wg[:, c, 1024 + fc * 128:1024 + (fc + 1) * 128], hTm[:, c, 0:N], c == 0, c == 7, [b_hTm, b_W], [b_gb])
                    for p in range(4):
                        mm(yap[:, 0:N], wof[:, p, fc * 128:(fc + 1) * 128], yaT[:, p, qtok0:qtok0 + N], p == 0, p == 3, [b_W], [b_yap])
                    for p in range(4):
                        mm(ybp[:, 0:N], wod[:, p, fc * 128:(fc + 1) * 128], ybT[:, p, qtok0:qtok0 + N], p == 0, p == 3, [b_W], [b_ybp])
                    sa, b_sa = sarot.next()
                    sb, b_sb = sbrot.next()
                    act(sa[:, 0:N], ga[:, 0:N], AF.Sigmoid, [b_ga], [b_sa])
                    act(sb[:, 0:N], gb[:, 0:N], AF.Sigmoid, [b_gb], [b_sb])
                    fw.op("dve", lambda: nc.vector.tensor_tensor(sa[:, 0:N], yap[:, 0:N], sa[:, 0:N], ALU.mult),
                          reads=[b_yap, b_sa], writes=[b_sa])
                    fw.op("dve", lambda: nc.vector.tensor_tensor(sb[:, 0:N], ybp[:, 0:N], sb[:, 0:N], ALU.mult),
                          reads=[b_ybp, b_sb], writes=[b_sb])
                    fw.op("pool", lambda fc=fc: nc.gpsimd.tensor_tensor(mixT[:, fc, 0:N], sa[:, 0:N], sb[:, 0:N], ALU.add),
                          reads=[b_sa, b_sb], writes=[b_mix])
                for bi in range(nb):
                    blk = b0 + bi
                    yps, yb_l = yrot.next()
                    for half in range(2):
                        for fc in range(8):
                            mm(yps[:, half * 512:(half + 1) * 512], mixT[:, fc, bi * 128:(bi + 1) * 128],
                               wout[:, fc, half * 512:(half + 1) * 512], fc == 0, fc == 7, [b_mix, b_W], yb_l)
                    st, b_st = rstd_from(nr, yps[:, :], yb_l)
                    xt, b_xt = nr.xrot.next()
                    fw.dma("sp", xt[:], xk[blk * 128:(blk + 1) * 128, :], writes=[b_xt])
                    fw.op("dve", lambda st=st: nc.vector.scalar_tensor_tensor(tmp[:], yps[:, :], st[:, 2:3], gB1[:], ALU.mult, ALU.mult),
                          reads=yb_l + [b_st, b_W], writes=[b_tmp])
                    x1t, b_x1t = x1rot.next()
                    fw.op("pool", lambda xt=xt, x1t=x1t: nc.gpsimd.tensor_tensor(x1t[:], tmp[:], xt[:], ALU.add),
                          reads=[b_tmp, b_xt], writes=[b_x1t])
                    fw.dma("sp", x1s[(blk - 16) * 128:(blk - 15) * 128, :], x1t[:], reads=[b_x1t], writes=[b_x1s[blk - 16]])
                    if debug:
                        out_toks.append(fw.dma("sp", dbg["x1"][(blk - 16) * 128:(blk - 15) * 128, :], x1t[:], reads=[b_x1t]))
            fw.barrier()
        att.close()

        with ExitStack() as ph:
            Wup = T(ph, "Wup", [128, 8, 2 * DFF], BF16)
            Wd = T(ph, "Wd", [128, 22, D], BF16)
            gB2 = T(ph, "gB2", [128, D], F32)
            cw = T(ph, "cw", [128, 3 * NFC], F32)
            cb = T(ph, "cb", [128, NFC], F32)
            b_W = Buf()
            for c in range(8):
                for q4 in range(4):
                    fw.dma("pool", Wup[:, c, q4 * 1408:(q4 + 1) * 1408], w_up_v[:, c, q4 * 1408:(q4 + 1) * 1408], writes=[b_W], par=True)
            loadw(Wd, w_down_v, 0, D, 22, b_W)
            fw.dma("sp", gB2[:], gB2_d, writes=[b_W], par=True)
            fw.dma("sp", cw[:], cw_d, writes=[b_W], par=True)
            fw.dma("sp", cb[:], cb_d, writes=[b_W], par=True)
            tr = P(ph, "tr", [128, 1024])
            u01 = P(ph, "u01", [128, 1024])
            u23 = P(ph, "u23", [128, 1024])
            ub_ = [Buf() for _ in range(4)]
            urot = Rot([(u01[:, 0:512], ub_[0]), (u01[:, 512:1024], ub_[1]),
                        (u23[:, 0:512], ub_[2]), (u23[:, 512:1024], ub_[3])])
            yps0 = P(ph, "yps", [128, 1024])
            yrot = Rot([(yps0, [Buf()]), (u01, [ub_[0], ub_[1]])])
            nr = make_norm_res(ph, tr)
            h2rot = Rot([(T(ph, "h2T%d" % i, [128, 8, 512], BF16), Buf()) for i in range(2)])
            h2h = T(ph, "h2h", [128, 8, 128], BF16)
            b_h2h = Buf()
            mT = T(ph, "mT", [128, 22, 512], BF16)
            b_mT = Buf()
            carry = T(ph, "carry", [128, NFC, 2], F32)
            b_carry = Buf()
            yarot = Rot([(T(ph, "Ya%d" % i, [128, 512], F32), Buf()) for i in range(2)])
            ybrot = Rot([(T(ph, "Yb%d" % i, [128, 512], F32), Buf()) for i in range(2)])
            sqrot = Rot([(T(ph, "sq%d" % i, [128, 512], F32), Buf()) for i in range(2)])
            orot = Rot([(T(ph, "ot%d" % i, [128, D], F32), Buf()) for i in range(1)])
            norm_transpose(nr, x1s[0:128, :], [b_x1s[0]], h2h, b_h2h, 0, gT2)
            for fc in range(NFC):
                ps, b_ps = urot.next()
                for c in range(8):
                    mm(ps[:, 0:2], Wup[:, c, fc * 128:(fc + 1) * 128], h2h[:, c, 126:128], c == 0, c == 7, [b_h2h, b_W], [b_ps])
                fw.op("dve", lambda ps=ps, fc=fc: nc.vector.tensor_scalar(carry[:, fc, :], ps[:, 0:2], hflag[:, 0:1], None, ALU.mult),
                      reads=[b_ps], writes=[b_carry], small=True)
            def prep_ffn(ci):
                h2T_, b_h2T_ = h2rot.next()
                for bi in range(4):
                    r0 = (1 + 4 * ci + bi) * 128
                    norm_transpose(nr, x1s[r0:r0 + 128, :], [b_x1s[1 + 4 * ci + bi]], h2T_, b_h2T_, bi * 128, gT2)
                return h2T_, b_h2T_

            cur_h2 = prep_ffn(0)
            for ci in range(4):
                h2T, b_h2T = cur_h2
                pend_a = []
                pend_b = []

                def stage1(f):
                    Ys = []
                    for which, fc in ((0, f), (1, 22 + f)):
                        ps, b_ps = urot.next()
                        for c in range(8):
                            mm(ps[:, :], Wup[:, c, fc * 128:(fc + 1) * 128], h2T[:, c, :], c == 0, c == 7, [b_h2T, b_W], [b_ps])
                        Y, b_Y = (yarot if which == 0 else ybrot).next()
                        act(Y[:, :], ps[:, :], AF.Identity, [b_ps, b_W], [b_Y],
                            scale=cw[:, 2 * NFC + fc:2 * NFC + fc + 1], bias=cb[:, fc:fc + 1])
                        w1 = cw[:, NFC + fc:NFC + fc + 1]
                        w0 = cw[:, fc:fc + 1]
                        fw.op("dve", lambda: nc.vector.scalar_tensor_tensor(
                            Y[:, 1:512], ps[:, 0:511], w1, Y[:, 1:512], ALU.mult, ALU.add), reads=[b_ps, b_Y], writes=[b_Y])
                        fw.op("dve", lambda: nc.vector.scalar_tensor_tensor(
                            Y[:, 2:512], ps[:, 0:510], w0, Y[:, 2:512], ALU.mult, ALU.add), reads=[b_ps, b_Y], writes=[b_Y])
                        fw.op("dve", lambda: nc.vector.scalar_tensor_tensor(
                            Y[:, 0:1], carry[:, fc, 1:2], w1, Y[:, 0:1], ALU.mult, ALU.add), reads=[b_carry, b_Y], writes=[b_Y], small=True)
                        fw.op("dve", lambda: nc.vector.scalar_tensor_tensor(
                            Y[:, 0:2], carry[:, fc, 0:2], w0, Y[:, 0:2], ALU.mult, ALU.add), reads=[b_carry, b_Y], writes=[b_Y], small=True)
                        fw.op("dve", lambda: nc.vector.tensor_copy(carry[:, fc, :], ps[:, 510:512]),
                              reads=[b_ps], writes=[b_carry], small=True)
                        Ys.append((Y, b_Y))
                    pend_a.append((f, Ys))

                def stage2a(f, Ys):
                    (Ya, b_Ya), (Yb, b_Yb) = Ys
                    sq, b_sq = sqrot.next()
                    act(sq[:, :], Ya[:, :], AF.Gelu_apprx_tanh, [b_Ya], [b_sq])
                    fw.op("pool", lambda: nc.gpsimd.tensor_tensor(mT[:, f, :], sq[:, :], Yb[:, :], ALU.mult),
                          reads=[b_sq, b_Yb], writes=[b_mT])

                for it in range(22 + 1):
                    if it < 22:
                        stage1(it)
                    if len(pend_a) > 0 and (it >= 1):
                        stage2a(*pend_a.pop(0))
                    if it == 15 and ci + 1 < 4:
                        cur_h2 = prep_ffn(ci + 1)
                while pend_a:
                    stage2a(*pend_a.pop(0))
                for bi in range(4):
                    lb = 4 * ci + bi
                    yps, yb_l = yrot.next()
                    for half in range(2):
                        for f in range(22):
                            mm(yps[:, half * 512:(half + 1) * 512], mT[:, f, bi * 128:(bi + 1) * 128],
                               Wd[:, f, half * 512:(half + 1) * 512], f == 0, f == 21, [b_mT, b_W], yb_l)
                    st, b_st = rstd_from(nr, yps[:, :], yb_l)
                    xt, b_xt = nr.xrot.next()
                    fw.dma("sp", xt[:], x1s[(1 + lb) * 128:(2 + lb) * 128, :], reads=[b_x1s[1 + lb]], writes=[b_xt])
                    ot, b_ot = orot.next()
                    fw.op("dve", lambda st=st, ot=ot: nc.vector.scalar_tensor_tensor(ot[:], yps[:, :], st[:, 2:3], gB2[:], ALU.mult, ALU.mult),
                          reads=yb_l + [b_st, b_W], writes=[b_ot])
                    fw.op("pool", lambda xt=xt, ot=ot: nc.gpsimd.tensor_tensor(ot[:], ot[:], xt[:], ALU.add),
                          reads=[b_ot, b_xt], writes=[b_ot])
                    out_toks.append(fw.dma("sp", y_out[lb * 128:(lb + 1) * 128, :], ot[:], reads=[b_ot]))
            for t in out_toks:
                fw._wait("sp", t)
            fw.barrier()
    return nc


def _constants():
    c = {}
    k = np.arange(128)[:, None]
    q = np.arange(512)[None, :]
    mmask = np.zeros((20, 128, 512), np.float32)
    for idx in range(20):
        rel = idx - 3
        dlt = rel * 128 + q - k
        m = ((dlt >= 0) & (dlt <= 128)).astype(np.float32)
        m += ((dlt >= 0) & (dlt <= 512) & (dlt % 4 == 0)).astype(np.float32)
        m += ((dlt >= 0) & (dlt <= 2048) & (dlt % 16 == 0)).astype(np.float32)
        mmask[idx] = m
    c["mmask"] = mmask
    kk = np.arange(128)[:, None]
    qq = np.arange(128)[None, :]
    c["causal"] = np.where(kk <= qq, 0.0, -240000.0).astype(np.float32)
    c["ident"] = np.eye(128, dtype=np.float32)
    sel = np.zeros((8, 8, 128), np.float32)
    for h in range(8):
        sel[h, h, :] = 8.0
    c["sel"] = sel.reshape(8, 8 * 128)
    return c


def _rope_tables(base):
    half = 8
    inv_freq = (np.float32(500000.0) ** (-(np.arange(half, dtype=np.float32) * np.float32(2.0) / np.float32(16.0)))).astype(np.float32)
    pos = np.maximum(np.arange(NTOK) + base, 0).astype(np.float32)
    ang = (pos[:, None] * inv_freq[None, :]).astype(np.float32)
    cos = np.cos(ang.astype(np.float64)).astype(np.float32).T
    sin = np.sin(ang.astype(np.float64)).astype(np.float32).T
    C = np.ones((128, NTOK), np.float32)
    S = np.zeros((128, NTOK), np.float32)
    for a in range(2):
        C[a * 64:a * 64 + 8] = cos
        C[a * 64 + 8:a * 64 + 16] = cos
        S[a * 64:a * 64 + 8] = -sin
        S[a * 64 + 8:a * 64 + 16] = sin
    return C, S


_PROG = {}


def kernel(x, g_pre_mix, w_in, b_forget, w_o_fox, w_o_dil, w_out, g_post_mix,
           g_pre_ffn, w_up, conv_w, conv_b, w_down, g_post_ffn, _debug=False):
    f32 = np.float32
    x = np.asarray(x, f32)
    B, S, _ = x.shape
    consts = _constants()
    shared = {
        "w_in": np.ascontiguousarray(np.asarray(w_in, f32)[0]),
        "w_o_fox": np.ascontiguousarray(np.asarray(w_o_fox, f32)[0]),
        "w_o_dil": np.ascontiguousarray(np.asarray(w_o_dil, f32)[0]),
        "w_out": np.ascontiguousarray(np.asarray(w_out, f32)[0]),
        "w_up": np.ascontiguousarray(np.asarray(w_up, f32)[0]),
        "w_down": np.ascontiguousarray(np.asarray(w_down, f32)[0]),
        "gT1": np.ascontiguousarray(np.asarray(g_pre_mix, f32)[0].reshape(8, 128).T),
        "gT2": np.ascontiguousarray(np.asarray(g_pre_ffn, f32)[0].reshape(8, 128).T),
        "gB1": np.ascontiguousarray(np.broadcast_to(np.asarray(g_post_mix, f32)[0][None, :], (128, D))),
        "gB2": np.ascontiguousarray(np.broadcast_to(np.asarray(g_post_ffn, f32)[0][None, :], (128, D))),
        "bf": np.ascontiguousarray(np.asarray(b_forget, f32)[0].reshape(8, 1)),
        "cw": np.ascontiguousarray(np.asarray(conv_w, f32)[0].reshape(3, NFC, 128).transpose(2, 0, 1).reshape(128, 3 * NFC)),
        "cb": np.ascontiguousarray(np.asarray(conv_b, f32)[0].reshape(NFC, 128).T),
    }
    shared.update(consts)
    in_maps = []
    for core in range(8):
        b, h = core // 2, core % 2
        base = 2048 * h - 2176
        xkc = np.zeros((NTOK, D), f32)
        lo = max(0, -base)
        xkc[lo:] = x[b, base + lo:base + NTOK]
        tok = np.arange(NTOK) + base
        valid = tok >= 0
        mbo = np.where(valid, 0.0, -30000.0).astype(f32)
        halo_valid = valid.copy()
        halo_valid[16 * 128:17 * 128] = True
        mbh = np.where(halo_valid, 0.0, -30000.0).astype(f32)
        C, Sn = _rope_tables(base)
        m = dict(shared)
        m["xk"] = xkc
        m["mb_own"] = np.ascontiguousarray(mbo.reshape(NBK, 128).T)
        m["mb_halo"] = np.ascontiguousarray(mbh.reshape(NBK, 128).T)
        m["ropeC"] = C
        m["ropeS"] = Sn
        m["hflag"] = np.full((128, 1), float(h), f32)
        in_maps.append(m)
    key = bool(_debug)
    if key not in _PROG:
        _PROG[key] = build_program(debug=key)
    nc = _PROG[key]
    res = run_bass_kernel_spmd(nc, in_maps, core_ids=list(range(8)))
    out = np.zeros((B, S, D), f32)
    for core in range(8):
        b, h = core // 2, core % 2
        out[b, 2048 * h:2048 * (h + 1)] = res.results[core]["y"]
    if _debug:
        return out, res.results
    return out
```

```python
import numpy as np
from contextlib import ExitStack
import concourse.bass as bass
import concourse.mybir as mybir
from concourse.bass_utils import run_bass_kernel_spmd

F32 = mybir.dt.float32
BF16 = mybir.dt.bfloat16
AF = mybir.ActivationFunctionType
ALU = mybir.AluOpType
NDSEM = 48

D = 1024
NBK = 33
NTOK = NBK * 128
KCH = [(0, 4), (4, 4), (8, 4), (12, 4), (16, 1), (17, 4), (21, 4), (25, 4), (29, 4)]
QCH = KCH[4:]
NQT = 17 * 128
OFF = dict(qa=0, ka=512, va=1024, fa=1536, qb=1544, kb=2056, vb=2568, ga=3080, gb=4104)
DFF = 2816
NFC = 44
EPS = 1e-6


class Buf:
    __slots__ = ("name", "w", "r", "wsmall", "wl")

    def __init__(self, name=""):
        self.name = name
        self.w = None
        self.r = {}
        self.wsmall = False
        self.wl = []


SKIP_SAME_ENGINE_RAW = True


class Rot:
    def __init__(self, items):
        self.items = items
        self.i = 0

    def next(self):
        it = self.items[self.i]
        self.i = (self.i + 1) % len(self.items)
        return it


class FW:
    ENG = ("pe", "act", "dve", "pool", "sp")

    def __init__(self, nc, es):
        self.nc = nc
        self.e = {"pe": nc.tensor, "act": nc.scalar, "dve": nc.vector,
                  "pool": nc.gpsimd, "sp": nc.sync}
        self.sem = {k: es.enter_context(nc.semaphore("s_" + k)) for k in self.ENG}
        self.cnt = {k: 0 for k in self.ENG}
        self.waited = {}
        self.dsems = [es.enter_context(nc.semaphore("d%d" % i)) for i in range(NDSEM)]
        self.dcnt = [0] * NDSEM
        self.dnext = 0
        self.dnext_q = [0, 0]

    def _wait(self, eng, tok):
        kind, key, val = tok
        wk = (eng, kind, key)
        if self.waited.get(wk, 0) >= val:
            return
        sem = self.sem[key] if kind == "e" else self.dsems[key]
        self.e[eng].wait_ge(sem, val)
        self.waited[wk] = val

    def _deps(self, eng, reads, writes, par=False):
        for b in reads:
            for tok in b.wl:
                self._wait(eng, tok)
            if b.w is not None:
                if (SKIP_SAME_ENGINE_RAW and b.w[0] == "e" and b.w[1] == eng and not b.wsmall
                        and eng != "pool"):
                    continue
                self._wait(eng, b.w)
        for b in writes:
            if par:
                continue
            for tok in b.wl:
                self._wait(eng, tok)
            if b.w is not None and not (b.w[0] == "e" and b.w[1] == eng and eng != "pool"):
                self._wait(eng, b.w)
            for tok in b.r.values():
                if tok[0] == "e" and tok[1] == eng and eng != "pool":
                    continue
                self._wait(eng, tok)

    def op(self, eng, fn, reads=(), writes=(), inc=True, small=False):
        self._deps(eng, reads, writes)
        ins = fn()
        if inc:
            self.cnt[eng] += 1
            ins.then_inc(self.sem[eng], 1)
            tok = ("e", eng, self.cnt[eng])
        else:
            tok = ("e", eng, self.cnt[eng] + 1)
        for b in reads:
            b.r[("e", eng)] = tok
        for b in writes:
            b.w = tok
            b.wl = []
            b.r = {}
            b.wsmall = small
        return tok

    def dma(self, q, out_ap, in_ap, reads=(), writes=(), par=False):
        self._deps(q, reads, writes, par=par)
        if par:
            for b in writes:
                for tok in b.r.values():
                    self._wait(q, tok)
        half = NDSEM // 2
        qi = 0 if q == "sp" else 1
        i = qi * half + self.dnext_q[qi]
        self.dnext_q[qi] = (self.dnext_q[qi] + 1) % half
        if self.dcnt[i] > 0:
            self._wait(q, ("d", i, self.dcnt[i]))
        self.dcnt[i] += 16
        self.e[q].dma_start(out=out_ap, in_=in_ap).then_inc(self.dsems[i], 16)
        tok = ("d", i, self.dcnt[i])
        for b in reads:
            b.r[("d", i)] = tok
        for b in writes:
            if par:
                b.wl.append(tok)
            else:
                b.w = tok
                b.wl = []
            b.r = {}
        return tok

    def barrier(self):
        import os
        if os.environ.get("KDEBUG"):
            print("barrier: sbuf remaining", self.nc.sbuf_bytes_remaining, "cnt", dict(self.cnt))
        for k in ("pe", "act", "dve", "pool"):
            if self.cnt[k] > 0:
                self._wait("sp", ("e", k, self.cnt[k]))
        for i in range(NDSEM):
            if self.dcnt[i] > 0:
                self._wait("sp", ("d", i, self.dcnt[i]))
        self.cnt["sp"] += 1
        self.e["sp"].nop().then_inc(self.sem["sp"], 1)
        for k in ("pe", "act", "dve", "pool"):
            self._wait(k, ("e", "sp", self.cnt["sp"]))


def build_program(debug=False):
    nc = bass.Bass("TRN2", target_bir_lowering=False)

    def din(name, shape):
        return nc.dram_tensor(name, shape, F32, kind="ExternalInput").ap()

    xk = din("xk", [NTOK, D])
    w_in = din("w_in", [D, 5128])
    w_o_fox = din("w_o_fox", [512, D])
    w_o_dil = din("w_o_dil", [512, D])
    w_out = din("w_out", [D, D])
    w_up = din("w_up", [D, 2 * DFF])
    w_down = din("w_down", [DFF, D])
    gT1_d = din("gT1", [128, 8])
    gT2_d = din("gT2", [128, 8])
    gB1_d = din("gB1", [128, D])
    gB2_d = din("gB2", [128, D])
    bf_d = din("bf", [8, 1])
    cw_d = din("cw", [128, 3 * NFC])
    cb_d = din("cb", [128, NFC])
    mbo_d = din("mb_own", [128, NBK])
    mbh_d = din("mb_halo", [128, NBK])
    ropeC_d = din("ropeC", [128, NTOK])
    ropeS_d = din("ropeS", [128, NTOK])
    mm_d = din("mmask", [20, 128, 512])
    causal_d = din("causal", [128, 128])
    ident_d = din("ident", [128, 128])
    sel_d = din("sel", [8, 8 * 128])
    hflag_d = din("hflag", [128, 1])
    y_out = nc.dram_tensor("y", [2048, D], F32, kind="ExternalOutput").ap()
    x1s = nc.dram_tensor("x1s", [NQT, D], F32).ap()
    dbg = {}
    if debug:
        dbg["ya"] = nc.dram_tensor("dbg_ya", [128, 4, NQT], BF16, kind="ExternalOutput").ap()
        dbg["yb"] = nc.dram_tensor("dbg_yb", [128, 4, NQT], BF16, kind="ExternalOutput").ap()
        dbg["x1"] = nc.dram_tensor("dbg_x1", [NQT, D], F32, kind="ExternalOutput").ap()

    w_in_v = w_in.rearrange("(c p) n -> p c n", p=128)
    w_up_v = w_up.rearrange("(c p) n -> p c n", p=128)
    w_down_v = w_down.rearrange("(c p) n -> p c n", p=128)
    w_out_v = w_out.rearrange("(c p) n -> p c n", p=128)
    wof_v = w_o_fox.rearrange("(c p) n -> p c n", p=128)
    wod_v = w_o_dil.rearrange("(c p) n -> p c n", p=128)

    out_toks = []
    with ExitStack() as es:
        fw = FW(nc, es)

        uid = [0]

        def T(stack, name, shape, dt):
            uid[0] += 1
            return stack.enter_context(nc.sbuf_tensor("sb%d_%s" % (uid[0], name), shape, dt))

        def P(stack, name, shape, dt=F32):
            uid[0] += 1
            return stack.enter_context(nc.psum_tensor("ps%d_%s" % (uid[0], name), shape, dt))

        def mm(out, lhsT, rhs, start, stop, reads, writes, inc=None):
            return fw.op("pe", lambda: nc.tensor.matmul(out, lhsT, rhs, start=start, stop=stop),
                         reads, writes, inc=(stop if inc is None else inc))

        def act(out, in_, func, reads, writes, **kw):
            return fw.op("act", lambda: nc.scalar.activation(out, in_, func, **kw), reads, writes)

        def loadw(dst, src_v, col0, ncols, nchunks, b):
            for c in range(nchunks):
                fw.dma("pool", dst[:, c, 0:ncols], src_v[:, c, col0:col0 + ncols], writes=[b], par=True)

        identf = T(es, "identf", [128, 128], F32)
        identb = T(es, "identb", [128, 128], BF16)
        causal = T(es, "causal", [128, 128], BF16)
        ones32 = T(es, "ones32", [128, 64], F32)
        gT1 = T(es, "gT1", [128, 8], F32)
        gT2 = T(es, "gT2", [128, 8], F32)
        mbo = T(es, "mbo", [128, NBK], F32)
        mbh = T(es, "mbh", [128, NBK], F32)
        hflag = T(es, "hflag", [128, 1], F32)
        bfT = T(es, "bfT", [8, 1], F32)
        b_const = Buf("const")
        for dst, src in ((identf, ident_d), (gT1, gT1_d), (gT2, gT2_d), (mbo, mbo_d),
                         (mbh, mbh_d), (hflag, hflag_d), (bfT, bf_d)):
            fw.dma("sp", dst[:], src, writes=[b_const])
        fw.dma("pool", identb[:], ident_d, writes=[b_const])
        fw.dma("pool", causal[:], causal_d, writes=[b_const])
        fw.op("dve", lambda: nc.vector.memset(ones32[:], 1.0), writes=[b_const])

        fw.barrier()

        att = ExitStack()
        ybT = T(att, "ybT", [128, 4, NQT], BF16)
        yaT = T(att, "yaT", [128, 4, NQT], BF16)

        class NormRes:
            pass

        def make_norm_res(stack, tr_ps):
            r = NormRes()
            r.xrot = Rot([(T(stack, "xt%d" % i, [128, D], F32), Buf()) for i in range(2)])
            r.junk = T(stack, "junk", [128, D], BF16)
            r.b_junk = Buf()
            r.strot = Rot([(T(stack, "st%d" % i, [128, 4], F32), Buf()) for i in range(2)])
            r.tr = tr_ps
            r.b_tr = Buf()
            return r

        def rstd_from(r, src_ap, src_bufs):
            st, b_st = r.strot.next()
            fw.op("act", lambda: nc.scalar.activation(r.junk[:], src_ap, AF.Square, accum_out=st[:, 0:1]),
                  reads=src_bufs, writes=[r.b_junk, b_st], small=True)
            fw.op("act", lambda: nc.scalar.activation(st[:, 1:2], st[:, 0:1], AF.Sqrt, scale=1.0 / D, bias=EPS),
                  reads=[b_st], writes=[b_st], small=True)
            fw.op("dve", lambda: nc.vector.reciprocal(st[:, 2:3], st[:, 1:2]), reads=[b_st], writes=[b_st], small=True)
            return st, b_st

        def norm_transpose_gen(r, src_rows_ap, src_bufs, hT, b_hT, col0, gT):
            xt, b_xt = r.xrot.next()
            fw.dma("sp", xt[:], src_rows_ap, reads=src_bufs, writes=[b_xt])
            st, b_st = rstd_from(r, xt[:], [b_xt])
            fw.op("dve", lambda: nc.vector.tensor_scalar(xt[:], xt[:], st[:, 2:3], None, ALU.mult),
                  reads=[b_xt, b_st], writes=[b_xt])
            yield
            for c in range(8):
                fw.op("pe", lambda c=c: nc.tensor.transpose(r.tr[:, c * 128:(c + 1) * 128], xt[:, c * 128:(c + 1) * 128], identf[:]),
                      reads=[b_xt], writes=[r.b_tr], inc=(c == 7))
            yield
            fw.op("dve", lambda: nc.vector.tensor_tensor(
                hT[:, :, col0:col0 + 128], r.tr[:, :].rearrange("p (c t) -> p c t", c=8),
                gT[:, :].unsqueeze(2).to_broadcast([128, 8, 128]), ALU.mult),
                reads=[r.b_tr], writes=[b_hT])
            yield

        def norm_transpose(r, src_rows_ap, src_bufs, hT, b_hT, col0, gT):
            for _ in norm_transpose_gen(r, src_rows_ap, src_bufs, hT, b_hT, col0, gT):
                pass

        class PFState:
            gen = None

        def pf_step(n=1):
            for _ in range(n):
                if PFState.gen is None:
                    return
                try:
                    next(PFState.gen)
                except StopIteration:
                    PFState.gen = None

        def pf_finish():
            while PFState.gen is not None:
                pf_step()

        def pipelined_chunks(chunks, nr, hrot, gT):
            def prep_gen(ch, out):
                b0, nb = ch
                hT, b_hT = hrot.next()
                out.append((b0, nb, hT, b_hT))
                for bi in range(nb):
                    yield from norm_transpose_gen(nr, xk[(b0 + bi) * 128:(b0 + bi + 1) * 128, :], [], hT, b_hT, bi * 128, gT)
            out = []
            for _ in prep_gen(chunks[0], out):
                pass
            cur = out[0]
            for i in range(len(chunks)):
                nxt_out = []
                if i + 1 < len(chunks):
                    PFState.gen = prep_gen(chunks[i + 1], nxt_out)
                    pf_step()
                yield cur
                pf_finish()
                cur = nxt_out[0] if nxt_out else None

        def proj_tokmajor_V(pjrot, hT, b_hT, nblk, W, b_W, Vt, blk0):
            for bi in range(nblk):
                ps, b_ps = pjrot.next()
                for c in range(8):
                    mm(ps[:, 0:512], hT[:, c, bi * 128:(bi + 1) * 128], W[:, c, 0:512], c == 0, c == 7, [b_hT, b_W], [b_ps])
                fw.op("dve", lambda ps=ps, bi=bi: nc.vector.tensor_copy(
                    Vt[:, blk0 + bi, :, 0:64], ps[:, 0:512].rearrange("p (h d) -> p h d", h=8)),
                    reads=[b_ps], writes=[])
                pf_step()

        def attention(kind, q0blk, nqb, QT, b_QT, KT, Vt, yT, res, FT=None, biasK=None, mb=None, MM=None):
            N = nqb * 128
            qtok0 = (q0blk - 16) * 128
            if kind == "fox":
                kbs = list(range(0, q0blk + nqb))
            else:
                kbs = list(range(max(0, q0blk - 16), q0blk + nqb))
            nk = len(kbs)
            L = 4
            tiles = [(h, i, kb) for h in range(8) for i, kb in enumerate(kbs)]
            fq = {}
            ot = {}
            pvbuf = {}
            deferred = []

            def emit_fq(h):
                ps, b_ps = res.pjrot.next()
                mm(ps[:, 0:N], res.sel[:, h * 128:(h + 1) * 128], FT[0:8, q0blk * 128:q0blk * 128 + N], True, True, [], [b_ps])
                fqb, b_fqb = res.fqrot.next()
                act(fqb[:, 0:N], ps[:, 0:N], AF.Copy, [b_ps], [b_fqb])
                fq[h] = (fqb, b_fqb)

            def stage_a(t):
                h, i, kb = tiles[t]
                p, a = h // 2, h % 2
                rs = slice(a * 64, (a + 1) * 64)
                if kind == "fox" and i == 0 and h + 1 < 8:
                    emit_fq(h + 1)
                j = kb - q0blk
                c0 = max(0, j) * 128
                diag = (j >= 0) and kind == "fox"
                S, b_S = res.srot.next()
                mm(S[:, c0:N], KT[:, p, kb * 128:(kb + 1) * 128], QT[a][:, p, c0:N], True, not diag, [b_QT], [b_S])
                if diag:
                    mm(S[:, c0:c0 + 128], identb[:], causal[:], False, True, [], [b_S])
                pt, b_pt = res.ptrot.next()
                if kind == "fox":
                    fqb, b_fqb = fq[h]
                    ssb, b_ssb = res.ssrot.next()
                    fw.op("dve", lambda: nc.vector.tensor_tensor(
                        ssb[:, c0:N], S[:, c0:N], fqb[:, c0:N], ALU.add), reads=[b_S, b_fqb], writes=[b_ssb])
                    act(pt[:, c0:N], ssb[:, c0:N], AF.Exp, [b_ssb], [b_pt], scale=0.125, bias=biasK[:, kb, h:h + 1])
                    pvbuf[t] = (pt, b_pt, c0)
                else:
                    act(pt[:, c0:N], S[:, c0:N], AF.Exp, [b_S], [b_pt], scale=0.125, bias=mb[:, kb:kb + 1])
                    pm, b_pm = res.pmrot.next()
                    rel = q0blk - kb + 3
                    if t % 3 == 2:
                        fw.op("pool", lambda: nc.gpsimd.tensor_tensor(
                            pm[:, c0:N], pt[:, c0:N], MM[:, rel, c0:N], ALU.mult), reads=[b_pt], writes=[b_pm])
                    else:
                        fw.op("dve", lambda: nc.vector.tensor_tensor(
                            pm[:, c0:N], pt[:, c0:N], MM[:, rel, c0:N], ALU.mult), reads=[b_pt], writes=[b_pm])
                    pvbuf[t] = (pm, b_pm, c0)

            def stage_b(t, step):
                h, i, kb = tiles[t]
                p, a = h // 2, h % 2
                rs = slice(a * 64, (a + 1) * 64)
                if i == 0:
                    ot[h] = res.otrot.next()
                oT, b_oT = ot[h]
                pv, b_pv, c0 = pvbuf.pop(t)
                vo = (kb * 8 + h) * 65
                mm(oT[:, c0:N], Vt[:, vo:vo + 128], pv[:, c0:N], i == 0, i == nk - 1, [b_pv], [b_oT])
                if i == nk - 1:
                    rec, b_rec = res.recrot.next()
                    ots, b_ots = res.otsrot.next()
                    act(ots[0:65, 0:N], oT[0:65, 0:N], AF.Copy, [b_oT], [b_ots])
                    if kind == "fox":
                        act(rec[64:65, 0:N], ots[64:65, 0:N], AF.Ln, [b_ots], [b_rec])
                        act(rec[64:65, 0:N], rec[64:65, 0:N], AF.Exp, [b_rec], [b_rec], scale=-1.0)
                    else:
                        fw.op("dve", lambda: nc.vector.reciprocal(rec[64:65, 0:N], ots[64:65, 0:N]),
                              reads=[b_ots], writes=[b_rec])

                    def part2():
                        R, b_R = res.pjrot.next()
                        mm(R[0:64, 0:N], ones32[64:65, 0:64], rec[64:65, 0:N], True, True, [b_rec], [b_R])
                        fw.op("dve", lambda: nc.vector.tensor_tensor(
                            yT[rs, p, qtok0:qtok0 + N], R[0:64, 0:N], ots[0:64, 0:N], ALU.mult),
                            reads=[b_R, b_ots], writes=[])
                    deferred.append((step + 5, part2))

            if kind == "fox":
                emit_fq(0)
            nt = len(tiles)
            for step in range(nt + L):
                if step % 8 == 7:
                    pf_step()
                if step < nt:
                    stage_a(step)
                if step - L >= 0:
                    stage_b(step - L, step)
                while deferred and deferred[0][0] <= step:
                    deferred.pop(0)[1]()
            while deferred:
                deferred.pop(0)[1]()

        class Res:
            pass

        with ExitStack() as ph:
            KdT = T(ph, "KdT", [128, 4, NTOK], BF16)
            Vd_flat = T(ph, "Vd", [128, NBK * 520 + 64], BF16)
            Vd = Vd_flat[:, 0:NBK * 520].rearrange("p (b h d) -> p b h d", b=NBK, h=8)
            fw.op("pool", lambda: nc.gpsimd.memset(Vd_flat[:, NBK * 520:NBK * 520 + 64], 0.0), writes=[])
            MM = T(ph, "MM", [128, 20, 512], BF16)
            b_mm = Buf()
            Wq = T(ph, "Wq", [128, 8, 512], BF16)
            Wqr = T(ph, "Wqr", [128, 8, 512], BF16)
            b_Wq = Buf()
            fw.op("pool", lambda: nc.gpsimd.memset(Vd[:, :, :, 64:65], 1.0), writes=[])
            hrot = Rot([(T(ph, "hT%d" % i, [128, 8, 512], BF16), Buf()) for i in range(2)])
            crot = Rot([(T(ph, "Ct%d" % i, [128, 512], F32), Buf()) for i in range(2)])
            srot_t = Rot([(T(ph, "St%d" % i, [128, 512], F32), Buf()) for i in range(2)])
            t1rot = Rot([(T(ph, "t1_%d" % i, [128, 512], F32), Buf()) for i in range(1)])
            t2rot = Rot([(T(ph, "t2_%d" % i, [128, 512], F32), Buf()) for i in range(1)])

            def rope_proj(pjrot, hT, b_hT, N, W, Wr, b_W, dstT, dcol0, tok0, b_dst_list, dstB=None):
                Ct, b_Ct = crot.next()
                St, b_St = srot_t.next()
                fw.dma("sp", Ct[:, 0:N], ropeC_d[:, tok0:tok0 + N], writes=[b_Ct])
                fw.dma("sp", St[:, 0:N], ropeS_d[:, tok0:tok0 + N], writes=[b_St])
                for p in range(4):
                    psA, b_A = pjrot.next()
                    psB, b_B = pjrot.next()
                    for c in range(8):
                        mm(psA[:, 0:N], W[:, c, p * 128:(p + 1) * 128], hT[:, c, 0:N], c == 0, c == 7, [b_hT, b_W], [b_A])
                    for c in range(8):
                        mm(psB[:, 0:N], Wr[:, c, p * 128:(p + 1) * 128], hT[:, c, 0:N], c == 0, c == 7, [b_hT, b_W], [b_B])
                    t1, b_t1 = t1rot.next()
                    t2, b_t2 = t2rot.next()
                    fw.op("dve", lambda: nc.vector.tensor_tensor(t1[:, 0:N], psA[:, 0:N], Ct[:, 0:N], ALU.mult),
                          reads=[b_A, b_Ct], writes=[b_t1])
                    fw.op("dve", lambda: nc.vector.tensor_tensor(t2[:, 0:N], psB[:, 0:N], St[:, 0:N], ALU.mult),
                          reads=[b_B, b_St], writes=[b_t2])
                    pf_step()
                    if dstB is None:
                        fw.op("pool", lambda p=p: nc.gpsimd.tensor_tensor(dstT[:, p, dcol0:dcol0 + N], t1[:, 0:N], t2[:, 0:N], ALU.add),
                              reads=[b_t1, b_t2], writes=b_dst_list)
                    else:
                        fw.op("pool", lambda p=p: nc.gpsimd.tensor_tensor(dstT[0:64, p, dcol0:dcol0 + N], t1[0:64, 0:N], t2[0:64, 0:N], ALU.add),
                              reads=[b_t1, b_t2], writes=b_dst_list)
                        fw.op("pool", lambda p=p: nc.gpsimd.tensor_tensor(dstB[64:128, p, dcol0:dcol0 + N], t1[64:128, 0:N], t2[64:128, 0:N], ALU.add),
                              reads=[b_t1, b_t2], writes=b_dst_list)

            def make_rot_w(W, Wr, b_W):
                fw.op("pool", lambda: nc.gpsimd.memset(Wr[:], 0.0), writes=[b_W])
                Wv = W[:, :, :].rearrange("p c (h d) -> p c h d", h=8)
                Wrv = Wr[:, :, :].rearrange("p c (h d) -> p c h d", h=8)
                for c in range(8):
                    fw.op("pool", lambda c=c: nc.gpsimd.tensor_copy(Wrv[:, c, :, 0:8], Wv[:, c, :, 8:16]), reads=[b_W], writes=[b_W])
                    fw.op("pool", lambda c=c: nc.gpsimd.tensor_copy(Wrv[:, c, :, 8:16], Wv[:, c, :, 0:8]), reads=[b_W], writes=[b_W])

            with ExitStack() as sp1:
                Wk = T(sp1, "Wk", [128, 8, 512], BF16)
                Wkr = T(sp1, "Wkr", [128, 8, 512], BF16)
                Wv = T(sp1, "Wv", [128, 8, 512], BF16)
                b_W = Buf()
                loadw(Wk, w_in_v, OFF["kb"], 512, 8, b_W)
                make_rot_w(Wk, Wkr, b_W)
                loadw(Wv, w_in_v, OFF["vb"], 512, 8, b_W)
                loadw(Wq, w_in_v, OFF["qb"], 512, 8, b_Wq)
                make_rot_w(Wq, Wqr, b_Wq)
                for i in range(20):
                    fw.dma("pool", MM[:, i, :], mm_d[i], writes=[b_mm], par=True)
                tr = P(sp1, "tr", [128, 1024])
                pjrot = Rot([(P(sp1, "pj%d" % i, [128, 512]), Buf()) for i in range(6)])
                nr = make_norm_res(sp1, tr)
                for (b0, nb, hT, b_hT) in pipelined_chunks(KCH, nr, hrot, gT1):
                    N = nb * 128
                    rope_proj(pjrot, hT, b_hT, N, Wk, Wkr, b_W, KdT, b0 * 128, b0 * 128, [])
                    proj_tokmajor_V(pjrot, hT, b_hT, nb, Wv, b_W, Vd, b0)
                fw.barrier()
            with ExitStack() as sp2:
                b_W = b_Wq
                tr = P(sp2, "tr", [128, 1024])
                res = Res()
                res.pjrot = Rot([(P(sp2, "pj%d" % i, [128, 512]), Buf()) for i in range(1)])
                res.srot = Rot([(P(sp2, "S%d" % i, [128, 512]), Buf()) for i in range(3)])
                res.otrot = Rot([(P(sp2, "oT%d" % i, [128, 512]), Buf()) for i in range(2)])
                res.ptrot = Rot([(T(sp2, "pt%d" % i, [128, 512], BF16), Buf()) for i in range(4)])
                res.pmrot = Rot([(T(sp2, "pm%d" % i, [128, 512], BF16), Buf()) for i in range(6)])
                rope_rot = Rot(res.pjrot.items + res.srot.items)
                res.recrot = Rot([(T(sp2, "rec%d" % i, [65, 512], F32), Buf()) for i in range(2)])
                res.otsrot = Rot([(T(sp2, "ots%d" % i, [65, 512], F32), Buf()) for i in range(2)])
                QdA = T(sp2, "QdA", [128, 4, 512], BF16)
                QdB = T(sp2, "QdB", [128, 4, 512], BF16)
                b_QdT = Buf()
                fw.op("pool", lambda: nc.gpsimd.memset(QdA[:], 0.0), writes=[b_QdT])
                fw.op("pool", lambda: nc.gpsimd.memset(QdB[:], 0.0), writes=[b_QdT])
                nr = make_norm_res(sp2, tr)
                for (b0, nb, hT, b_hT) in pipelined_chunks(QCH, nr, hrot, gT1):
                    N = nb * 128
                    rope_proj(rope_rot, hT, b_hT, N, Wq, Wqr, b_W, QdA, 0, b0 * 128, [b_QdT], dstB=QdB)
                    attention("dil", b0, nb, (QdA, QdB), b_QdT, KdT, Vd_flat, ybT, res,
                              mb=(mbh if nb == 1 else mbo), MM=MM)
                fw.barrier()

        with ExitStack() as ph:
            KfT = T(ph, "KfT", [128, 4, NTOK], BF16)
            Vf_flat = T(ph, "Vf", [128, NBK * 520 + 64], BF16)
            Vf = Vf_flat[:, 0:NBK * 520].rearrange("p (b h d) -> p b h d", b=NBK, h=8)
            fw.op("pool", lambda: nc.gpsimd.memset(Vf_flat[:, NBK * 520:NBK * 520 + 64], 0.0), writes=[])
            FT = T(ph, "FT", [8, NTOK], F32)
            biasO = T(ph, "biasO", [128, NBK, 8], F32)
            biasH = T(ph, "biasH", [128, NBK, 8], F32)
            sel = T(ph, "sel", [8, 8 * 128], F32)
            negb = T(ph, "negb", [8, 1], F32)
            onesr = T(ph, "onesr", [8, 512], F32)
            b_c2 = Buf()
            fw.dma("sp", sel[:], sel_d, writes=[b_c2])
            fw.op("dve", lambda: nc.vector.tensor_scalar(negb[:], bfT[:], -1.0, None, ALU.mult), writes=[b_c2])
            fw.op("dve", lambda: nc.vector.memset(onesr[:], 1.0), writes=[b_c2])
            fw.op("pool", lambda: nc.gpsimd.memset(Vf[:, :, :, 64:65], 1.0), writes=[])
            hrot = Rot([(T(ph, "hT%d" % i, [128, 8, 512], BF16), Buf()) for i in range(2)])
            WqF = T(ph, "WqF", [128, 8, 512], BF16)
            b_WqF = Buf()
            with ExitStack() as sp1:
                Wk = T(sp1, "Wk", [128, 8, 512], BF16)
                Wv = T(sp1, "Wv", [128, 8, 512], BF16)
                Wf = T(sp1, "Wf", [128, 8, 8], BF16)
                b_W = Buf()
                loadw(Wk, w_in_v, OFF["ka"], 512, 8, b_W)
                loadw(Wv, w_in_v, OFF["va"], 512, 8, b_W)
                loadw(Wf, w_in_v, OFF["fa"], 8, 8, b_W)
                loadw(WqF, w_in_v, OFF["qa"], 512, 8, b_WqF)
                tr = P(sp1, "tr", [128, 1024])
                pjrot = Rot([(P(sp1, "pj%d" % i, [128, 512]), Buf()) for i in range(6)])
                nr = make_norm_res(sp1, tr)
                elrot = Rot([(T(sp1, "el%d" % i, [8, 512], F32), Buf()) for i in range(2)])
                b_FT = Buf()
                b_bias = Buf()
                prev_end = None
                for (b0, nb, hT, b_hT) in pipelined_chunks(KCH, nr, hrot, gT1):
                    N = nb * 128
                    t0 = b0 * 128
                    for p in range(4):
                        ps, b_ps = pjrot.next()
                        for c in range(8):
                            mm(ps[:, 0:N], Wk[:, c, p * 128:(p + 1) * 128], hT[:, c, 0:N], c == 0, c == 7, [b_hT, b_W], [b_ps])
                        act(KfT[:, p, t0:t0 + N], ps[:, 0:N], AF.Copy, [b_ps], [])
                        pf_step()
                    proj_tokmajor_V(pjrot, hT, b_hT, nb, Wv, b_W, Vf, b0)
                    ps, b_ps = pjrot.next()
                    for c in range(8):
                        mm(ps[0:8, 0:N], Wf[:, c, 0:8], hT[:, c, 0:N], c == 0, c == 7, [b_hT, b_W], [b_ps])
                    el, b_el = elrot.next()
                    act(el[:, 0:N], ps[0:8, 0:N], AF.Exp, [b_ps, b_c2], [b_el], scale=-1.0, bias=negb[:, 0:1])
                    act(el[:, 0:N], el[:, 0:N], AF.Ln, [b_el], [b_el], bias=1.0)
                    init = 0.0 if prev_end is None else FT[:, prev_end - 1:prev_end]
                    fw.op("dve", lambda el=el, init=init, t0=t0, N=N: nc.vector.tensor_tensor_scan(
                        FT[:, t0:t0 + N], onesr[:, 0:N], el[:, 0:N], init, ALU.mult, ALU.subtract),
                        reads=[b_el, b_FT, b_c2], writes=[b_FT], small=True)
                    prev_end = t0 + N
                    for bi in range(nb):
                        blk = b0 + bi
                        ps2, b_ps2 = pjrot.next()
                        fw.op("pe", lambda ps2=ps2, blk=blk: nc.tensor.transpose(ps2[:, 0:8], FT[0:8, blk * 128:(blk + 1) * 128], identf[0:8, 0:8]),
                              reads=[b_FT], writes=[b_ps2])
                        fw.op("dve", lambda ps2=ps2, blk=blk: nc.vector.tensor_scalar(
                            biasO[:, blk, :], ps2[:, 0:8], -1.0, mbo[:, blk:blk + 1], ALU.mult, ALU.add),
                            reads=[b_ps2], writes=[b_bias])
                        fw.op("dve", lambda ps2=ps2, blk=blk: nc.vector.tensor_scalar(
                            biasH[:, blk, :], ps2[:, 0:8], -1.0, mbh[:, blk:blk + 1], ALU.mult, ALU.add),
                            reads=[b_ps2], writes=[b_bias])
                fw.barrier()
            with ExitStack() as sp2:
                Wq = WqF
                b_W = b_WqF
                tr = P(sp2, "tr", [128, 1024])
                res = Res()
                res.sel = sel
                res.pjrot = Rot([(P(sp2, "pj%d" % i, [128, 512]), Buf()) for i in range(1)])
                res.srot = Rot([(P(sp2, "S%d" % i, [128, 512]), Buf()) for i in range(3)])
                res.otrot = Rot([(P(sp2, "oT%d" % i, [128, 512]), Buf()) for i in range(2)])
                res.ptrot = Rot([(T(sp2, "pt%d" % i, [128, 512], BF16), Buf()) for i in range(6)])
                res.ssrot = Rot([(T(sp2, "ss%d" % i, [128, 512], F32), Buf()) for i in range(4)])
                res.fqrot = Rot([(T(sp2, "fq%d" % i, [128, 512], F32), Buf()) for i in range(3)])
                res.recrot = Rot([(T(sp2, "rec%d" % i, [65, 512], F32), Buf()) for i in range(2)])
                res.otsrot = Rot([(T(sp2, "ots%d" % i, [65, 512], F32), Buf()) for i in range(2)])
                QfA = T(sp2, "QfA", [128, 4, 512], BF16)
                QfB = T(sp2, "QfB", [128, 4, 512], BF16)
                b_QfT = Buf()
                fw.op("pool", lambda: nc.gpsimd.memset(QfA[:], 0.0), writes=[b_QfT])
                fw.op("pool", lambda: nc.gpsimd.memset(QfB[:], 0.0), writes=[b_QfT])
                nr = make_norm_res(sp2, tr)
                for (b0, nb, hT, b_hT) in pipelined_chunks(QCH, nr, hrot, gT1):
                    N = nb * 128
                    for p in range(4):
                        ps, b_ps = res.pjrot.next()
                        for c in range(8):
                            mm(ps[:, 0:N], Wq[:, c, p * 128:(p + 1) * 128], hT[:, c, 0:N], c == 0, c == 7, [b_hT, b_W], [b_ps])
                        act(QfA[0:64, p, 0:N], ps[0:64, 0:N], AF.Copy, [b_ps], [b_QfT])
                        act(QfB[64:128, p, 0:N], ps[64:128, 0:N], AF.Copy, [b_ps], [b_QfT])
                    attention("fox", b0, nb, (QfA, QfB), b_QfT, KfT, Vf_flat, yaT, res, FT=FT,
                              biasK=(biasH if nb == 1 else biasO))
                fw.barrier()

        b_x1s = [Buf() for _ in range(17)]
        with ExitStack() as ph:
            Wg = T(ph, "Wg", [128, 8, 2048], BF16)
            wof = T(ph, "wof", [128, 4, D], BF16)
            wod = T(ph, "wod", [128, 4, D], BF16)
            wout = T(ph, "wout", [128, 8, D], BF16)
            gB1 = T(ph, "gB1", [128, D], F32)
            b_W = Buf()
            loadw(Wg, w_in_v, OFF["ga"], 2048, 8, b_W)
            loadw(wof, wof_v, 0, D, 4, b_W)
            loadw(wod, wod_v, 0, D, 4, b_W)
            loadw(wout, w_out_v, 0, D, 8, b_W)
            fw.dma("sp", gB1[:], gB1_d, writes=[b_W], par=True)
            tr = P(ph, "tr", [128, 1024])
            g01 = P(ph, "g01", [128, 1024])
            g23 = P(ph, "g23", [128, 1024])
            gb_ = [Buf() for _ in range(4)]
            grot = Rot([(g01[:, 0:512], gb_[0]), (g01[:, 512:1024], gb_[1]),
                        (g23[:, 0:512], gb_[2]), (g23[:, 512:1024], gb_[3])])
            yps0 = P(ph, "yps", [128, 1024])
            yrot = Rot([(yps0, [Buf()]), (g01, [gb_[0], gb_[1]])])
            nr = make_norm_res(ph, tr)
            hrotm = Rot([(T(ph, "hTm%d" % i, [128, 8, 512], BF16), Buf()) for i in range(2)])
            mixT = T(ph, "mixT", [128, 8, 512], BF16)
            b_mix = Buf()
            sarot = Rot([(T(ph, "sa%d" % i, [128, 512], F32), Buf()) for i in range(2)])
            sbrot = Rot([(T(ph, "sb%d" % i, [128, 512], F32), Buf()) for i in range(2)])
            tmp = T(ph, "tmpm", [128, D], F32)
            b_tmp = Buf()
            x1rot = Rot([(T(ph, "x1t%d" % i, [128, D], F32), Buf()) for i in range(2)])
            if debug:
                out_toks.append(fw.dma("sp", dbg["ya"], yaT[:, :, :]))
                out_toks.append(fw.dma("sp", dbg["yb"], ybT[:, :, :]))
            for (b0, nb, hTm, b_hTm) in pipelined_chunks(QCH, nr, hrotm, gT1):
                N = nb * 128
                qtok0 = (b0 - 16) * 128
                for fc in range(8):
                    ga, b_ga = grot.next()
                    gb, b_gb = grot.next()
                    yap, b_yap = grot.next()
                    ybp, b_ybp = grot.next()
                    for c in range(8):
                        mm(ga[:, 0:N], Wg[:, c, fc * 128:(fc + 1) * 128], hTm[:, c, 0:N], c == 0, c == 7, [b_hTm, b_W], [b_ga])
                    for c in range(8):
                        mm(gb[:, 0:N], Wg[:, c, 1024 + fc * 128:1024 + (fc + 1) * 128], hTm[:, c, 0:N], c == 0, c == 7, [b_hTm, b_W], [b_gb])
                    for p in range(4):
                        mm(yap[:, 0:N], wof[:, p, fc * 128:(fc + 1) * 128], yaT[:, p, qtok0:qtok0 + N], p == 0, p == 3, [b_W], [b_yap])
                    for p in range(4):
                        mm(ybp[:, 0:N], wod[:, p, fc * 128:(fc + 1) * 128], ybT[:, p, qtok0:qtok0 + N], p == 0, p == 3, [b_W], [b_ybp])
                    sa, b_sa = sarot.next()
                    sb, b_sb = sbrot.next()
                    act(sa[:, 0:N], ga[:, 0:N], AF.Sigmoid, [b_ga], [b_sa])
                    act(sb[:, 0:N], gb[:, 0:N], AF.Sigmoid, [b_gb], [b_sb])
                    fw.op("dve", lambda: nc.vector.tensor_tensor(sa[:, 0:N], yap[:, 0:N], sa[:, 0:N], ALU.mult),
                          reads=[b_yap, b_sa], writes=[b_sa])
                    fw.op("dve", lambda: nc.vector.tensor_tensor(sb[:, 0:N], ybp[:, 0:N], sb[:, 0:N], ALU.mult),
                          reads=[b_ybp, b_sb], writes=[b_sb])
                    fw.op("pool", lambda fc=fc: nc.gpsimd.tensor_tensor(mixT[:, fc, 0:N], sa[:, 0:N], sb[:, 0:N], ALU.add),
                          reads=[b_sa, b_sb], writes=[b_mix])
                    pf_step()
                for bi in range(nb):
                    blk = b0 + bi
                    yps, yb_l = yrot.next()
                    for half in range(2):
                        for fc in range(8):
                            mm(yps[:, half * 512:(half + 1) * 512], mixT[:, fc, bi * 128:(bi + 1) * 128],
                               wout[:, fc, half * 512:(half + 1) * 512], fc == 0, fc == 7, [b_mix, b_W], yb_l)
                    st, b_st = rstd_from(nr, yps[:, :], yb_l)
                    xt, b_xt = nr.xrot.next()
                    fw.dma("sp", xt[:], xk[blk * 128:(blk + 1) * 128, :], writes=[b_xt])
                    fw.op("dve", lambda st=st: nc.vector.scalar_tensor_tensor(tmp[:], yps[:, :], st[:, 2:3], gB1[:], ALU.mult, ALU.mult),
                          reads=yb_l + [b_st, b_W], writes=[b_tmp])
                    x1t, b_x1t = x1rot.next()
                    fw.op("pool", lambda xt=xt, x1t=x1t: nc.gpsimd.tensor_tensor(x1t[:], tmp[:], xt[:], ALU.add),
                          reads=[b_tmp, b_xt], writes=[b_x1t])
                    fw.dma("sp", x1s[(blk - 16) * 128:(blk - 15) * 128, :], x1t[:], reads=[b_x1t], writes=[b_x1s[blk - 16]])
                    pf_step()
                    if debug:
                        out_toks.append(fw.dma("sp", dbg["x1"][(blk - 16) * 128:(blk - 15) * 128, :], x1t[:], reads=[b_x1t]))
            fw.barrier()
        att.close()

        with ExitStack() as ph:
            Wup = T(ph, "Wup", [128, 8, 2 * DFF], BF16)
            Wd = T(ph, "Wd", [128, 22, D], BF16)
            gB2 = T(ph, "gB2", [128, D], F32)
            cw = T(ph, "cw", [128, 3 * NFC], F32)
            cb = T(ph, "cb", [128, NFC], F32)
            b_W = Buf()
            NG = 11
            b_Wg = [Buf() for _ in range(NG)]
            for g in range(NG):
                for part in range(2):
                    c0_ = part * DFF + g * 256
                    for ch in range(2):
                        fw.dma("pool", Wup[:, ch * 4:(ch + 1) * 4, c0_:c0_ + 256], w_up_v[:, ch * 4:(ch + 1) * 4, c0_:c0_ + 256],
                               writes=[b_Wg[g]], par=True)
            loadw(Wd, w_down_v, 0, D, 22, b_W)
            b_sm = Buf()
            fw.dma("sp", gB2[:], gB2_d, writes=[b_sm], par=True)
            fw.dma("sp", cw[:], cw_d, writes=[b_sm], par=True)
            fw.dma("sp", cb[:], cb_d, writes=[b_sm], par=True)
            tr = P(ph, "tr", [128, 1024])
            u01 = P(ph, "u01", [128, 1024])
            u23 = P(ph, "u23", [128, 1024])
            ub_ = [Buf() for _ in range(4)]
            urot = Rot([(u01[:, 0:512], ub_[0]), (u01[:, 512:1024], ub_[1]),
                        (u23[:, 0:512], ub_[2]), (u23[:, 512:1024], ub_[3])])
            yps0 = P(ph, "yps", [128, 1024])
            yrot = Rot([(yps0, [Buf()]), (u01, [ub_[0], ub_[1]])])
            nr = make_norm_res(ph, tr)
            h2rot = Rot([(T(ph, "h2T%d" % i, [128, 8, 512], BF16), Buf()) for i in range(2)])
            h2h = T(ph, "h2h", [128, 8, 128], BF16)
            b_h2h = Buf()
            mT = T(ph, "mT", [128, 22, 512], BF16)
            b_mT = Buf()
            carry = T(ph, "carry", [128, NFC, 2], F32)
            b_carry = Buf()
            yarot = Rot([(T(ph, "Ya%d" % i, [128, 512], F32), Buf()) for i in range(2)])
            ybrot = Rot([(T(ph, "Yb%d" % i, [128, 512], F32), Buf()) for i in range(2)])
            sqrot = Rot([(T(ph, "sq%d" % i, [128, 512], F32), Buf()) for i in range(2)])
            orot = Rot([(T(ph, "ot%d" % i, [128, D], F32), Buf()) for i in range(1)])
            norm_transpose(nr, x1s[0:128, :], [b_x1s[0]], h2h, b_h2h, 0, gT2)
            def halo_carry(fc, bw):
                ps, b_ps = urot.next()
                for c in range(8):
                    mm(ps[:, 0:2], Wup[:, c, fc * 128:(fc + 1) * 128], h2h[:, c, 126:128], c == 0, c == 7, [b_h2h, bw], [b_ps])
                fw.op("dve", lambda: nc.vector.tensor_scalar(carry[:, fc, :], ps[:, 0:2], hflag[:, 0:1], None, ALU.mult),
                      reads=[b_ps], writes=[b_carry], small=True)
            def prep_ffn_gen(ci, out):
                h2T_, b_h2T_ = h2rot.next()
                out.append((h2T_, b_h2T_))
                for bi in range(4):
                    r0 = (1 + 4 * ci + bi) * 128
                    yield from norm_transpose_gen(nr, x1s[r0:r0 + 128, :], [b_x1s[1 + 4 * ci + bi]], h2T_, b_h2T_, bi * 128, gT2)

            h2out = []
            for _ in prep_ffn_gen(0, h2out):
                pass
            for ci in range(4):
                h2T, b_h2T = h2out[ci]
                pend_a = []
                pend_b = []

                def stage1(f):
                    Ys = []
                    bw = b_Wg[f // 2]
                    if ci == 0:
                        halo_carry(f, bw)
                        halo_carry(22 + f, bw)
                    for which, fc in ((0, f), (1, 22 + f)):
                        ps, b_ps = urot.next()
                        for c in range(8):
                            mm(ps[:, :], Wup[:, c, fc * 128:(fc + 1) * 128], h2T[:, c, :], c == 0, c == 7, [b_h2T, bw], [b_ps])
                        Y, b_Y = (yarot if which == 0 else ybrot).next()
                        act(Y[:, :], ps[:, :], AF.Identity, [b_ps, b_sm], [b_Y],
                            scale=cw[:, 2 * NFC + fc:2 * NFC + fc + 1], bias=cb[:, fc:fc + 1])
                        w1 = cw[:, NFC + fc:NFC + fc + 1]
                        w0 = cw[:, fc:fc + 1]
                        fw.op("dve", lambda: nc.vector.scalar_tensor_tensor(
                            Y[:, 1:512], ps[:, 0:511], w1, Y[:, 1:512], ALU.mult, ALU.add), reads=[b_ps, b_Y], writes=[b_Y])
                        fw.op("dve", lambda: nc.vector.scalar_tensor_tensor(
                            Y[:, 2:512], ps[:, 0:510], w0, Y[:, 2:512], ALU.mult, ALU.add), reads=[b_ps, b_Y], writes=[b_Y])
                        fw.op("dve", lambda: nc.vector.scalar_tensor_tensor(
                            Y[:, 0:1], carry[:, fc, 1:2], w1, Y[:, 0:1], ALU.mult, ALU.add), reads=[b_carry, b_Y], writes=[b_Y], small=True)
                        fw.op("dve", lambda: nc.vector.scalar_tensor_tensor(
                            Y[:, 0:2], carry[:, fc, 0:2], w0, Y[:, 0:2], ALU.mult, ALU.add), reads=[b_carry, b_Y], writes=[b_Y], small=True)
                        fw.op("dve", lambda: nc.vector.tensor_copy(carry[:, fc, :], ps[:, 510:512]),
                              reads=[b_ps], writes=[b_carry], small=True)
                        Ys.append((Y, b_Y))
                    pend_a.append((f, Ys))

                def stage2a(f, Ys):
                    (Ya, b_Ya), (Yb, b_Yb) = Ys
                    sq, b_sq = sqrot.next()
                    act(sq[:, :], Ya[:, :], AF.Gelu_apprx_tanh, [b_Ya], [b_sq])
                    fw.op("pool", lambda: nc.gpsimd.tensor_tensor(mT[:, f, :], sq[:, :], Yb[:, :], ALU.mult),
                          reads=[b_sq, b_Yb], writes=[b_mT])

                for it in range(22 + 1):
                    if it < 22:
                        stage1(it)
                    if len(pend_a) > 0 and (it >= 1):
                        stage2a(*pend_a.pop(0))
                    if it == 2 and ci + 1 < 4:
                        PFState.gen = prep_ffn_gen(ci + 1, h2out)
                    if it >= 2:
                        pf_step()
                while pend_a:
                    stage2a(*pend_a.pop(0))
                pf_finish()
                for bi in range(4):
                    lb = 4 * ci + bi
                    yps, yb_l = yrot.next()
                    for half in range(2):
                        for f in range(22):
                            mm(yps[:, half * 512:(half + 1) * 512], mT[:, f, bi * 128:(bi + 1) * 128],
                               Wd[:, f, half * 512:(half + 1) * 512], f == 0, f == 21, [b_mT, b_W], yb_l)
                    st, b_st = rstd_from(nr, yps[:, :], yb_l)
                    xt, b_xt = nr.xrot.next()
                    fw.dma("sp", xt[:], x1s[(1 + lb) * 128:(2 + lb) * 128, :], reads=[b_x1s[1 + lb]], writes=[b_xt])
                    ot, b_ot = orot.next()
                    fw.op("dve", lambda st=st, ot=ot: nc.vector.scalar_tensor_tensor(ot[:], yps[:, :], st[:, 2:3], gB2[:], ALU.mult, ALU.mult),
                          reads=yb_l + [b_st, b_sm], writes=[b_ot])
                    fw.op("pool", lambda xt=xt, ot=ot: nc.gpsimd.tensor_tensor(ot[:], ot[:], xt[:], ALU.add),
                          reads=[b_ot, b_xt], writes=[b_ot])
                    out_toks.append(fw.dma("sp", y_out[lb * 128:(lb + 1) * 128, :], ot[:], reads=[b_ot]))
            for t in out_toks:
                fw._wait("sp", t)
            fw.barrier()
    return nc


def _constants():
    c = {}
    k = np.arange(128)[:, None]
    q = np.arange(512)[None, :]
    mmask = np.zeros((20, 128, 512), np.float32)
    for idx in range(20):
        rel = idx - 3
        dlt = rel * 128 + q - k
        m = ((dlt >= 0) & (dlt <= 128)).astype(np.float32)
        m += ((dlt >= 0) & (dlt <= 512) & (dlt % 4 == 0)).astype(np.float32)
        m += ((dlt >= 0) & (dlt <= 2048) & (dlt % 16 == 0)).astype(np.float32)
        mmask[idx] = m
    c["mmask"] = mmask
    kk = np.arange(128)[:, None]
    qq = np.arange(128)[None, :]
    c["causal"] = np.where(kk <= qq, 0.0, -240000.0).astype(np.float32)
    c["ident"] = np.eye(128, dtype=np.float32)
    sel = np.zeros((8, 8, 128), np.float32)
    for h in range(8):
        sel[h, h, :] = 8.0
    c["sel"] = sel.reshape(8, 8 * 128)
    return c


def _rope_tables(base):
    half = 8
    inv_freq = (np.float32(500000.0) ** (-(np.arange(half, dtype=np.float32) * np.float32(2.0) / np.float32(16.0)))).astype(np.float32)
    pos = np.maximum(np.arange(NTOK) + base, 0).astype(np.float32)
    ang = (pos[:, None] * inv_freq[None, :]).astype(np.float32)
    cos = np.cos(ang.astype(np.float64)).astype(np.float32).T
    sin = np.sin(ang.astype(np.float64)).astype(np.float32).T
    C = np.ones((128, NTOK), np.float32)
    S = np.zeros((128, NTOK), np.float32)
    for a in range(2):
        C[a * 64:a * 64 + 8] = cos
        C[a * 64 + 8:a * 64 + 16] = cos
        S[a * 64:a * 64 + 8] = -sin
        S[a * 64 + 8:a * 64 + 16] = sin
    return C, S


_PROG = {}


def kernel(x, g_pre_mix, w_in, b_forget, w_o_fox, w_o_dil, w_out, g_post_mix,
           g_pre_ffn, w_up, conv_w, conv_b, w_down, g_post_ffn, _debug=False):
    f32 = np.float32
    x = np.asarray(x, f32)
    B, S, _ = x.shape
    consts = _constants()
    shared = {
        "w_in": np.ascontiguousarray(np.asarray(w_in, f32)[0]),
        "w_o_fox": np.ascontiguousarray(np.asarray(w_o_fox, f32)[0]),
        "w_o_dil": np.ascontiguousarray(np.asarray(w_o_dil, f32)[0]),
        "w_out": np.ascontiguousarray(np.asarray(w_out, f32)[0]),
        "w_up": np.ascontiguousarray(np.asarray(w_up, f32)[0]),
        "w_down": np.ascontiguousarray(np.asarray(w_down, f32)[0]),
        "gT1": np.ascontiguousarray(np.asarray(g_pre_mix, f32)[0].reshape(8, 128).T),
        "gT2": np.ascontiguousarray(np.asarray(g_pre_ffn, f32)[0].reshape(8, 128).T),
        "gB1": np.ascontiguousarray(np.broadcast_to(np.asarray(g_post_mix, f32)[0][None, :], (128, D))),
        "gB2": np.ascontiguousarray(np.broadcast_to(np.asarray(g_post_ffn, f32)[0][None, :], (128, D))),
        "bf": np.ascontiguousarray(np.asarray(b_forget, f32)[0].reshape(8, 1)),
        "cw": np.ascontiguousarray(np.asarray(conv_w, f32)[0].reshape(3, NFC, 128).transpose(2, 0, 1).reshape(128, 3 * NFC)),
        "cb": np.ascontiguousarray(np.asarray(conv_b, f32)[0].reshape(NFC, 128).T),
    }
    shared.update(consts)
    in_maps = []
    for core in range(8):
        b, h = core // 2, core % 2
        base = 2048 * h - 2176
        xkc = np.zeros((NTOK, D), f32)
        lo = max(0, -base)
        xkc[lo:] = x[b, base + lo:base + NTOK]
        tok = np.arange(NTOK) + base
        valid = tok >= 0
        mbo = np.where(valid, 0.0, -30000.0).astype(f32)
        halo_valid = valid.copy()
        halo_valid[16 * 128:17 * 128] = True
        mbh = np.where(halo_valid, 0.0, -30000.0).astype(f32)
        C, Sn = _rope_tables(base)
        m = dict(shared)
        m["xk"] = xkc
        m["mb_own"] = np.ascontiguousarray(mbo.reshape(NBK, 128).T)
        m["mb_halo"] = np.ascontiguousarray(mbh.reshape(NBK, 128).T)
        m["ropeC"] = C
        m["ropeS"] = Sn
        m["hflag"] = np.full((128, 1), float(h), f32)
        in_maps.append(m)
    key = bool(_debug)
    if key not in _PROG:
        _PROG[key] = build_program(debug=key)
    nc = _PROG[key]
    res = run_bass_kernel_spmd(nc, in_maps, core_ids=list(range(8)))
    out = np.zeros((B, S, D), f32)
    for core in range(8):
        b, h = core // 2, core % 2
        out[b, 2048 * h:2048 * (h + 1)] = res.results[core]["y"]
    if _debug:
        return out, res.results
    return out
```

```python
import numpy as np
from contextlib import ExitStack
import concourse.bass as bass
import concourse.mybir as mybir
from concourse.bass_utils import run_bass_kernel_spmd

F32 = mybir.dt.float32
BF16 = mybir.dt.bfloat16
AF = mybir.ActivationFunctionType
ALU = mybir.AluOpType
NDSEM = 48

D = 1024
NBK = 33
NTOK = NBK * 128
KCH = [(0, 4), (4, 4), (8, 4), (12, 4), (16, 1), (17, 4), (21, 4), (25, 4), (29, 4)]
QCH = KCH[4:]
NQT = 17 * 128
OFF = dict(qa=0, ka=512, va=1024, fa=1536, qb=1544, kb=2056, vb=2568, ga=3080, gb=4104)
DFF = 2816
NFC = 44
EPS = 1e-6


class Buf:
    __slots__ = ("name", "w", "r", "wsmall", "wl")

    def __init__(self, name=""):
        self.name = name
        self.w = None
        self.r = {}
        self.wsmall = False
        self.wl = []


SKIP_SAME_ENGINE_RAW = False


class Rot:
    def __init__(self, items):
        self.items = items
        self.i = 0

    def next(self):
        it = self.items[self.i]
        self.i = (self.i + 1) % len(self.items)
        return it


class FW:
    ENG = ("pe", "act", "dve", "pool", "sp")

    def __init__(self, nc, es):
        self.nc = nc
        self.e = {"pe": nc.tensor, "act": nc.scalar, "dve": nc.vector,
                  "pool": nc.gpsimd, "sp": nc.sync}
        self.sem = {k: es.enter_context(nc.semaphore("s_" + k)) for k in self.ENG}
        self.cnt = {k: 0 for k in self.ENG}
        self.waited = {}
        self.dsems = [es.enter_context(nc.semaphore("d%d" % i)) for i in range(NDSEM)]
        self.dcnt = [0] * NDSEM
        self.dnext = 0
        self.dnext_q = [0, 0]

    def _wait(self, eng, tok):
        kind, key, val = tok
        wk = (eng, kind, key)
        if self.waited.get(wk, 0) >= val:
            return
        sem = self.sem[key] if kind == "e" else self.dsems[key]
        self.e[eng].wait_ge(sem, val)
        self.waited[wk] = val

    def _deps(self, eng, reads, writes, par=False):
        for b in reads:
            for tok in b.wl:
                self._wait(eng, tok)
            if b.w is not None:
                if (SKIP_SAME_ENGINE_RAW and b.w[0] == "e" and b.w[1] == eng and not b.wsmall
                        and eng != "pool"):
                    continue
                self._wait(eng, b.w)
        for b in writes:
            if par:
                continue
            for tok in b.wl:
                self._wait(eng, tok)
            if b.w is not None and not (b.w[0] == "e" and b.w[1] == eng and eng != "pool"):
                self._wait(eng, b.w)
            for tok in b.r.values():
                if tok[0] == "e" and tok[1] == eng and eng != "pool":
                    continue
                self._wait(eng, tok)

    def op(self, eng, fn, reads=(), writes=(), inc=True, small=False):
        self._deps(eng, reads, writes)
        ins = fn()
        if inc:
            self.cnt[eng] += 1
            ins.then_inc(self.sem[eng], 1)
            tok = ("e", eng, self.cnt[eng])
        else:
            tok = ("e", eng, self.cnt[eng] + 1)
        for b in reads:
            b.r[("e", eng)] = tok
        for b in writes:
            b.w = tok
            b.wl = []
            b.r = {}
            b.wsmall = small
        return tok

    def dma(self, q, out_ap, in_ap, reads=(), writes=(), par=False):
        self._deps(q, reads, writes, par=par)
        if par:
            for b in writes:
                for tok in b.r.values():
                    self._wait(q, tok)
        half = NDSEM // 2
        qi = 0 if q == "sp" else 1
        i = qi * half + self.dnext_q[qi]
        self.dnext_q[qi] = (self.dnext_q[qi] + 1) % half
        if self.dcnt[i] > 0:
            self._wait(q, ("d", i, self.dcnt[i]))
        self.dcnt[i] += 16
        self.e[q].dma_start(out=out_ap, in_=in_ap).then_inc(self.dsems[i], 16)
        tok = ("d", i, self.dcnt[i])
        for b in reads:
            b.r[("d", i)] = tok
        for b in writes:
            if par:
                b.wl.append(tok)
            else:
                b.w = tok
                b.wl = []
            b.r = {}
        return tok

    def barrier(self):
        import os
        if os.environ.get("KDEBUG"):
            print("barrier: sbuf remaining", self.nc.sbuf_bytes_remaining, "cnt", dict(self.cnt))
        for k in ("pe", "act", "dve", "pool"):
            if self.cnt[k] > 0:
                self._wait("sp", ("e", k, self.cnt[k]))
        for i in range(NDSEM):
            if self.dcnt[i] > 0:
                self._wait("sp", ("d", i, self.dcnt[i]))
        self.cnt["sp"] += 1
        self.e["sp"].nop().then_inc(self.sem["sp"], 1)
        for k in ("pe", "act", "dve", "pool"):
            self._wait(k, ("e", "sp", self.cnt["sp"]))


def build_program(debug=False):
    nc = bass.Bass("TRN2", target_bir_lowering=False)

    def din(name, shape):
        return nc.dram_tensor(name, shape, F32, kind="ExternalInput").ap()

    xk = din("xk", [NTOK, D])
    w_in = din("w_in", [D, 5128])
    w_o_fox = din("w_o_fox", [512, D])
    w_o_dil = din("w_o_dil", [512, D])
    w_out = din("w_out", [D, D])
    w_up = din("w_up", [D, 2 * DFF])
    w_down = din("w_down", [DFF, D])
    gT1_d = din("gT1", [128, 8])
    gT2_d = din("gT2", [128, 8])
    gB1_d = din("gB1", [128, D])
    gB2_d = din("gB2", [128, D])
    bf_d = din("bf", [8, 1])
    cw_d = din("cw", [128, 3 * NFC])
    cb_d = din("cb", [128, NFC])
    mbo_d = din("mb_own", [128, NBK])
    mbh_d = din("mb_halo", [128, NBK])
    ropeC_d = din("ropeC", [128, NTOK])
    ropeS_d = din("ropeS", [128, NTOK])
    mm_d = din("mmask", [20, 128, 512])
    causal_d = din("causal", [128, 128])
    ident_d = din("ident", [128, 128])
    sel_d = din("sel", [8, 8 * 128])
    hflag_d = din("hflag", [128, 1])
    y_out = nc.dram_tensor("y", [2048, D], F32, kind="ExternalOutput").ap()
    x1s = nc.dram_tensor("x1s", [NQT, D], F32).ap()
    dbg = {}
    if debug:
        dbg["ya"] = nc.dram_tensor("dbg_ya", [128, 4, NQT], BF16, kind="ExternalOutput").ap()
        dbg["yb"] = nc.dram_tensor("dbg_yb", [128, 4, NQT], BF16, kind="ExternalOutput").ap()
        dbg["x1"] = nc.dram_tensor("dbg_x1", [NQT, D], F32, kind="ExternalOutput").ap()

    w_in_v = w_in.rearrange("(c p) n -> p c n", p=128)
    w_up_v = w_up.rearrange("(c p) n -> p c n", p=128)
    w_down_v = w_down.rearrange("(c p) n -> p c n", p=128)
    w_out_v = w_out.rearrange("(c p) n -> p c n", p=128)
    wof_v = w_o_fox.rearrange("(c p) n -> p c n", p=128)
    wod_v = w_o_dil.rearrange("(c p) n -> p c n", p=128)

    out_toks = []
    with ExitStack() as es:
        fw = FW(nc, es)

        uid = [0]

        def T(stack, name, shape, dt):
            uid[0] += 1
            return stack.enter_context(nc.sbuf_tensor("sb%d_%s" % (uid[0], name), shape, dt))

        def P(stack, name, shape, dt=F32):
            uid[0] += 1
            return stack.enter_context(nc.psum_tensor("ps%d_%s" % (uid[0], name), shape, dt))

        def mm(out, lhsT, rhs, start, stop, reads, writes, inc=None):
            return fw.op("pe", lambda: nc.tensor.matmul(out, lhsT, rhs, start=start, stop=stop),
                         reads, writes, inc=(stop if inc is None else inc))

        def act(out, in_, func, reads, writes, **kw):
            return fw.op("act", lambda: nc.scalar.activation(out, in_, func, **kw), reads, writes)

        def loadw(dst, src_v, col0, ncols, nchunks, b):
            for c in range(nchunks):
                fw.dma("pool", dst[:, c, 0:ncols], src_v[:, c, col0:col0 + ncols], writes=[b], par=True)

        identf = T(es, "identf", [128, 128], F32)
        identb = T(es, "identb", [128, 128], BF16)
        causal = T(es, "causal", [128, 128], BF16)
        ones32 = T(es, "ones32", [128, 64], F32)
        gT1 = T(es, "gT1", [128, 8], F32)
        gT2 = T(es, "gT2", [128, 8], F32)
        mbo = T(es, "mbo", [128, NBK], F32)
        mbh = T(es, "mbh", [128, NBK], F32)
        hflag = T(es, "hflag", [128, 1], F32)
        bfT = T(es, "bfT", [8, 1], F32)
        b_const = Buf("const")
        for dst, src in ((identf, ident_d), (gT1, gT1_d), (gT2, gT2_d), (mbo, mbo_d),
                         (mbh, mbh_d), (hflag, hflag_d), (bfT, bf_d)):
            fw.dma("sp", dst[:], src, writes=[b_const])
        fw.dma("pool", identb[:], ident_d, writes=[b_const])
        fw.dma("pool", causal[:], causal_d, writes=[b_const])
        fw.op("dve", lambda: nc.vector.memset(ones32[:], 1.0), writes=[b_const])

        fw.barrier()

        att = ExitStack()
        ybT = T(att, "ybT", [128, 4, NQT], BF16)
        yaT = T(att, "yaT", [128, 4, NQT], BF16)

        class NormRes:
            pass

        def make_norm_res(stack, tr_ps):
            r = NormRes()
            r.xrot = Rot([(T(stack, "xt%d" % i, [128, D], F32), Buf()) for i in range(2)])
            r.junk = T(stack, "junk", [128, D], BF16)
            r.b_junk = Buf()
            r.strot = Rot([(T(stack, "st%d" % i, [128, 4], F32), Buf()) for i in range(2)])
            r.tr = tr_ps
            r.b_tr = Buf()
            return r

        def rstd_from(r, src_ap, src_bufs):
            st, b_st = r.strot.next()
            fw.op("act", lambda: nc.scalar.activation(r.junk[:], src_ap, AF.Square, accum_out=st[:, 0:1]),
                  reads=src_bufs, writes=[r.b_junk, b_st], small=True)
            fw.op("act", lambda: nc.scalar.activation(st[:, 1:2], st[:, 0:1], AF.Sqrt, scale=1.0 / D, bias=EPS),
                  reads=[b_st], writes=[b_st], small=True)
            fw.op("dve", lambda: nc.vector.reciprocal(st[:, 2:3], st[:, 1:2]), reads=[b_st], writes=[b_st], small=True)
            return st, b_st

        def norm_transpose_gen(r, src_rows_ap, src_bufs, hT, b_hT, col0, gT):
            xt, b_xt = r.xrot.next()
            fw.dma("sp", xt[:], src_rows_ap, reads=src_bufs, writes=[b_xt])
            st, b_st = rstd_from(r, xt[:], [b_xt])
            fw.op("dve", lambda: nc.vector.tensor_scalar(xt[:], xt[:], st[:, 2:3], None, ALU.mult),
                  reads=[b_xt, b_st], writes=[b_xt])
            yield
            for c in range(8):
                fw.op("pe", lambda c=c: nc.tensor.transpose(r.tr[:, c * 128:(c + 1) * 128], xt[:, c * 128:(c + 1) * 128], identf[:]),
                      reads=[b_xt], writes=[r.b_tr], inc=(c == 7))
            yield
            fw.op("dve", lambda: nc.vector.tensor_tensor(
                hT[:, :, col0:col0 + 128], r.tr[:, :].rearrange("p (c t) -> p c t", c=8),
                gT[:, :].unsqueeze(2).to_broadcast([128, 8, 128]), ALU.mult),
                reads=[r.b_tr], writes=[b_hT])
            yield

        def norm_transpose(r, src_rows_ap, src_bufs, hT, b_hT, col0, gT):
            for _ in norm_transpose_gen(r, src_rows_ap, src_bufs, hT, b_hT, col0, gT):
                pass

        class PFState:
            gen = None

        def pf_step(n=1):
            for _ in range(n):
                if PFState.gen is None:
                    return
                try:
                    next(PFState.gen)
                except StopIteration:
                    PFState.gen = None

        def pf_finish():
            while PFState.gen is not None:
                pf_step()

        def pipelined_chunks(chunks, nr, hrot, gT):
            def prep_gen(ch, out):
                b0, nb = ch
                hT, b_hT = hrot.next()
                out.append((b0, nb, hT, b_hT))
                for bi in range(nb):
                    yield from norm_transpose_gen(nr, xk[(b0 + bi) * 128:(b0 + bi + 1) * 128, :], [], hT, b_hT, bi * 128, gT)
            out = []
            for _ in prep_gen(chunks[0], out):
                pass
            cur = out[0]
            for i in range(len(chunks)):
                nxt_out = []
                if i + 1 < len(chunks):
                    PFState.gen = prep_gen(chunks[i + 1], nxt_out)
                    pf_step()
                yield cur
                pf_finish()
                cur = nxt_out[0] if nxt_out else None

        def proj_tokmajor_V(pjrot, hT, b_hT, nblk, W, b_W, Vt, blk0):
            for bi in range(nblk):
                ps, b_ps = pjrot.next()
                for c in range(8):
                    mm(ps[:, 0:512], hT[:, c, bi * 128:(bi + 1) * 128], W[:, c, 0:512], c == 0, c == 7, [b_hT, b_W], [b_ps])
                fw.op("dve", lambda ps=ps, bi=bi: nc.vector.tensor_copy(
                    Vt[:, blk0 + bi, :, 0:64], ps[:, 0:512].rearrange("p (h d) -> p h d", h=8)),
                    reads=[b_ps], writes=[])
                pf_step()

        def attention(kind, q0blk, nqb, QT, b_QT, KT, Vt, yT, res, FT=None, biasK=None, mb=None, MM=None):
            N = nqb * 128
            qtok0 = (q0blk - 16) * 128
            if kind == "fox":
                kbs = list(range(0, q0blk + nqb))
            else:
                kbs = list(range(max(0, q0blk - 16), q0blk + nqb))
            nk = len(kbs)
            L = 6
            tiles = [(h, i, kb) for h in range(8) for i, kb in enumerate(kbs)]
            fq = {}
            ot = {}
            pvbuf = {}
            deferred = []

            def emit_fq(h):
                ps, b_ps = res.pjrot.next()
                mm(ps[:, 0:N], res.sel[:, h * 128:(h + 1) * 128], FT[0:8, q0blk * 128:q0blk * 128 + N], True, True, [], [b_ps])
                fqb, b_fqb = res.fqrot.next()
                act(fqb[:, 0:N], ps[:, 0:N], AF.Copy, [b_ps], [b_fqb])
                fq[h] = (fqb, b_fqb)

            def stage_a(t):
                h, i, kb = tiles[t]
                p, a = h // 2, h % 2
                rs = slice(a * 64, (a + 1) * 64)
                if kind == "fox" and i == 0 and h + 1 < 8:
                    emit_fq(h + 1)
                j = kb - q0blk
                c0 = max(0, j) * 128
                diag = (j >= 0) and kind == "fox"
                S, b_S = res.srot.next()
                mm(S[:, c0:N], KT[:, p, kb * 128:(kb + 1) * 128], QT[a][:, p, c0:N], True, not diag, [b_QT], [b_S])
                if diag:
                    mm(S[:, c0:c0 + 128], identb[:], causal[:], False, True, [], [b_S])
                pt, b_pt = res.ptrot.next()
                if kind == "fox":
                    fqb, b_fqb = fq[h]
                    ssb, b_ssb = res.ssrot.next()
                    fw.op("dve", lambda: nc.vector.tensor_tensor(
                        ssb[:, c0:N], S[:, c0:N], fqb[:, c0:N], ALU.add), reads=[b_S, b_fqb], writes=[b_ssb])
                    act(pt[:, c0:N], ssb[:, c0:N], AF.Exp, [b_ssb], [b_pt], scale=0.125, bias=biasK[:, kb, h:h + 1])
                    pvbuf[t] = (pt, b_pt, c0)
                else:
                    act(pt[:, c0:N], S[:, c0:N], AF.Exp, [b_S], [b_pt], scale=0.125, bias=mb[:, kb:kb + 1])
                    pm, b_pm = res.pmrot.next()
                    rel = q0blk - kb + 3
                    if t % 3 == 2:
                        fw.op("pool", lambda: nc.gpsimd.tensor_tensor(
                            pm[:, c0:N], pt[:, c0:N], MM[:, rel, c0:N], ALU.mult), reads=[b_pt], writes=[b_pm])
                    else:
                        fw.op("dve", lambda: nc.vector.tensor_tensor(
                            pm[:, c0:N], pt[:, c0:N], MM[:, rel, c0:N], ALU.mult), reads=[b_pt], writes=[b_pm])
                    pvbuf[t] = (pm, b_pm, c0)

            def stage_b(t, step):
                h, i, kb = tiles[t]
                p, a = h // 2, h % 2
                rs = slice(a * 64, (a + 1) * 64)
                if i == 0:
                    ot[h] = res.otrot.next()
                oT, b_oT = ot[h]
                pv, b_pv, c0 = pvbuf.pop(t)
                vo = (kb * 8 + h) * 65
                mm(oT[:, c0:N], Vt[:, vo:vo + 128], pv[:, c0:N], i == 0, i == nk - 1, [b_pv], [b_oT])
                if i == nk - 1:
                    rec, b_rec = res.recrot.next()
                    ots, b_ots = res.otsrot.next()
                    act(ots[0:65, 0:N], oT[0:65, 0:N], AF.Copy, [b_oT], [b_ots])
                    if kind == "fox":
                        act(rec[64:65, 0:N], ots[64:65, 0:N], AF.Ln, [b_ots], [b_rec])
                        act(rec[64:65, 0:N], rec[64:65, 0:N], AF.Exp, [b_rec], [b_rec], scale=-1.0)
                    else:
                        fw.op("dve", lambda: nc.vector.reciprocal(rec[64:65, 0:N], ots[64:65, 0:N]),
                              reads=[b_ots], writes=[b_rec])

                    def part2():
                        R, b_R = res.pjrot.next()
                        mm(R[0:64, 0:N], ones32[64:65, 0:64], rec[64:65, 0:N], True, True, [b_rec], [b_R])
                        fw.op("dve", lambda: nc.vector.tensor_tensor(
                            yT[rs, p, qtok0:qtok0 + N], R[0:64, 0:N], ots[0:64, 0:N], ALU.mult),
                            reads=[b_R, b_ots], writes=[])
                    deferred.append((step + 7, part2))

            if kind == "fox":
                emit_fq(0)
            nt = len(tiles)
            for step in range(nt + L):
                if step % 8 == 7:
                    pf_step()
                if step < nt:
                    stage_a(step)
                if step - L >= 0:
                    stage_b(step - L, step)
                while deferred and deferred[0][0] <= step:
                    deferred.pop(0)[1]()
            while deferred:
                deferred.pop(0)[1]()

        class Res:
            pass

        with ExitStack() as ph:
            KdT = T(ph, "KdT", [128, 4, NTOK], BF16)
            Vd_flat = T(ph, "Vd", [128, NBK * 520 + 64], BF16)
            Vd = Vd_flat[:, 0:NBK * 520].rearrange("p (b h d) -> p b h d", b=NBK, h=8)
            fw.op("pool", lambda: nc.gpsimd.memset(Vd_flat[:, NBK * 520:NBK * 520 + 64], 0.0), writes=[])
            MM = T(ph, "MM", [128, 20, 512], BF16)
            b_mm = Buf()
            Wq = T(ph, "Wq", [128, 8, 512], BF16)
            Wqr = T(ph, "Wqr", [128, 8, 512], BF16)
            b_Wq = Buf()
            fw.op("pool", lambda: nc.gpsimd.memset(Vd[:, :, :, 64:65], 1.0), writes=[])
            hrot = Rot([(T(ph, "hT%d" % i, [128, 8, 512], BF16), Buf()) for i in range(2)])
            crot = Rot([(T(ph, "Ct%d" % i, [128, 512], F32), Buf()) for i in range(2)])
            srot_t = Rot([(T(ph, "St%d" % i, [128, 512], F32), Buf()) for i in range(2)])
            t1rot = Rot([(T(ph, "t1_%d" % i, [128, 512], F32), Buf()) for i in range(1)])
            t2rot = Rot([(T(ph, "t2_%d" % i, [128, 512], F32), Buf()) for i in range(1)])

            def rope_proj(pjrot, hT, b_hT, N, W, Wr, b_W, dstT, dcol0, tok0, b_dst_list, dstB=None):
                Ct, b_Ct = crot.next()
                St, b_St = srot_t.next()
                fw.dma("sp", Ct[:, 0:N], ropeC_d[:, tok0:tok0 + N], writes=[b_Ct])
                fw.dma("sp", St[:, 0:N], ropeS_d[:, tok0:tok0 + N], writes=[b_St])
                for p in range(4):
                    psA, b_A = pjrot.next()
                    psB, b_B = pjrot.next()
                    for c in range(8):
                        mm(psA[:, 0:N], W[:, c, p * 128:(p + 1) * 128], hT[:, c, 0:N], c == 0, c == 7, [b_hT, b_W], [b_A])
                    for c in range(8):
                        mm(psB[:, 0:N], Wr[:, c, p * 128:(p + 1) * 128], hT[:, c, 0:N], c == 0, c == 7, [b_hT, b_W], [b_B])
                    t1, b_t1 = t1rot.next()
                    t2, b_t2 = t2rot.next()
                    fw.op("dve", lambda: nc.vector.tensor_tensor(t1[:, 0:N], psA[:, 0:N], Ct[:, 0:N], ALU.mult),
                          reads=[b_A, b_Ct], writes=[b_t1])
                    fw.op("dve", lambda: nc.vector.tensor_tensor(t2[:, 0:N], psB[:, 0:N], St[:, 0:N], ALU.mult),
                          reads=[b_B, b_St], writes=[b_t2])
                    pf_step()
                    if dstB is None:
                        fw.op("pool", lambda p=p: nc.gpsimd.tensor_tensor(dstT[:, p, dcol0:dcol0 + N], t1[:, 0:N], t2[:, 0:N], ALU.add),
                              reads=[b_t1, b_t2], writes=b_dst_list)
                    else:
                        fw.op("pool", lambda p=p: nc.gpsimd.tensor_tensor(dstT[0:64, p, dcol0:dcol0 + N], t1[0:64, 0:N], t2[0:64, 0:N], ALU.add),
                              reads=[b_t1, b_t2], writes=b_dst_list)
                        fw.op("pool", lambda p=p: nc.gpsimd.tensor_tensor(dstB[64:128, p, dcol0:dcol0 + N], t1[64:128, 0:N], t2[64:128, 0:N], ALU.add),
                              reads=[b_t1, b_t2], writes=b_dst_list)

            def make_rot_w(W, Wr, b_W):
                fw.op("pool", lambda: nc.gpsimd.memset(Wr[:], 0.0), writes=[b_W])
                Wv = W[:, :, :].rearrange("p c (h d) -> p c h d", h=8)
                Wrv = Wr[:, :, :].rearrange("p c (h d) -> p c h d", h=8)
                for c in range(8):
                    fw.op("pool", lambda c=c: nc.gpsimd.tensor_copy(Wrv[:, c, :, 0:8], Wv[:, c, :, 8:16]), reads=[b_W], writes=[b_W])
                    fw.op("pool", lambda c=c: nc.gpsimd.tensor_copy(Wrv[:, c, :, 8:16], Wv[:, c, :, 0:8]), reads=[b_W], writes=[b_W])

            with ExitStack() as sp1:
                Wk = T(sp1, "Wk", [128, 8, 512], BF16)
                Wkr = T(sp1, "Wkr", [128, 8, 512], BF16)
                Wv = T(sp1, "Wv", [128, 8, 512], BF16)
                b_W = Buf()
                loadw(Wk, w_in_v, OFF["kb"], 512, 8, b_W)
                make_rot_w(Wk, Wkr, b_W)
                loadw(Wv, w_in_v, OFF["vb"], 512, 8, b_W)
                loadw(Wq, w_in_v, OFF["qb"], 512, 8, b_Wq)
                make_rot_w(Wq, Wqr, b_Wq)
                for i in range(20):
                    fw.dma("pool", MM[:, i, :], mm_d[i], writes=[b_mm], par=True)
                tr = P(sp1, "tr", [128, 1024])
                pjrot = Rot([(P(sp1, "pj%d" % i, [128, 512]), Buf()) for i in range(6)])
                nr = make_norm_res(sp1, tr)
                for (b0, nb, hT, b_hT) in pipelined_chunks(KCH, nr, hrot, gT1):
                    N = nb * 128
                    rope_proj(pjrot, hT, b_hT, N, Wk, Wkr, b_W, KdT, b0 * 128, b0 * 128, [])
                    proj_tokmajor_V(pjrot, hT, b_hT, nb, Wv, b_W, Vd, b0)
                fw.barrier()
            with ExitStack() as sp2:
                b_W = b_Wq
                tr = P(sp2, "tr", [128, 1024])
                res = Res()
                res.pjrot = Rot([(P(sp2, "pj%d" % i, [128, 512]), Buf()) for i in range(1)])
                res.srot = Rot([(P(sp2, "S%d" % i, [128, 512]), Buf()) for i in range(3)])
                res.otrot = Rot([(P(sp2, "oT%d" % i, [128, 512]), Buf()) for i in range(2)])
                res.ptrot = Rot([(T(sp2, "pt%d" % i, [128, 512], BF16), Buf()) for i in range(4)])
                res.pmrot = Rot([(T(sp2, "pm%d" % i, [128, 512], BF16), Buf()) for i in range(8)])
                rope_rot = Rot(res.pjrot.items + res.srot.items)
                res.recrot = Rot([(T(sp2, "rec%d" % i, [65, 512], F32), Buf()) for i in range(2)])
                res.otsrot = Rot([(T(sp2, "ots%d" % i, [65, 512], F32), Buf()) for i in range(2)])
                QdA = T(sp2, "QdA", [128, 4, 512], BF16)
                QdB = T(sp2, "QdB", [128, 4, 512], BF16)
                b_QdT = Buf()
                fw.op("pool", lambda: nc.gpsimd.memset(QdA[:], 0.0), writes=[b_QdT])
                fw.op("pool", lambda: nc.gpsimd.memset(QdB[:], 0.0), writes=[b_QdT])
                nr = make_norm_res(sp2, tr)
                for (b0, nb, hT, b_hT) in pipelined_chunks(QCH, nr, hrot, gT1):
                    N = nb * 128
                    rope_proj(rope_rot, hT, b_hT, N, Wq, Wqr, b_W, QdA, 0, b0 * 128, [b_QdT], dstB=QdB)
                    attention("dil", b0, nb, (QdA, QdB), b_QdT, KdT, Vd_flat, ybT, res,
                              mb=(mbh if nb == 1 else mbo), MM=MM)
                fw.barrier()

        with ExitStack() as ph:
            KfT = T(ph, "KfT", [128, 4, NTOK], BF16)
            Vf_flat = T(ph, "Vf", [128, NBK * 520 + 64], BF16)
            Vf = Vf_flat[:, 0:NBK * 520].rearrange("p (b h d) -> p b h d", b=NBK, h=8)
            fw.op("pool", lambda: nc.gpsimd.memset(Vf_flat[:, NBK * 520:NBK * 520 + 64], 0.0), writes=[])
            FT = T(ph, "FT", [8, NTOK], F32)
            biasO = T(ph, "biasO", [128, NBK, 8], F32)
            biasH = T(ph, "biasH", [128, NBK, 8], F32)
            sel = T(ph, "sel", [8, 8 * 128], F32)
            negb = T(ph, "negb", [8, 1], F32)
            onesr = T(ph, "onesr", [8, 512], F32)
            b_c2 = Buf()
            fw.dma("sp", sel[:], sel_d, writes=[b_c2])
            fw.op("dve", lambda: nc.vector.tensor_scalar(negb[:], bfT[:], -1.0, None, ALU.mult), writes=[b_c2])
            fw.op("dve", lambda: nc.vector.memset(onesr[:], 1.0), writes=[b_c2])
            fw.op("pool", lambda: nc.gpsimd.memset(Vf[:, :, :, 64:65], 1.0), writes=[])
            hrot = Rot([(T(ph, "hT%d" % i, [128, 8, 512], BF16), Buf()) for i in range(2)])
            WqF = T(ph, "WqF", [128, 8, 512], BF16)
            b_WqF = Buf()
            with ExitStack() as sp1:
                Wk = T(sp1, "Wk", [128, 8, 512], BF16)
                Wv = T(sp1, "Wv", [128, 8, 512], BF16)
                Wf = T(sp1, "Wf", [128, 8, 8], BF16)
                b_W = Buf()
                loadw(Wk, w_in_v, OFF["ka"], 512, 8, b_W)
                loadw(Wv, w_in_v, OFF["va"], 512, 8, b_W)
                loadw(Wf, w_in_v, OFF["fa"], 8, 8, b_W)
                loadw(WqF, w_in_v, OFF["qa"], 512, 8, b_WqF)
                tr = P(sp1, "tr", [128, 1024])
                pjrot = Rot([(P(sp1, "pj%d" % i, [128, 512]), Buf()) for i in range(6)])
                nr = make_norm_res(sp1, tr)
                elrot = Rot([(T(sp1, "el%d" % i, [8, 512], F32), Buf()) for i in range(2)])
                b_FT = Buf()
                b_bias = Buf()
                prev_end = None
                for (b0, nb, hT, b_hT) in pipelined_chunks(KCH, nr, hrot, gT1):
                    N = nb * 128
                    t0 = b0 * 128
                    for p in range(4):
                        ps, b_ps = pjrot.next()
                        for c in range(8):
                            mm(ps[:, 0:N], Wk[:, c, p * 128:(p + 1) * 128], hT[:, c, 0:N], c == 0, c == 7, [b_hT, b_W], [b_ps])
                        act(KfT[:, p, t0:t0 + N], ps[:, 0:N], AF.Copy, [b_ps], [])
                        pf_step()
                    proj_tokmajor_V(pjrot, hT, b_hT, nb, Wv, b_W, Vf, b0)
                    ps, b_ps = pjrot.next()
                    for c in range(8):
                        mm(ps[0:8, 0:N], Wf[:, c, 0:8], hT[:, c, 0:N], c == 0, c == 7, [b_hT, b_W], [b_ps])
                    el, b_el = elrot.next()
                    act(el[:, 0:N], ps[0:8, 0:N], AF.Exp, [b_ps, b_c2], [b_el], scale=-1.0, bias=negb[:, 0:1])
                    act(el[:, 0:N], el[:, 0:N], AF.Ln, [b_el], [b_el], bias=1.0)
                    init = 0.0 if prev_end is None else FT[:, prev_end - 1:prev_end]
                    fw.op("dve", lambda el=el, init=init, t0=t0, N=N: nc.vector.tensor_tensor_scan(
                        FT[:, t0:t0 + N], onesr[:, 0:N], el[:, 0:N], init, ALU.mult, ALU.subtract),
                        reads=[b_el, b_FT, b_c2], writes=[b_FT], small=True)
                    prev_end = t0 + N
                    for bi in range(nb):
                        blk = b0 + bi
                        ps2, b_ps2 = pjrot.next()
                        fw.op("pe", lambda ps2=ps2, blk=blk: nc.tensor.transpose(ps2[:, 0:8], FT[0:8, blk * 128:(blk + 1) * 128], identf[0:8, 0:8]),
                              reads=[b_FT], writes=[b_ps2])
                        fw.op("dve", lambda ps2=ps2, blk=blk: nc.vector.tensor_scalar(
                            biasO[:, blk, :], ps2[:, 0:8], -1.0, mbo[:, blk:blk + 1], ALU.mult, ALU.add),
                            reads=[b_ps2], writes=[b_bias])
                        fw.op("dve", lambda ps2=ps2, blk=blk: nc.vector.tensor_scalar(
                            biasH[:, blk, :], ps2[:, 0:8], -1.0, mbh[:, blk:blk + 1], ALU.mult, ALU.add),
                            reads=[b_ps2], writes=[b_bias])
                fw.barrier()
            with ExitStack() as sp2:
                Wq = WqF
                b_W = b_WqF
                tr = P(sp2, "tr", [128, 1024])
                res = Res()
                res.sel = sel
                res.pjrot = Rot([(P(sp2, "pj%d" % i, [128, 512]), Buf()) for i in range(1)])
                res.srot = Rot([(P(sp2, "S%d" % i, [128, 512]), Buf()) for i in range(3)])
                res.otrot = Rot([(P(sp2, "oT%d" % i, [128, 512]), Buf()) for i in range(2)])
                res.ptrot = Rot([(T(sp2, "pt%d" % i, [128, 512], BF16), Buf()) for i in range(8)])
                res.ssrot = Rot([(T(sp2, "ss%d" % i, [128, 512], F32), Buf()) for i in range(4)])
                res.fqrot = Rot([(T(sp2, "fq%d" % i, [128, 512], F32), Buf()) for i in range(3)])
                res.recrot = Rot([(T(sp2, "rec%d" % i, [65, 512], F32), Buf()) for i in range(2)])
                res.otsrot = Rot([(T(sp2, "ots%d" % i, [65, 512], F32), Buf()) for i in range(2)])
                QfA = T(sp2, "QfA", [128, 4, 512], BF16)
                QfB = T(sp2, "QfB", [128, 4, 512], BF16)
                b_QfT = Buf()
                fw.op("pool", lambda: nc.gpsimd.memset(QfA[:], 0.0), writes=[b_QfT])
                fw.op("pool", lambda: nc.gpsimd.memset(QfB[:], 0.0), writes=[b_QfT])
                nr = make_norm_res(sp2, tr)
                for (b0, nb, hT, b_hT) in pipelined_chunks(QCH, nr, hrot, gT1):
                    N = nb * 128
                    for p in range(4):
                        ps, b_ps = res.pjrot.next()
                        for c in range(8):
                            mm(ps[:, 0:N], Wq[:, c, p * 128:(p + 1) * 128], hT[:, c, 0:N], c == 0, c == 7, [b_hT, b_W], [b_ps])
                        act(QfA[0:64, p, 0:N], ps[0:64, 0:N], AF.Copy, [b_ps], [b_QfT])
                        act(QfB[64:128, p, 0:N], ps[64:128, 0:N], AF.Copy, [b_ps], [b_QfT])
                    attention("fox", b0, nb, (QfA, QfB), b_QfT, KfT, Vf_flat, yaT, res, FT=FT,
                              biasK=(biasH if nb == 1 else biasO))
                fw.barrier()

        b_x1s = [Buf() for _ in range(17)]
        with ExitStack() as ph:
            Wg = T(ph, "Wg", [128, 8, 2048], BF16)
            wof = T(ph, "wof", [128, 4, D], BF16)
            wod = T(ph, "wod", [128, 4, D], BF16)
            wout = T(ph, "wout", [128, 8, D], BF16)
            gB1 = T(ph, "gB1", [128, D], F32)
            b_W = Buf()
            loadw(Wg, w_in_v, OFF["ga"], 2048, 8, b_W)
            loadw(wof, wof_v, 0, D, 4, b_W)
            loadw(wod, wod_v, 0, D, 4, b_W)
            loadw(wout, w_out_v, 0, D, 8, b_W)
            fw.dma("sp", gB1[:], gB1_d, writes=[b_W], par=True)
            tr = P(ph, "tr", [128, 1024])
            g01 = P(ph, "g01", [128, 1024])
            g23 = P(ph, "g23", [128, 1024])
            gb_ = [Buf() for _ in range(4)]
            grot = Rot([(g01[:, 0:512], gb_[0]), (g01[:, 512:1024], gb_[1]),
                        (g23[:, 0:512], gb_[2]), (g23[:, 512:1024], gb_[3])])
            yps0 = P(ph, "yps", [128, 1024])
            yrot = Rot([(yps0, [Buf()]), (g01, [gb_[0], gb_[1]])])
            nr = make_norm_res(ph, tr)
            hrotm = Rot([(T(ph, "hTm%d" % i, [128, 8, 512], BF16), Buf()) for i in range(2)])
            mixT = T(ph, "mixT", [128, 8, 512], BF16)
            b_mix = Buf()
            sarot = Rot([(T(ph, "sa%d" % i, [128, 512], F32), Buf()) for i in range(2)])
            sbrot = Rot([(T(ph, "sb%d" % i, [128, 512], F32), Buf()) for i in range(2)])
            tmp = T(ph, "tmpm", [128, D], F32)
            b_tmp = Buf()
            x1rot = Rot([(T(ph, "x1t%d" % i, [128, D], F32), Buf()) for i in range(2)])
            if debug:
                out_toks.append(fw.dma("sp", dbg["ya"], yaT[:, :, :]))
                out_toks.append(fw.dma("sp", dbg["yb"], ybT[:, :, :]))
            for (b0, nb, hTm, b_hTm) in pipelined_chunks(QCH, nr, hrotm, gT1):
                N = nb * 128
                qtok0 = (b0 - 16) * 128
                for fc in range(8):
                    ga, b_ga = grot.next()
                    gb, b_gb = grot.next()
                    yap, b_yap = grot.next()
                    ybp, b_ybp = grot.next()
                    for c in range(8):
                        mm(ga[:, 0:N], Wg[:, c, fc * 128:(fc + 1) * 128], hTm[:, c, 0:N], c == 0, c == 7, [b_hTm, b_W], [b_ga])
                    for c in range(8):
                        mm(gb[:, 0:N], Wg[:, c, 1024 + fc * 128:1024 + (fc + 1) * 128], hTm[:, c, 0:N], c == 0, c == 7, [b_hTm, b_W], [b_gb])
                    for p in range(4):
                        mm(yap[:, 0:N], wof[:, p, fc * 128:(fc + 1) * 128], yaT[:, p, qtok0:qtok0 + N], p == 0, p == 3, [b_W], [b_yap])
                    for p in range(4):
                        mm(ybp[:, 0:N], wod[:, p, fc * 128:(fc + 1) * 128], ybT[:, p, qtok0:qtok0 + N], p == 0, p == 3, [b_W], [b_ybp])
                    sa, b_sa = sarot.next()
                    sb, b_sb = sbrot.next()
                    act(sa[:, 0:N], ga[:, 0:N], AF.Sigmoid, [b_ga], [b_sa])
                    act(sb[:, 0:N], gb[:, 0:N], AF.Sigmoid, [b_gb], [b_sb])
                    fw.op("dve", lambda: nc.vector.tensor_tensor(sa[:, 0:N], yap[:, 0:N], sa[:, 0:N], ALU.mult),
                          reads=[b_yap, b_sa], writes=[b_sa])
                    fw.op("dve", lambda: nc.vector.tensor_tensor(sb[:, 0:N], ybp[:, 0:N], sb[:, 0:N], ALU.mult),
                          reads=[b_ybp, b_sb], writes=[b_sb])
                    fw.op("pool", lambda fc=fc: nc.gpsimd.tensor_tensor(mixT[:, fc, 0:N], sa[:, 0:N], sb[:, 0:N], ALU.add),
                          reads=[b_sa, b_sb], writes=[b_mix])
                    pf_step()
                for bi in range(nb):
                    blk = b0 + bi
                    yps, yb_l = yrot.next()
                    for half in range(2):
                        for fc in range(8):
                            mm(yps[:, half * 512:(half + 1) * 512], mixT[:, fc, bi * 128:(bi + 1) * 128],
                               wout[:, fc, half * 512:(half + 1) * 512], fc == 0, fc == 7, [b_mix, b_W], yb_l)
                    st, b_st = rstd_from(nr, yps[:, :], yb_l)
                    xt, b_xt = nr.xrot.next()
                    fw.dma("sp", xt[:], xk[blk * 128:(blk + 1) * 128, :], writes=[b_xt])
                    fw.op("dve", lambda st=st: nc.vector.scalar_tensor_tensor(tmp[:], yps[:, :], st[:, 2:3], gB1[:], ALU.mult, ALU.mult),
                          reads=yb_l + [b_st, b_W], writes=[b_tmp])
                    x1t, b_x1t = x1rot.next()
                    fw.op("pool", lambda xt=xt, x1t=x1t: nc.gpsimd.tensor_tensor(x1t[:], tmp[:], xt[:], ALU.add),
                          reads=[b_tmp, b_xt], writes=[b_x1t])
                    fw.dma("sp", x1s[(blk - 16) * 128:(blk - 15) * 128, :], x1t[:], reads=[b_x1t], writes=[b_x1s[blk - 16]])
                    pf_step()
                    if debug:
                        out_toks.append(fw.dma("sp", dbg["x1"][(blk - 16) * 128:(blk - 15) * 128, :], x1t[:], reads=[b_x1t]))
            fw.barrier()
        att.close()

        with ExitStack() as ph:
            Wup = T(ph, "Wup", [128, 8, 2 * DFF], BF16)
            Wd = T(ph, "Wd", [128, 22, D], BF16)
            gB2 = T(ph, "gB2", [128, D], F32)
            cw = T(ph, "cw", [128, 3 * NFC], F32)
            cb = T(ph, "cb", [128, NFC], F32)
            b_W = Buf()
            NG = 11
            b_Wg = [Buf() for _ in range(NG)]
            for g in range(NG):
                for part in range(2):
                    c0_ = part * DFF + g * 256
                    for ch in range(2):
                        fw.dma("pool", Wup[:, ch * 4:(ch + 1) * 4, c0_:c0_ + 256], w_up_v[:, ch * 4:(ch + 1) * 4, c0_:c0_ + 256],
                               writes=[b_Wg[g]], par=True)
            loadw(Wd, w_down_v, 0, D, 22, b_W)
            b_sm = Buf()
            fw.dma("sp", gB2[:], gB2_d, writes=[b_sm], par=True)
            fw.dma("sp", cw[:], cw_d, writes=[b_sm], par=True)
            fw.dma("sp", cb[:], cb_d, writes=[b_sm], par=True)
            tr = P(ph, "tr", [128, 1024])
            u01 = P(ph, "u01", [128, 1024])
            u23 = P(ph, "u23", [128, 1024])
            ub_ = [Buf() for _ in range(4)]
            urot = Rot([(u01[:, 0:512], ub_[0]), (u01[:, 512:1024], ub_[1]),
                        (u23[:, 0:512], ub_[2]), (u23[:, 512:1024], ub_[3])])
            yps0 = P(ph, "yps", [128, 1024])
            yrot = Rot([(yps0, [Buf()]), (u01, [ub_[0], ub_[1]])])
            nr = make_norm_res(ph, tr)
            h2rot = Rot([(T(ph, "h2T%d" % i, [128, 8, 512], BF16), Buf()) for i in range(2)])
            h2h = T(ph, "h2h", [128, 8, 128], BF16)
            b_h2h = Buf()
            mT = T(ph, "mT", [128, 22, 512], BF16)
            b_mT = Buf()
            carry = T(ph, "carry", [128, NFC, 2], F32)
            b_carry = Buf()
            yarot = Rot([(T(ph, "Ya%d" % i, [128, 512], F32), Buf()) for i in range(2)])
            ybrot = Rot([(T(ph, "Yb%d" % i, [128, 512], F32), Buf()) for i in range(2)])
            sqrot = Rot([(T(ph, "sq%d" % i, [128, 512], F32), Buf()) for i in range(2)])
            orot = Rot([(T(ph, "ot%d" % i, [128, D], F32), Buf()) for i in range(1)])
            norm_transpose(nr, x1s[0:128, :], [b_x1s[0]], h2h, b_h2h, 0, gT2)
            def halo_carry(fc, bw):
                ps, b_ps = urot.next()
                for c in range(8):
                    mm(ps[:, 0:2], Wup[:, c, fc * 128:(fc + 1) * 128], h2h[:, c, 126:128], c == 0, c == 7, [b_h2h, bw], [b_ps])
                fw.op("dve", lambda: nc.vector.tensor_scalar(carry[:, fc, :], ps[:, 0:2], hflag[:, 0:1], None, ALU.mult),
                      reads=[b_ps], writes=[b_carry], small=True)
            def prep_ffn_gen(ci, out):
                h2T_, b_h2T_ = h2rot.next()
                out.append((h2T_, b_h2T_))
                for bi in range(4):
                    r0 = (1 + 4 * ci + bi) * 128
                    yield from norm_transpose_gen(nr, x1s[r0:r0 + 128, :], [b_x1s[1 + 4 * ci + bi]], h2T_, b_h2T_, bi * 128, gT2)

            h2out = []
            for _ in prep_ffn_gen(0, h2out):
                pass
            for ci in range(4):
                h2T, b_h2T = h2out[ci]
                pend_a = []
                pend_b = []

                def stage1(f):
                    Ys = []
                    bw = b_Wg[f // 2]
                    if ci == 0:
                        halo_carry(f, bw)
                        halo_carry(22 + f, bw)
                    for which, fc in ((0, f), (1, 22 + f)):
                        ps, b_ps = urot.next()
                        for c in range(8):
                            mm(ps[:, :], Wup[:, c, fc * 128:(fc + 1) * 128], h2T[:, c, :], c == 0, c == 7, [b_h2T, bw], [b_ps])
                        Y, b_Y = (yarot if which == 0 else ybrot).next()
                        act(Y[:, :], ps[:, :], AF.Identity, [b_ps, b_sm], [b_Y],
                            scale=cw[:, 2 * NFC + fc:2 * NFC + fc + 1], bias=cb[:, fc:fc + 1])
                        w1 = cw[:, NFC + fc:NFC + fc + 1]
                        w0 = cw[:, fc:fc + 1]
                        fw.op("dve", lambda: nc.vector.scalar_tensor_tensor(
                            Y[:, 1:512], ps[:, 0:511], w1, Y[:, 1:512], ALU.mult, ALU.add), reads=[b_ps, b_Y], writes=[b_Y])
                        fw.op("dve", lambda: nc.vector.scalar_tensor_tensor(
                            Y[:, 2:512], ps[:, 0:510], w0, Y[:, 2:512], ALU.mult, ALU.add), reads=[b_ps, b_Y], writes=[b_Y])
                        fw.op("dve", lambda: nc.vector.scalar_tensor_tensor(
                            Y[:, 0:1], carry[:, fc, 1:2], w1, Y[:, 0:1], ALU.mult, ALU.add), reads=[b_carry, b_Y], writes=[b_Y], small=True)
                        fw.op("dve", lambda: nc.vector.scalar_tensor_tensor(
                            Y[:, 0:2], carry[:, fc, 0:2], w0, Y[:, 0:2], ALU.mult, ALU.add), reads=[b_carry, b_Y], writes=[b_Y], small=True)
                        fw.op("dve", lambda: nc.vector.tensor_copy(carry[:, fc, :], ps[:, 510:512]),
                              reads=[b_ps], writes=[b_carry], small=True)
                        Ys.append((Y, b_Y))
                    pend_a.append((f, Ys))

                def stage2a(f, Ys):
                    (Ya, b_Ya), (Yb, b_Yb) = Ys
                    sq, b_sq = sqrot.next()
                    act(sq[:, :], Ya[:, :], AF.Gelu_apprx_tanh, [b_Ya], [b_sq])
                    fw.op("pool", lambda: nc.gpsimd.tensor_tensor(mT[:, f, :], sq[:, :], Yb[:, :], ALU.mult),
                          reads=[b_sq, b_Yb], writes=[b_mT])

                for it in range(22 + 1):
                    if it < 22:
                        stage1(it)
                    if len(pend_a) > 0 and (it >= 1):
                        stage2a(*pend_a.pop(0))
                    if it == 2 and ci + 1 < 4:
                        PFState.gen = prep_ffn_gen(ci + 1, h2out)
                    if it >= 2:
                        pf_step()
                while pend_a:
                    stage2a(*pend_a.pop(0))
                pf_finish()
                for bi in range(4):
                    lb = 4 * ci + bi
                    yps, yb_l = yrot.next()
                    for half in range(2):
                        for f in range(22):
                            mm(yps[:, half * 512:(half + 1) * 512], mT[:, f, bi * 128:(bi + 1) * 128],
                               Wd[:, f, half * 512:(half + 1) * 512], f == 0, f == 21, [b_mT, b_W], yb_l)
                    st, b_st = rstd_from(nr, yps[:, :], yb_l)
                    xt, b_xt = nr.xrot.next()
                    fw.dma("sp", xt[:], x1s[(1 + lb) * 128:(2 + lb) * 128, :], reads=[b_x1s[1 + lb]], writes=[b_xt])
                    ot, b_ot = orot.next()
                    fw.op("dve", lambda st=st, ot=ot: nc.vector.scalar_tensor_tensor(ot[:], yps[:, :], st[:, 2:3], gB2[:], ALU.mult, ALU.mult),
                          reads=yb_l + [b_st, b_sm], writes=[b_ot])
                    fw.op("pool", lambda xt=xt, ot=ot: nc.gpsimd.tensor_tensor(ot[:], ot[:], xt[:], ALU.add),
                          reads=[b_ot, b_xt], writes=[b_ot])
                    out_toks.append(fw.dma("sp", y_out[lb * 128:(lb + 1) * 128, :], ot[:], reads=[b_ot]))
            for t in out_toks:
                fw._wait("sp", t)
            fw.barrier()
    return nc


def _constants():
    c = {}
    k = np.arange(128)[:, None]
    q = np.arange(512)[None, :]
    mmask = np.zeros((20, 128, 512), np.float32)
    for idx in range(20):
        rel = idx - 3
        dlt = rel * 128 + q - k
        m = ((dlt >= 0) & (dlt <= 128)).astype(np.float32)
        m += ((dlt >= 0) & (dlt <= 512) & (dlt % 4 == 0)).astype(np.float32)
        m += ((dlt >= 0) & (dlt <= 2048) & (dlt % 16 == 0)).astype(np.float32)
        mmask[idx] = m
    c["mmask"] = mmask
    kk = np.arange(128)[:, None]
    qq = np.arange(128)[None, :]
    c["causal"] = np.where(kk <= qq, 0.0, -240000.0).astype(np.float32)
    c["ident"] = np.eye(128, dtype=np.float32)
    sel = np.zeros((8, 8, 128), np.float32)
    for h in range(8):
        sel[h, h, :] = 8.0
    c["sel"] = sel.reshape(8, 8 * 128)
    return c


def _rope_tables(base):
    half = 8
    inv_freq = (np.float32(500000.0) ** (-(np.arange(half, dtype=np.float32) * np.float32(2.0) / np.float32(16.0)))).astype(np.float32)
    pos = np.maximum(np.arange(NTOK) + base, 0).astype(np.float32)
    ang = (pos[:, None] * inv_freq[None, :]).astype(np.float32)
    cos = np.cos(ang.astype(np.float64)).astype(np.float32).T
    sin = np.sin(ang.astype(np.float64)).astype(np.float32).T
    C = np.ones((128, NTOK), np.float32)
    S = np.zeros((128, NTOK), np.float32)
    for a in range(2):
        C[a * 64:a * 64 + 8] = cos
        C[a * 64 + 8:a * 64 + 16] = cos
        S[a * 64:a * 64 + 8] = -sin
        S[a * 64 + 8:a * 64 + 16] = sin
    return C, S


_PROG = {}


def kernel(x, g_pre_mix, w_in, b_forget, w_o_fox, w_o_dil, w_out, g_post_mix,
           g_pre_ffn, w_up, conv_w, conv_b, w_down, g_post_ffn, _debug=False):
    f32 = np.float32
    x = np.asarray(x, f32)
    B, S, _ = x.shape
    consts = _constants()
    shared = {
        "w_in": np.ascontiguousarray(np.asarray(w_in, f32)[0]),
        "w_o_fox": np.ascontiguousarray(np.asarray(w_o_fox, f32)[0]),
        "w_o_dil": np.ascontiguousarray(np.asarray(w_o_dil, f32)[0]),
        "w_out": np.ascontiguousarray(np.asarray(w_out, f32)[0]),
        "w_up": np.ascontiguousarray(np.asarray(w_up, f32)[0]),
        "w_down": np.ascontiguousarray(np.asarray(w_down, f32)[0]),
        "gT1": np.ascontiguousarray(np.asarray(g_pre_mix, f32)[0].reshape(8, 128).T),
        "gT2": np.ascontiguousarray(np.asarray(g_pre_ffn, f32)[0].reshape(8, 128).T),
        "gB1": np.ascontiguousarray(np.broadcast_to(np.asarray(g_post_mix, f32)[0][None, :], (128, D))),
        "gB2": np.ascontiguousarray(np.broadcast_to(np.asarray(g_post_ffn, f32)[0][None, :], (128, D))),
        "bf": np.ascontiguousarray(np.asarray(b_forget, f32)[0].reshape(8, 1)),
        "cw": np.ascontiguousarray(np.asarray(conv_w, f32)[0].reshape(3, NFC, 128).transpose(2, 0, 1).reshape(128, 3 * NFC)),
        "cb": np.ascontiguousarray(np.asarray(conv_b, f32)[0].reshape(NFC, 128).T),
    }
    shared.update(consts)
    in_maps = []
    for core in range(8):
        b, h = core // 2, core % 2
        base = 2048 * h - 2176
        xkc = np.zeros((NTOK, D), f32)
        lo = max(0, -base)
        xkc[lo:] = x[b, base + lo:base + NTOK]
        tok = np.arange(NTOK) + base
        valid = tok >= 0
        mbo = np.where(valid, 0.0, -30000.0).astype(f32)
        halo_valid = valid.copy()
        halo_valid[16 * 128:17 * 128] = True
        mbh = np.where(halo_valid, 0.0, -30000.0).astype(f32)
        C, Sn = _rope_tables(base)
        m = dict(shared)
        m["xk"] = xkc
        m["mb_own"] = np.ascontiguousarray(mbo.reshape(NBK, 128).T)
        m["mb_halo"] = np.ascontiguousarray(mbh.reshape(NBK, 128).T)
        m["ropeC"] = C
        m["ropeS"] = Sn
        m["hflag"] = np.full((128, 1), float(h), f32)
        in_maps.append(m)
    key = bool(_debug)
    if key not in _PROG:
        _PROG[key] = build_program(debug=key)
    nc = _PROG[key]
    res = run_bass_kernel_spmd(nc, in_maps, core_ids=list(range(8)))
    out = np.zeros((B, S, D), f32)
    for core in range(8):
        b, h = core // 2, core % 2
        out[b, 2048 * h:2048 * (h + 1)] = res.results[core]["y"]
    if _debug:
        return out, res.results
    return out
```

```python
import numpy as np
from contextlib import ExitStack
import concourse.bass as bass
import concourse.mybir as mybir
from concourse.bass_utils import run_bass_kernel_spmd

F32 = mybir.dt.float32
BF16 = mybir.dt.bfloat16
AF = mybir.ActivationFunctionType
ALU = mybir.AluOpType
NDSEM = 48

D = 1024
NBK = 33
NTOK = NBK * 128
KCH = [(0, 4), (4, 4), (8, 4), (12, 4), (16, 1), (17, 4), (21, 4), (25, 4), (29, 4)]
QCH = KCH[4:]
NQT = 17 * 128
OFF = dict(qa=0, ka=512, va=1024, fa=1536, qb=1544, kb=2056, vb=2568, ga=3080, gb=4104)
DFF = 2816
NFC = 44
EPS = 1e-6


class Buf:
    __slots__ = ("name", "w", "r", "wsmall", "wl")

    def __init__(self, name=""):
        self.name = name
        self.w = None
        self.r = {}
        self.wsmall = False
        self.wl = []


SKIP_SAME_ENGINE_RAW = False


class Rot:
    def __init__(self, items):
        self.items = items
        self.i = 0

    def next(self):
        it = self.items[self.i]
        self.i = (self.i + 1) % len(self.items)
        return it


class FW:
    ENG = ("pe", "act", "dve", "pool", "sp")

    def __init__(self, nc, es):
        self.nc = nc
        self.e = {"pe": nc.tensor, "act": nc.scalar, "dve": nc.vector,
                  "pool": nc.gpsimd, "sp": nc.sync}
        self.sem = {k: es.enter_context(nc.semaphore("s_" + k)) for k in self.ENG}
        self.cnt = {k: 0 for k in self.ENG}
        self.waited = {}
        self.dsems = [es.enter_context(nc.semaphore("d%d" % i)) for i in range(NDSEM)]
        self.dcnt = [0] * NDSEM
        self.dnext = 0
        self.dnext_q = [0, 0]

    def _wait(self, eng, tok):
        kind, key, val = tok
        wk = (eng, kind, key)
        if self.waited.get(wk, 0) >= val:
            return
        sem = self.sem[key] if kind == "e" else self.dsems[key]
        self.e[eng].wait_ge(sem, val)
        self.waited[wk] = val

    def _deps(self, eng, reads, writes, par=False):
        for b in reads:
            for tok in b.wl:
                self._wait(eng, tok)
            if b.w is not None:
                if (SKIP_SAME_ENGINE_RAW and b.w[0] == "e" and b.w[1] == eng and not b.wsmall
                        and eng != "pool"):
                    continue
                self._wait(eng, b.w)
        for b in writes:
            if par:
                continue
            for tok in b.wl:
                self._wait(eng, tok)
            if b.w is not None and not (b.w[0] == "e" and b.w[1] == eng and eng != "pool"):
                self._wait(eng, b.w)
            for tok in b.r.values():
                if tok[0] == "e" and tok[1] == eng and eng != "pool":
                    continue
                self._wait(eng, tok)

    def op(self, eng, fn, reads=(), writes=(), inc=True, small=False):
        self._deps(eng, reads, writes)
        ins = fn()
        if inc:
            self.cnt[eng] += 1
            ins.then_inc(self.sem[eng], 1)
            tok = ("e", eng, self.cnt[eng])
        else:
            tok = ("e", eng, self.cnt[eng] + 1)
        for b in reads:
            b.r[("e", eng)] = tok
        for b in writes:
            b.w = tok
            b.wl = []
            b.r = {}
            b.wsmall = small
        return tok

    def dma(self, q, out_ap, in_ap, reads=(), writes=(), par=False):
        self._deps(q, reads, writes, par=par)
        if par:
            for b in writes:
                for tok in b.r.values():
                    self._wait(q, tok)
        half = NDSEM // 2
        qi = 0 if q == "sp" else 1
        i = qi * half + self.dnext_q[qi]
        self.dnext_q[qi] = (self.dnext_q[qi] + 1) % half
        if self.dcnt[i] > 0:
            self._wait(q, ("d", i, self.dcnt[i]))
        self.dcnt[i] += 16
        self.e[q].dma_start(out=out_ap, in_=in_ap).then_inc(self.dsems[i], 16)
        tok = ("d", i, self.dcnt[i])
        for b in reads:
            b.r[("d", i)] = tok
        for b in writes:
            if par:
                b.wl.append(tok)
            else:
                b.w = tok
                b.wl = []
            b.r = {}
        return tok

    def barrier(self):
        import os
        if os.environ.get("KDEBUG"):
            print("barrier: sbuf remaining", self.nc.sbuf_bytes_remaining, "cnt", dict(self.cnt))
        for k in ("pe", "act", "dve", "pool"):
            if self.cnt[k] > 0:
                self._wait("sp", ("e", k, self.cnt[k]))
        for i in range(NDSEM):
            if self.dcnt[i] > 0:
                self._wait("sp", ("d", i, self.dcnt[i]))
        self.cnt["sp"] += 1
        self.e["sp"].nop().then_inc(self.sem["sp"], 1)
        for k in ("pe", "act", "dve", "pool"):
            self._wait(k, ("e", "sp", self.cnt["sp"]))


def build_program(debug=False):
    nc = bass.Bass("TRN2", target_bir_lowering=False)

    def din(name, shape):
        return nc.dram_tensor(name, shape, F32, kind="ExternalInput").ap()

    xk = din("xk", [NTOK, D])
    w_in = din("w_in", [D, 5128])
    w_o_fox = din("w_o_fox", [512, D])
    w_o_dil = din("w_o_dil", [512, D])
    w_out = din("w_out", [D, D])
    w_up = din("w_up", [D, 2 * DFF])
    w_down = din("w_down", [DFF, D])
    gT1_d = din("gT1", [128, 8])
    gT2_d = din("gT2", [128, 8])
    gB1_d = din("gB1", [128, D])
    gB2_d = din("gB2", [128, D])
    bf_d = din("bf", [8, 1])
    cw_d = din("cw", [128, 3 * NFC])
    cb_d = din("cb", [128, NFC])
    mbo_d = din("mb_own", [128, NBK])
    mbh_d = din("mb_halo", [128, NBK])
    ropeC_d = din("ropeC", [128, NTOK])
    ropeS_d = din("ropeS", [128, NTOK])
    mm_d = din("mmask", [20, 128, 512])
    causal_d = din("causal", [128, 128])
    ident_d = din("ident", [128, 128])
    sel_d = din("sel", [8, 8 * 128])
    hflag_d = din("hflag", [128, 1])
    y_out = nc.dram_tensor("y", [2048, D], F32, kind="ExternalOutput").ap()
    x1s = nc.dram_tensor("x1s", [NQT, D], F32).ap()
    dbg = {}
    if debug:
        dbg["ya"] = nc.dram_tensor("dbg_ya", [128, 4, NQT], BF16, kind="ExternalOutput").ap()
        dbg["yb"] = nc.dram_tensor("dbg_yb", [128, 4, NQT], BF16, kind="ExternalOutput").ap()
        dbg["x1"] = nc.dram_tensor("dbg_x1", [NQT, D], F32, kind="ExternalOutput").ap()

    w_in_v = w_in.rearrange("(c p) n -> p c n", p=128)
    w_up_v = w_up.rearrange("(c p) n -> p c n", p=128)
    w_down_v = w_down.rearrange("(c p) n -> p c n", p=128)
    w_out_v = w_out.rearrange("(c p) n -> p c n", p=128)
    wof_v = w_o_fox.rearrange("(c p) n -> p c n", p=128)
    wod_v = w_o_dil.rearrange("(c p) n -> p c n", p=128)

    out_toks = []
    with ExitStack() as es:
        fw = FW(nc, es)

        uid = [0]

        def T(stack, name, shape, dt):
            uid[0] += 1
            return stack.enter_context(nc.sbuf_tensor("sb%d_%s" % (uid[0], name), shape, dt))

        def P(stack, name, shape, dt=F32):
            uid[0] += 1
            return stack.enter_context(nc.psum_tensor("ps%d_%s" % (uid[0], name), shape, dt))

        def mm(out, lhsT, rhs, start, stop, reads, writes, inc=None):
            return fw.op("pe", lambda: nc.tensor.matmul(out, lhsT, rhs, start=start, stop=stop),
                         reads, writes, inc=(stop if inc is None else inc))

        def act(out, in_, func, reads, writes, **kw):
            return fw.op("act", lambda: nc.scalar.activation(out, in_, func, **kw), reads, writes)

        def loadw(dst, src_v, col0, ncols, nchunks, b):
            for c in range(nchunks):
                fw.dma("pool", dst[:, c, 0:ncols], src_v[:, c, col0:col0 + ncols], writes=[b], par=True)

        identf = T(es, "identf", [128, 128], F32)
        identb = T(es, "identb", [128, 128], BF16)
        causal = T(es, "causal", [128, 128], BF16)
        ones32 = T(es, "ones32", [128, 64], F32)
        gT1 = T(es, "gT1", [128, 8], F32)
        gT2 = T(es, "gT2", [128, 8], F32)
        mbo = T(es, "mbo", [128, NBK], F32)
        mbh = T(es, "mbh", [128, NBK], F32)
        hflag = T(es, "hflag", [128, 1], F32)
        bfT = T(es, "bfT", [8, 1], F32)
        b_const = Buf("const")
        for dst, src in ((identf, ident_d), (gT1, gT1_d), (gT2, gT2_d), (mbo, mbo_d),
                         (mbh, mbh_d), (hflag, hflag_d), (bfT, bf_d)):
            fw.dma("sp", dst[:], src, writes=[b_const])
        fw.dma("pool", identb[:], ident_d, writes=[b_const])
        fw.dma("pool", causal[:], causal_d, writes=[b_const])
        fw.op("dve", lambda: nc.vector.memset(ones32[:], 1.0), writes=[b_const])

        fw.barrier()

        att = ExitStack()
        ybT = T(att, "ybT", [128, 4, NQT], BF16)
        yaT = T(att, "yaT", [128, 4, NQT], BF16)

        class NormRes:
            pass

        def make_norm_res(stack, tr_ps):
            r = NormRes()
            r.xrot = Rot([(T(stack, "xt%d" % i, [128, D], F32), Buf()) for i in range(2)])
            r.junk = T(stack, "junk", [128, D], BF16)
            r.b_junk = Buf()
            r.strot = Rot([(T(stack, "st%d" % i, [128, 4], F32), Buf()) for i in range(2)])
            r.tr = tr_ps
            r.b_tr = Buf()
            return r

        def rstd_from(r, src_ap, src_bufs):
            st, b_st = r.strot.next()
            fw.op("act", lambda: nc.scalar.activation(r.junk[:], src_ap, AF.Square, accum_out=st[:, 0:1]),
                  reads=src_bufs, writes=[r.b_junk, b_st], small=True)
            fw.op("act", lambda: nc.scalar.activation(st[:, 1:2], st[:, 0:1], AF.Sqrt, scale=1.0 / D, bias=EPS),
                  reads=[b_st], writes=[b_st], small=True)
            fw.op("dve", lambda: nc.vector.reciprocal(st[:, 2:3], st[:, 1:2]), reads=[b_st], writes=[b_st], small=True)
            return st, b_st

        def norm_transpose_gen(r, src_rows_ap, src_bufs, hT, b_hT, col0, gT):
            xt, b_xt = r.xrot.next()
            fw.dma("sp", xt[:], src_rows_ap, reads=src_bufs, writes=[b_xt])
            st, b_st = rstd_from(r, xt[:], [b_xt])
            fw.op("dve", lambda: nc.vector.tensor_scalar(xt[:], xt[:], st[:, 2:3], None, ALU.mult),
                  reads=[b_xt, b_st], writes=[b_xt])
            yield
            for c in range(8):
                fw.op("pe", lambda c=c: nc.tensor.transpose(r.tr[:, c * 128:(c + 1) * 128], xt[:, c * 128:(c + 1) * 128], identf[:]),
                      reads=[b_xt], writes=[r.b_tr], inc=(c == 7))
            yield
            fw.op("dve", lambda: nc.vector.tensor_tensor(
                hT[:, :, col0:col0 + 128], r.tr[:, :].rearrange("p (c t) -> p c t", c=8),
                gT[:, :].unsqueeze(2).to_broadcast([128, 8, 128]), ALU.mult),
                reads=[r.b_tr], writes=[b_hT])
            yield

        def norm_transpose(r, src_rows_ap, src_bufs, hT, b_hT, col0, gT):
            for _ in norm_transpose_gen(r, src_rows_ap, src_bufs, hT, b_hT, col0, gT):
                pass

        class PFState:
            gen = None

        def pf_step(n=1):
            for _ in range(n):
                if PFState.gen is None:
                    return
                try:
                    next(PFState.gen)
                except StopIteration:
                    PFState.gen = None

        def pf_finish():
            while PFState.gen is not None:
                pf_step()

        def pipelined_chunks(chunks, nr, hrot, gT):
            def prep_gen(ch, out):
                b0, nb = ch
                hT, b_hT = hrot.next()
                out.append((b0, nb, hT, b_hT))
                for bi in range(nb):
                    yield from norm_transpose_gen(nr, xk[(b0 + bi) * 128:(b0 + bi + 1) * 128, :], [], hT, b_hT, bi * 128, gT)
            out = []
            for _ in prep_gen(chunks[0], out):
                pass
            cur = out[0]
            for i in range(len(chunks)):
                nxt_out = []
                if i + 1 < len(chunks):
                    PFState.gen = prep_gen(chunks[i + 1], nxt_out)
                    pf_step()
                yield cur
                pf_finish()
                cur = nxt_out[0] if nxt_out else None

        def proj_tokmajor_V(pjrot, hT, b_hT, nblk, W, b_W, Vt, blk0):
            for bi in range(nblk):
                ps, b_ps = pjrot.next()
                for c in range(8):
                    mm(ps[:, 0:512], hT[:, c, bi * 128:(bi + 1) * 128], W[:, c, 0:512], c == 0, c == 7, [b_hT, b_W], [b_ps])
                fw.op("dve", lambda ps=ps, bi=bi: nc.vector.tensor_copy(
                    Vt[:, blk0 + bi, :, 0:64], ps[:, 0:512].rearrange("p (h d) -> p h d", h=8)),
                    reads=[b_ps], writes=[])
                pf_step()

        def attention(kind, q0blk, nqb, QT, b_QT, KT, Vt, yT, res, FT=None, biasK=None, mb=None, MM=None):
            N = nqb * 128
            qtok0 = (q0blk - 16) * 128
            if kind == "fox":
                kbs = list(range(0, q0blk + nqb))
            else:
                kbs = list(range(max(0, q0blk - 16), q0blk + nqb))
            nk = len(kbs)
            L = 6
            tiles = [(h, i, kb) for h in range(8) for i, kb in enumerate(kbs)]
            fq = {}
            ot = {}
            pvbuf = {}
            deferred = []

            def emit_fq(h):
                ps, b_ps = res.pjrot.next()
                mm(ps[:, 0:N], res.sel[:, h * 128:(h + 1) * 128], FT[0:8, q0blk * 128:q0blk * 128 + N], True, True, [], [b_ps])
                fqb, b_fqb = res.fqrot.next()
                act(fqb[:, 0:N], ps[:, 0:N], AF.Copy, [b_ps], [b_fqb])
                fq[h] = (fqb, b_fqb)

            def stage_a(t):
                h, i, kb = tiles[t]
                p, a = h // 2, h % 2
                rs = slice(a * 64, (a + 1) * 64)
                if kind == "fox" and i == 0 and h + 1 < 8:
                    emit_fq(h + 1)
                j = kb - q0blk
                c0 = max(0, j) * 128
                diag = (j >= 0) and kind == "fox"
                S, b_S = res.srot.next()
                mm(S[:, c0:N], KT[:, p, kb * 128:(kb + 1) * 128], QT[a][:, p, c0:N], True, not diag, [b_QT], [b_S])
                if diag:
                    mm(S[:, c0:c0 + 128], identb[:], causal[:], False, True, [], [b_S])
                pt, b_pt = res.ptrot.next()
                if kind == "fox":
                    fqb, b_fqb = fq[h]
                    ssb, b_ssb = res.ssrot.next()
                    fw.op("dve", lambda: nc.vector.tensor_tensor(
                        ssb[:, c0:N], S[:, c0:N], fqb[:, c0:N], ALU.add), reads=[b_S, b_fqb], writes=[b_ssb])
                    act(pt[:, c0:N], ssb[:, c0:N], AF.Exp, [b_ssb], [b_pt], scale=0.125, bias=biasK[:, kb, h:h + 1])
                    pvbuf[t] = (pt, b_pt, c0)
                else:
                    act(pt[:, c0:N], S[:, c0:N], AF.Exp, [b_S], [b_pt], scale=0.125, bias=mb[:, kb:kb + 1])
                    pm, b_pm = res.pmrot.next()
                    rel = q0blk - kb + 3
                    if t % 3 == 2:
                        fw.op("pool", lambda: nc.gpsimd.tensor_tensor(
                            pm[:, c0:N], pt[:, c0:N], MM[:, rel, c0:N], ALU.mult), reads=[b_pt], writes=[b_pm])
                    else:
                        fw.op("dve", lambda: nc.vector.tensor_tensor(
                            pm[:, c0:N], pt[:, c0:N], MM[:, rel, c0:N], ALU.mult), reads=[b_pt], writes=[b_pm])
                    pvbuf[t] = (pm, b_pm, c0)

            def stage_b(t, step):
                h, i, kb = tiles[t]
                p, a = h // 2, h % 2
                rs = slice(a * 64, (a + 1) * 64)
                if i == 0:
                    ot[h] = res.otrot.next()
                oT, b_oT = ot[h]
                pv, b_pv, c0 = pvbuf.pop(t)
                vo = (kb * 8 + h) * 65
                mm(oT[:, c0:N], Vt[:, vo:vo + 128], pv[:, c0:N], i == 0, i == nk - 1, [b_pv], [b_oT])
                if i == nk - 1:
                    rec, b_rec = res.recrot.next()
                    ots, b_ots = res.otsrot.next()
                    act(ots[0:65, 0:N], oT[0:65, 0:N], AF.Copy, [b_oT], [b_ots])
                    if kind == "fox":
                        act(rec[64:65, 0:N], ots[64:65, 0:N], AF.Ln, [b_ots], [b_rec])
                        act(rec[64:65, 0:N], rec[64:65, 0:N], AF.Exp, [b_rec], [b_rec], scale=-1.0)
                    else:
                        fw.op("dve", lambda: nc.vector.reciprocal(rec[64:65, 0:N], ots[64:65, 0:N]),
                              reads=[b_ots], writes=[b_rec])

                    def part2():
                        R, b_R = res.pjrot.next()
                        mm(R[0:64, 0:N], ones32[64:65, 0:64], rec[64:65, 0:N], True, True, [b_rec], [b_R])
                        fw.op("dve", lambda: nc.vector.tensor_tensor(
                            yT[rs, p, qtok0:qtok0 + N], R[0:64, 0:N], ots[0:64, 0:N], ALU.mult),
                            reads=[b_R, b_ots], writes=[])
                    deferred.append((step + 7, part2))

            if kind == "fox":
                emit_fq(0)
            nt = len(tiles)
            for step in range(nt + L):
                if step % 8 == 7:
                    pf_step()
                if step < nt:
                    stage_a(step)
                if step - L >= 0:
                    stage_b(step - L, step)
                while deferred and deferred[0][0] <= step:
                    deferred.pop(0)[1]()
            while deferred:
                deferred.pop(0)[1]()

        class Res:
            pass

        with ExitStack() as ph:
            KdT = T(ph, "KdT", [128, 4, NTOK], BF16)
            Vd_flat = T(ph, "Vd", [128, NBK * 520 + 64], BF16)
            Vd = Vd_flat[:, 0:NBK * 520].rearrange("p (b h d) -> p b h d", b=NBK, h=8)
            fw.op("pool", lambda: nc.gpsimd.memset(Vd_flat[:, NBK * 520:NBK * 520 + 64], 0.0), writes=[])
            MM = T(ph, "MM", [128, 20, 512], BF16)
            b_mm = Buf()
            Wq = T(ph, "Wq", [128, 8, 512], BF16)
            Wqr = T(ph, "Wqr", [128, 8, 512], BF16)
            b_Wq = Buf()
            fw.op("pool", lambda: nc.gpsimd.memset(Vd[:, :, :, 64:65], 1.0), writes=[])
            hrot = Rot([(T(ph, "hT%d" % i, [128, 8, 512], BF16), Buf()) for i in range(2)])
            crot = Rot([(T(ph, "Ct%d" % i, [128, 512], F32), Buf()) for i in range(2)])
            srot_t = Rot([(T(ph, "St%d" % i, [128, 512], F32), Buf()) for i in range(2)])
            t1rot = Rot([(T(ph, "t1_%d" % i, [128, 512], F32), Buf()) for i in range(1)])
            t2rot = Rot([(T(ph, "t2_%d" % i, [128, 512], F32), Buf()) for i in range(1)])

            def rope_proj(pjrot, hT, b_hT, N, W, Wr, b_W, dstT, dcol0, tok0, b_dst_list, dstB=None):
                Ct, b_Ct = crot.next()
                St, b_St = srot_t.next()
                fw.dma("sp", Ct[:, 0:N], ropeC_d[:, tok0:tok0 + N], writes=[b_Ct])
                fw.dma("sp", St[:, 0:N], ropeS_d[:, tok0:tok0 + N], writes=[b_St])
                for p in range(4):
                    psA, b_A = pjrot.next()
                    psB, b_B = pjrot.next()
                    for c in range(8):
                        mm(psA[:, 0:N], W[:, c, p * 128:(p + 1) * 128], hT[:, c, 0:N], c == 0, c == 7, [b_hT, b_W], [b_A])
                    for c in range(8):
                        mm(psB[:, 0:N], Wr[:, c, p * 128:(p + 1) * 128], hT[:, c, 0:N], c == 0, c == 7, [b_hT, b_W], [b_B])
                    t1, b_t1 = t1rot.next()
                    t2, b_t2 = t2rot.next()
                    fw.op("dve", lambda: nc.vector.tensor_tensor(t1[:, 0:N], psA[:, 0:N], Ct[:, 0:N], ALU.mult),
                          reads=[b_A, b_Ct], writes=[b_t1])
                    fw.op("dve", lambda: nc.vector.tensor_tensor(t2[:, 0:N], psB[:, 0:N], St[:, 0:N], ALU.mult),
                          reads=[b_B, b_St], writes=[b_t2])
                    pf_step()
                    if dstB is None:
                        fw.op("pool", lambda p=p: nc.gpsimd.tensor_tensor(dstT[:, p, dcol0:dcol0 + N], t1[:, 0:N], t2[:, 0:N], ALU.add),
                              reads=[b_t1, b_t2], writes=b_dst_list)
                    else:
                        fw.op("pool", lambda p=p: nc.gpsimd.tensor_tensor(dstT[0:64, p, dcol0:dcol0 + N], t1[0:64, 0:N], t2[0:64, 0:N], ALU.add),
                              reads=[b_t1, b_t2], writes=b_dst_list)
                        fw.op("pool", lambda p=p: nc.gpsimd.tensor_tensor(dstB[64:128, p, dcol0:dcol0 + N], t1[64:128, 0:N], t2[64:128, 0:N], ALU.add),
                              reads=[b_t1, b_t2], writes=b_dst_list)

            def make_rot_w(W, Wr, b_W):
                fw.op("pool", lambda: nc.gpsimd.memset(Wr[:], 0.0), writes=[b_W])
                Wv = W[:, :, :].rearrange("p c (h d) -> p c h d", h=8)
                Wrv = Wr[:, :, :].rearrange("p c (h d) -> p c h d", h=8)
                for c in range(8):
                    fw.op("pool", lambda c=c: nc.gpsimd.tensor_copy(Wrv[:, c, :, 0:8], Wv[:, c, :, 8:16]), reads=[b_W], writes=[b_W])
                    fw.op("pool", lambda c=c: nc.gpsimd.tensor_copy(Wrv[:, c, :, 8:16], Wv[:, c, :, 0:8]), reads=[b_W], writes=[b_W])

            with ExitStack() as sp1:
                Wk = T(sp1, "Wk", [128, 8, 512], BF16)
                Wkr = T(sp1, "Wkr", [128, 8, 512], BF16)
                Wv = T(sp1, "Wv", [128, 8, 512], BF16)
                b_W = Buf()
                loadw(Wk, w_in_v, OFF["kb"], 512, 8, b_W)
                make_rot_w(Wk, Wkr, b_W)
                loadw(Wv, w_in_v, OFF["vb"], 512, 8, b_W)
                loadw(Wq, w_in_v, OFF["qb"], 512, 8, b_Wq)
                make_rot_w(Wq, Wqr, b_Wq)
                for i in range(20):
                    fw.dma("pool", MM[:, i, :], mm_d[i], writes=[b_mm], par=True)
                tr = P(sp1, "tr", [128, 1024])
                pjrot = Rot([(P(sp1, "pj%d" % i, [128, 512]), Buf()) for i in range(6)])
                nr = make_norm_res(sp1, tr)
                for (b0, nb, hT, b_hT) in pipelined_chunks(KCH, nr, hrot, gT1):
                    N = nb * 128
                    rope_proj(pjrot, hT, b_hT, N, Wk, Wkr, b_W, KdT, b0 * 128, b0 * 128, [])
                    proj_tokmajor_V(pjrot, hT, b_hT, nb, Wv, b_W, Vd, b0)
                fw.barrier()
            with ExitStack() as sp2:
                b_W = b_Wq
                tr = P(sp2, "tr", [128, 1024])
                res = Res()
                res.pjrot = Rot([(P(sp2, "pj%d" % i, [128, 512]), Buf()) for i in range(1)])
                res.srot = Rot([(P(sp2, "S%d" % i, [128, 512]), Buf()) for i in range(3)])
                res.otrot = Rot([(P(sp2, "oT%d" % i, [128, 512]), Buf()) for i in range(2)])
                res.ptrot = Rot([(T(sp2, "pt%d" % i, [128, 512], BF16), Buf()) for i in range(4)])
                res.pmrot = Rot([(T(sp2, "pm%d" % i, [128, 512], BF16), Buf()) for i in range(8)])
                rope_rot = Rot(res.pjrot.items + res.srot.items)
                res.recrot = Rot([(T(sp2, "rec%d" % i, [65, 512], F32), Buf()) for i in range(2)])
                res.otsrot = Rot([(T(sp2, "ots%d" % i, [65, 512], F32), Buf()) for i in range(2)])
                QdA = T(sp2, "QdA", [128, 4, 512], BF16)
                QdB = T(sp2, "QdB", [128, 4, 512], BF16)
                b_QdT = Buf()
                fw.op("pool", lambda: nc.gpsimd.memset(QdA[:], 0.0), writes=[b_QdT])
                fw.op("pool", lambda: nc.gpsimd.memset(QdB[:], 0.0), writes=[b_QdT])
                nr = make_norm_res(sp2, tr)
                for (b0, nb, hT, b_hT) in pipelined_chunks(QCH, nr, hrot, gT1):
                    N = nb * 128
                    rope_proj(rope_rot, hT, b_hT, N, Wq, Wqr, b_W, QdA, 0, b0 * 128, [b_QdT], dstB=QdB)
                    attention("dil", b0, nb, (QdA, QdB), b_QdT, KdT, Vd_flat, ybT, res,
                              mb=(mbh if nb == 1 else mbo), MM=MM)
                fw.barrier()

        with ExitStack() as ph:
            KfT = T(ph, "KfT", [128, 4, NTOK], BF16)
            Vf_flat = T(ph, "Vf", [128, NBK * 520 + 64], BF16)
            Vf = Vf_flat[:, 0:NBK * 520].rearrange("p (b h d) -> p b h d", b=NBK, h=8)
            fw.op("pool", lambda: nc.gpsimd.memset(Vf_flat[:, NBK * 520:NBK * 520 + 64], 0.0), writes=[])
            FT = T(ph, "FT", [8, NTOK], F32)
            biasO = T(ph, "biasO", [128, NBK, 8], F32)
            biasH = T(ph, "biasH", [128, NBK, 8], F32)
            sel = T(ph, "sel", [8, 8 * 128], F32)
            negb = T(ph, "negb", [8, 1], F32)
            onesr = T(ph, "onesr", [8, 512], F32)
            b_c2 = Buf()
            fw.dma("sp", sel[:], sel_d, writes=[b_c2])
            fw.op("dve", lambda: nc.vector.tensor_scalar(negb[:], bfT[:], -1.0, None, ALU.mult), writes=[b_c2])
            fw.op("dve", lambda: nc.vector.memset(onesr[:], 1.0), writes=[b_c2])
            fw.op("pool", lambda: nc.gpsimd.memset(Vf[:, :, :, 64:65], 1.0), writes=[])
            hrot = Rot([(T(ph, "hT%d" % i, [128, 8, 512], BF16), Buf()) for i in range(2)])
            WqF = T(ph, "WqF", [128, 8, 512], BF16)
            b_WqF = Buf()
            with ExitStack() as sp1:
                Wk = T(sp1, "Wk", [128, 8, 512], BF16)
                Wv = T(sp1, "Wv", [128, 8, 512], BF16)
                Wf = T(sp1, "Wf", [128, 8, 8], BF16)
                b_W = Buf()
                loadw(Wk, w_in_v, OFF["ka"], 512, 8, b_W)
                loadw(Wv, w_in_v, OFF["va"], 512, 8, b_W)
                loadw(Wf, w_in_v, OFF["fa"], 8, 8, b_W)
                loadw(WqF, w_in_v, OFF["qa"], 512, 8, b_WqF)
                tr = P(sp1, "tr", [128, 1024])
                pjrot = Rot([(P(sp1, "pj%d" % i, [128, 512]), Buf()) for i in range(6)])
                nr = make_norm_res(sp1, tr)
                elrot = Rot([(T(sp1, "el%d" % i, [8, 512], F32), Buf()) for i in range(2)])
                b_FT = Buf()
                b_bias = Buf()
                prev_end = None
                for (b0, nb, hT, b_hT) in pipelined_chunks(KCH, nr, hrot, gT1):
                    N = nb * 128
                    t0 = b0 * 128
                    for p in range(4):
                        ps, b_ps = pjrot.next()
                        for c in range(8):
                            mm(ps[:, 0:N], Wk[:, c, p * 128:(p + 1) * 128], hT[:, c, 0:N], c == 0, c == 7, [b_hT, b_W], [b_ps])
                        act(KfT[:, p, t0:t0 + N], ps[:, 0:N], AF.Copy, [b_ps], [])
                        pf_step()
                    proj_tokmajor_V(pjrot, hT, b_hT, nb, Wv, b_W, Vf, b0)
                    ps, b_ps = pjrot.next()
                    for c in range(8):
                        mm(ps[0:8, 0:N], Wf[:, c, 0:8], hT[:, c, 0:N], c == 0, c == 7, [b_hT, b_W], [b_ps])
                    el, b_el = elrot.next()
                    act(el[:, 0:N], ps[0:8, 0:N], AF.Exp, [b_ps, b_c2], [b_el], scale=-1.0, bias=negb[:, 0:1])
                    act(el[:, 0:N], el[:, 0:N], AF.Ln, [b_el], [b_el], bias=1.0)
                    init = 0.0 if prev_end is None else FT[:, prev_end - 1:prev_end]
                    fw.op("dve", lambda el=el, init=init, t0=t0, N=N: nc.vector.tensor_tensor_scan(
                        FT[:, t0:t0 + N], onesr[:, 0:N], el[:, 0:N], init, ALU.mult, ALU.subtract),
                        reads=[b_el, b_FT, b_c2], writes=[b_FT], small=True)
                    prev_end = t0 + N
                    for bi in range(nb):
                        blk = b0 + bi
                        ps2, b_ps2 = pjrot.next()
                        fw.op("pe", lambda ps2=ps2, blk=blk: nc.tensor.transpose(ps2[:, 0:8], FT[0:8, blk * 128:(blk + 1) * 128], identf[0:8, 0:8]),
                              reads=[b_FT], writes=[b_ps2])
                        fw.op("dve", lambda ps2=ps2, blk=blk: nc.vector.tensor_scalar(
                            biasO[:, blk, :], ps2[:, 0:8], -1.0, mbo[:, blk:blk + 1], ALU.mult, ALU.add),
                            reads=[b_ps2], writes=[b_bias])
                        fw.op("dve", lambda ps2=ps2, blk=blk: nc.vector.tensor_scalar(
                            biasH[:, blk, :], ps2[:, 0:8], -1.0, mbh[:, blk:blk + 1], ALU.mult, ALU.add),
                            reads=[b_ps2], writes=[b_bias])
                fw.barrier()
            with ExitStack() as sp2:
                Wq = WqF
                b_W = b_WqF
                tr = P(sp2, "tr", [128, 1024])
                res = Res()
                res.sel = sel
                res.pjrot = Rot([(P(sp2, "pj%d" % i, [128, 512]), Buf()) for i in range(1)])
                res.srot = Rot([(P(sp2, "S%d" % i, [128, 512]), Buf()) for i in range(3)])
                res.otrot = Rot([(P(sp2, "oT%d" % i, [128, 512]), Buf()) for i in range(2)])
                res.ptrot = Rot([(T(sp2, "pt%d" % i, [128, 512], BF16), Buf()) for i in range(8)])
                res.ssrot = Rot([(T(sp2, "ss%d" % i, [128, 512], F32), Buf()) for i in range(4)])
                res.fqrot = Rot([(T(sp2, "fq%d" % i, [128, 512], F32), Buf()) for i in range(3)])
                res.recrot = Rot([(T(sp2, "rec%d" % i, [65, 512], F32), Buf()) for i in range(2)])
                res.otsrot = Rot([(T(sp2, "ots%d" % i, [65, 512], F32), Buf()) for i in range(2)])
                QfA = T(sp2, "QfA", [128, 4, 512], BF16)
                QfB = T(sp2, "QfB", [128, 4, 512], BF16)
                b_QfT = Buf()
                fw.op("pool", lambda: nc.gpsimd.memset(QfA[:], 0.0), writes=[b_QfT])
                fw.op("pool", lambda: nc.gpsimd.memset(QfB[:], 0.0), writes=[b_QfT])
                nr = make_norm_res(sp2, tr)
                for (b0, nb, hT, b_hT) in pipelined_chunks(QCH, nr, hrot, gT1):
                    N = nb * 128
                    for p in range(4):
                        ps, b_ps = res.pjrot.next()
                        for c in range(8):
                            mm(ps[:, 0:N], Wq[:, c, p * 128:(p + 1) * 128], hT[:, c, 0:N], c == 0, c == 7, [b_hT, b_W], [b_ps])
                        act(QfA[0:64, p, 0:N], ps[0:64, 0:N], AF.Copy, [b_ps], [b_QfT])
                        act(QfB[64:128, p, 0:N], ps[64:128, 0:N], AF.Copy, [b_ps], [b_QfT])
                    attention("fox", b0, nb, (QfA, QfB), b_QfT, KfT, Vf_flat, yaT, res, FT=FT,
                              biasK=(biasH if nb == 1 else biasO))
                fw.barrier()

        b_x1s = [Buf() for _ in range(17)]
        with ExitStack() as ph:
            Wg = T(ph, "Wg", [128, 8, 2048], BF16)
            wof = T(ph, "wof", [128, 4, D], BF16)
            wod = T(ph, "wod", [128, 4, D], BF16)
            wout = T(ph, "wout", [128, 8, D], BF16)
            gB1 = T(ph, "gB1", [128, D], F32)
            b_W = Buf()
            b_Wgm = [Buf() for _ in range(8)]
            b_Wo = Buf()
            b_Wout = Buf()

            def load_wg_group(fc):
                for part in range(2):
                    c0_ = OFF["ga"] + part * 1024 + fc * 128
                    d0_ = part * 1024 + fc * 128
                    for ch in range(2):
                        fw.dma("pool", Wg[:, ch * 4:(ch + 1) * 4, d0_:d0_ + 128], w_in_v[:, ch * 4:(ch + 1) * 4, c0_:c0_ + 128],
                               writes=[b_Wgm[fc]], par=True)
            load_wg_group(0)
            loadw(wof, wof_v, 0, D, 4, b_Wo)
            loadw(wod, wod_v, 0, D, 4, b_Wo)
            for fc_ in range(1, 8):
                load_wg_group(fc_)
            loadw(wout, w_out_v, 0, D, 8, b_Wout)
            fw.dma("sp", gB1[:], gB1_d, writes=[b_W], par=True)
            tr = P(ph, "tr", [128, 1024])
            g01 = P(ph, "g01", [128, 1024])
            g23 = P(ph, "g23", [128, 1024])
            gb_ = [Buf() for _ in range(4)]
            grot = Rot([(g01[:, 0:512], gb_[0]), (g01[:, 512:1024], gb_[1]),
                        (g23[:, 0:512], gb_[2]), (g23[:, 512:1024], gb_[3])])
            yps0 = P(ph, "yps", [128, 1024])
            yrot = Rot([(yps0, [Buf()]), (g01, [gb_[0], gb_[1]])])
            nr = make_norm_res(ph, tr)
            hrotm = Rot([(T(ph, "hTm%d" % i, [128, 8, 512], BF16), Buf()) for i in range(2)])
            mixT = T(ph, "mixT", [128, 8, 512], BF16)
            b_mix = Buf()
            sarot = Rot([(T(ph, "sa%d" % i, [128, 512], F32), Buf()) for i in range(2)])
            sbrot = Rot([(T(ph, "sb%d" % i, [128, 512], F32), Buf()) for i in range(2)])
            tmp = T(ph, "tmpm", [128, D], F32)
            b_tmp = Buf()
            x1rot = Rot([(T(ph, "x1t%d" % i, [128, D], F32), Buf()) for i in range(2)])
            if debug:
                out_toks.append(fw.dma("sp", dbg["ya"], yaT[:, :, :]))
                out_toks.append(fw.dma("sp", dbg["yb"], ybT[:, :, :]))
            for (b0, nb, hTm, b_hTm) in pipelined_chunks(QCH, nr, hrotm, gT1):
                N = nb * 128
                qtok0 = (b0 - 16) * 128
                for fc in range(8):
                    ga, b_ga = grot.next()
                    gb, b_gb = grot.next()
                    yap, b_yap = grot.next()
                    ybp, b_ybp = grot.next()
                    for c in range(8):
                        mm(ga[:, 0:N], Wg[:, c, fc * 128:(fc + 1) * 128], hTm[:, c, 0:N], c == 0, c == 7, [b_hTm, b_Wgm[fc]], [b_ga])
                    for c in range(8):
                        mm(gb[:, 0:N], Wg[:, c, 1024 + fc * 128:1024 + (fc + 1) * 128], hTm[:, c, 0:N], c == 0, c == 7, [b_hTm, b_Wgm[fc]], [b_gb])
                    for p in range(4):
                        mm(yap[:, 0:N], wof[:, p, fc * 128:(fc + 1) * 128], yaT[:, p, qtok0:qtok0 + N], p == 0, p == 3, [b_Wo], [b_yap])
                    for p in range(4):
                        mm(ybp[:, 0:N], wod[:, p, fc * 128:(fc + 1) * 128], ybT[:, p, qtok0:qtok0 + N], p == 0, p == 3, [b_Wo], [b_ybp])
                    sa, b_sa = sarot.next()
                    sb, b_sb = sbrot.next()
                    act(sa[:, 0:N], ga[:, 0:N], AF.Sigmoid, [b_ga], [b_sa])
                    act(sb[:, 0:N], gb[:, 0:N], AF.Sigmoid, [b_gb], [b_sb])
                    fw.op("dve", lambda: nc.vector.tensor_tensor(sa[:, 0:N], yap[:, 0:N], sa[:, 0:N], ALU.mult),
                          reads=[b_yap, b_sa], writes=[b_sa])
                    fw.op("dve", lambda: nc.vector.tensor_tensor(sb[:, 0:N], ybp[:, 0:N], sb[:, 0:N], ALU.mult),
                          reads=[b_ybp, b_sb], writes=[b_sb])
                    fw.op("pool", lambda fc=fc: nc.gpsimd.tensor_tensor(mixT[:, fc, 0:N], sa[:, 0:N], sb[:, 0:N], ALU.add),
                          reads=[b_sa, b_sb], writes=[b_mix])
                    pf_step()
                for bi in range(nb):
                    blk = b0 + bi
                    yps, yb_l = yrot.next()
                    for half in range(2):
                        for fc in range(8):
                            mm(yps[:, half * 512:(half + 1) * 512], mixT[:, fc, bi * 128:(bi + 1) * 128],
                               wout[:, fc, half * 512:(half + 1) * 512], fc == 0, fc == 7, [b_mix, b_Wout], yb_l)
                    st, b_st = rstd_from(nr, yps[:, :], yb_l)
                    xt, b_xt = nr.xrot.next()
                    fw.dma("sp", xt[:], xk[blk * 128:(blk + 1) * 128, :], writes=[b_xt])
                    fw.op("dve", lambda st=st: nc.vector.scalar_tensor_tensor(tmp[:], yps[:, :], st[:, 2:3], gB1[:], ALU.mult, ALU.mult),
                          reads=yb_l + [b_st, b_W], writes=[b_tmp])
                    x1t, b_x1t = x1rot.next()
                    fw.op("pool", lambda xt=xt, x1t=x1t: nc.gpsimd.tensor_tensor(x1t[:], tmp[:], xt[:], ALU.add),
                          reads=[b_tmp, b_xt], writes=[b_x1t])
                    fw.dma("sp", x1s[(blk - 16) * 128:(blk - 15) * 128, :], x1t[:], reads=[b_x1t], writes=[b_x1s[blk - 16]])
                    pf_step()
                    if debug:
                        out_toks.append(fw.dma("sp", dbg["x1"][(blk - 16) * 128:(blk - 15) * 128, :], x1t[:], reads=[b_x1t]))
            fw.barrier()
        att.close()

        with ExitStack() as ph:
            Wup = T(ph, "Wup", [128, 8, 2 * DFF], BF16)
            Wd = T(ph, "Wd", [128, 22, D], BF16)
            gB2 = T(ph, "gB2", [128, D], F32)
            cw = T(ph, "cw", [128, 3 * NFC], F32)
            cb = T(ph, "cb", [128, NFC], F32)
            b_W = Buf()
            NG = 11
            b_Wg = [Buf() for _ in range(NG)]
            for g in range(NG):
                for part in range(2):
                    c0_ = part * DFF + g * 256
                    for ch in range(2):
                        fw.dma("pool", Wup[:, ch * 4:(ch + 1) * 4, c0_:c0_ + 256], w_up_v[:, ch * 4:(ch + 1) * 4, c0_:c0_ + 256],
                               writes=[b_Wg[g]], par=True)
            loadw(Wd, w_down_v, 0, D, 22, b_W)
            b_sm = Buf()
            fw.dma("sp", gB2[:], gB2_d, writes=[b_sm], par=True)
            fw.dma("sp", cw[:], cw_d, writes=[b_sm], par=True)
            fw.dma("sp", cb[:], cb_d, writes=[b_sm], par=True)
            tr = P(ph, "tr", [128, 1024])
            u01 = P(ph, "u01", [128, 1024])
            u23 = P(ph, "u23", [128, 1024])
            ub_ = [Buf() for _ in range(4)]
            urot = Rot([(u01[:, 0:512], ub_[0]), (u01[:, 512:1024], ub_[1]),
                        (u23[:, 0:512], ub_[2]), (u23[:, 512:1024], ub_[3])])
            yps0 = P(ph, "yps", [128, 1024])
            yrot = Rot([(yps0, [Buf()]), (u01, [ub_[0], ub_[1]])])
            nr = make_norm_res(ph, tr)
            h2rot = Rot([(T(ph, "h2T%d" % i, [128, 8, 512], BF16), Buf()) for i in range(2)])
            h2h = T(ph, "h2h", [128, 8, 128], BF16)
            b_h2h = Buf()
            mT = T(ph, "mT", [128, 22, 512], BF16)
            b_mT = Buf()
            carry = T(ph, "carry", [128, NFC, 2], F32)
            b_carry = Buf()
            yarot = Rot([(T(ph, "Ya%d" % i, [128, 512], F32), Buf()) for i in range(2)])
            ybrot = Rot([(T(ph, "Yb%d" % i, [128, 512], F32), Buf()) for i in range(2)])
            sqrot = Rot([(T(ph, "sq%d" % i, [128, 512], F32), Buf()) for i in range(2)])
            orot = Rot([(T(ph, "ot%d" % i, [128, D], F32), Buf()) for i in range(1)])
            norm_transpose(nr, x1s[0:128, :], [b_x1s[0]], h2h, b_h2h, 0, gT2)
            def halo_carry(fc, bw):
                ps, b_ps = urot.next()
                for c in range(8):
                    mm(ps[:, 0:2], Wup[:, c, fc * 128:(fc + 1) * 128], h2h[:, c, 126:128], c == 0, c == 7, [b_h2h, bw], [b_ps])
                fw.op("dve", lambda: nc.vector.tensor_scalar(carry[:, fc, :], ps[:, 0:2], hflag[:, 0:1], None, ALU.mult),
                      reads=[b_ps], writes=[b_carry], small=True)
            def prep_ffn_gen(ci, out):
                h2T_, b_h2T_ = h2rot.next()
                out.append((h2T_, b_h2T_))
                for bi in range(4):
                    r0 = (1 + 4 * ci + bi) * 128
                    yield from norm_transpose_gen(nr, x1s[r0:r0 + 128, :], [b_x1s[1 + 4 * ci + bi]], h2T_, b_h2T_, bi * 128, gT2)

            h2out = []
            for _ in prep_ffn_gen(0, h2out):
                pass
            for ci in range(4):
                h2T, b_h2T = h2out[ci]
                pend_a = []
                pend_b = []

                def stage1(f):
                    Ys = []
                    bw = b_Wg[f // 2]
                    if ci == 0:
                        halo_carry(f, bw)
                        halo_carry(22 + f, bw)
                    for which, fc in ((0, f), (1, 22 + f)):
                        ps, b_ps = urot.next()
                        for c in range(8):
                            mm(ps[:, :], Wup[:, c, fc * 128:(fc + 1) * 128], h2T[:, c, :], c == 0, c == 7, [b_h2T, bw], [b_ps])
                        Y, b_Y = (yarot if which == 0 else ybrot).next()
                        act(Y[:, :], ps[:, :], AF.Identity, [b_ps, b_sm], [b_Y],
                            scale=cw[:, 2 * NFC + fc:2 * NFC + fc + 1], bias=cb[:, fc:fc + 1])
                        w1 = cw[:, NFC + fc:NFC + fc + 1]
                        w0 = cw[:, fc:fc + 1]
                        fw.op("dve", lambda: nc.vector.scalar_tensor_tensor(
                            Y[:, 1:512], ps[:, 0:511], w1, Y[:, 1:512], ALU.mult, ALU.add), reads=[b_ps, b_Y], writes=[b_Y])
                        fw.op("dve", lambda: nc.vector.scalar_tensor_tensor(
                            Y[:, 2:512], ps[:, 0:510], w0, Y[:, 2:512], ALU.mult, ALU.add), reads=[b_ps, b_Y], writes=[b_Y])
                        fw.op("dve", lambda: nc.vector.scalar_tensor_tensor(
                            Y[:, 0:1], carry[:, fc, 1:2], w1, Y[:, 0:1], ALU.mult, ALU.add), reads=[b_carry, b_Y], writes=[b_Y], small=True)
                        fw.op("dve", lambda: nc.vector.scalar_tensor_tensor(
                            Y[:, 0:2], carry[:, fc, 0:2], w0, Y[:, 0:2], ALU.mult, ALU.add), reads=[b_carry, b_Y], writes=[b_Y], small=True)
                        fw.op("dve", lambda: nc.vector.tensor_copy(carry[:, fc, :], ps[:, 510:512]),
                              reads=[b_ps], writes=[b_carry], small=True)
                        Ys.append((Y, b_Y))
                    pend_a.append((f, Ys))

                def stage2a(f, Ys):
                    (Ya, b_Ya), (Yb, b_Yb) = Ys
                    sq, b_sq = sqrot.next()
                    act(sq[:, :], Ya[:, :], AF.Gelu_apprx_tanh, [b_Ya], [b_sq])
                    fw.op("pool", lambda: nc.gpsimd.tensor_tensor(mT[:, f, :], sq[:, :], Yb[:, :], ALU.mult),
                          reads=[b_sq, b_Yb], writes=[b_mT])

                for it in range(22 + 1):
                    if it < 22:
                        stage1(it)
                    if len(pend_a) > 0 and (it >= 1):
                        stage2a(*pend_a.pop(0))
                    if it == 2 and ci + 1 < 4:
                        PFState.gen = prep_ffn_gen(ci + 1, h2out)
                    if it >= 2:
                        pf_step()
                while pend_a:
                    stage2a(*pend_a.pop(0))
                pf_finish()
                for bi in range(4):
                    lb = 4 * ci + bi
                    yps, yb_l = yrot.next()
                    for half in range(2):
                        for f in range(22):
                            mm(yps[:, half * 512:(half + 1) * 512], mT[:, f, bi * 128:(bi + 1) * 128],
                               Wd[:, f, half * 512:(half + 1) * 512], f == 0, f == 21, [b_mT, b_W], yb_l)
                    st, b_st = rstd_from(nr, yps[:, :], yb_l)
                    xt, b_xt = nr.xrot.next()
                    fw.dma("sp", xt[:], x1s[(1 + lb) * 128:(2 + lb) * 128, :], reads=[b_x1s[1 + lb]], writes=[b_xt])
                    ot, b_ot = orot.next()
                    fw.op("dve", lambda st=st, ot=ot: nc.vector.scalar_tensor_tensor(ot[:], yps[:, :], st[:, 2:3], gB2[:], ALU.mult, ALU.mult),
                          reads=yb_l + [b_st, b_sm], writes=[b_ot])
                    fw.op("pool", lambda xt=xt, ot=ot: nc.gpsimd.tensor_tensor(ot[:], ot[:], xt[:], ALU.add),
                          reads=[b_ot, b_xt], writes=[b_ot])
                    out_toks.append(fw.dma("sp", y_out[lb * 128:(lb + 1) * 128, :], ot[:], reads=[b_ot]))
            for t in out_toks:
                fw._wait("sp", t)
            fw.barrier()
    return nc


def _constants():
    c = {}
    k = np.arange(128)[:, None]
    q = np.arange(512)[None, :]
    mmask = np.zeros((20, 128, 512), np.float32)
    for idx in range(20):
        rel = idx - 3
        dlt = rel * 128 + q - k
        m = ((dlt >= 0) & (dlt <= 128)).astype(np.float32)
        m += ((dlt >= 0) & (dlt <= 512) & (dlt % 4 == 0)).astype(np.float32)
        m += ((dlt >= 0) & (dlt <= 2048) & (dlt % 16 == 0)).astype(np.float32)
        mmask[idx] = m
    c["mmask"] = mmask
    kk = np.arange(128)[:, None]
    qq = np.arange(128)[None, :]
    c["causal"] = np.where(kk <= qq, 0.0, -240000.0).astype(np.float32)
    c["ident"] = np.eye(128, dtype=np.float32)
    sel = np.zeros((8, 8, 128), np.float32)
    for h in range(8):
        sel[h, h, :] = 8.0
    c["sel"] = sel.reshape(8, 8 * 128)
    return c


def _rope_tables(base):
    half = 8
    inv_freq = (np.float32(500000.0) ** (-(np.arange(half, dtype=np.float32) * np.float32(2.0) / np.float32(16.0)))).astype(np.float32)
    pos = np.maximum(np.arange(NTOK) + base, 0).astype(np.float32)
    ang = (pos[:, None] * inv_freq[None, :]).astype(np.float32)
    cos = np.cos(ang.astype(np.float64)).astype(np.float32).T
    sin = np.sin(ang.astype(np.float64)).astype(np.float32).T
    C = np.ones((128, NTOK), np.float32)
    S = np.zeros((128, NTOK), np.float32)
    for a in range(2):
        C[a * 64:a * 64 + 8] = cos
        C[a * 64 + 8:a * 64 + 16] = cos
        S[a * 64:a * 64 + 8] = -sin
        S[a * 64 + 8:a * 64 + 16] = sin
    return C, S


_PROG = {}


def kernel(x, g_pre_mix, w_in, b_forget, w_o_fox, w_o_dil, w_out, g_post_mix,
           g_pre_ffn, w_up, conv_w, conv_b, w_down, g_post_ffn, _debug=False):
    f32 = np.float32
    x = np.asarray(x, f32)
    B, S, _ = x.shape
    consts = _constants()
    shared = {
        "w_in": np.ascontiguousarray(np.asarray(w_in, f32)[0]),
        "w_o_fox": np.ascontiguousarray(np.asarray(w_o_fox, f32)[0]),
        "w_o_dil": np.ascontiguousarray(np.asarray(w_o_dil, f32)[0]),
        "w_out": np.ascontiguousarray(np.asarray(w_out, f32)[0]),
        "w_up": np.ascontiguousarray(np.asarray(w_up, f32)[0]),
        "w_down": np.ascontiguousarray(np.asarray(w_down, f32)[0]),
        "gT1": np.ascontiguousarray(np.asarray(g_pre_mix, f32)[0].reshape(8, 128).T),
        "gT2": np.ascontiguousarray(np.asarray(g_pre_ffn, f32)[0].reshape(8, 128).T),
        "gB1": np.ascontiguousarray(np.broadcast_to(np.asarray(g_post_mix, f32)[0][None, :], (128, D))),
        "gB2": np.ascontiguousarray(np.broadcast_to(np.asarray(g_post_ffn, f32)[0][None, :], (128, D))),
        "bf": np.ascontiguousarray(np.asarray(b_forget, f32)[0].reshape(8, 1)),
        "cw": np.ascontiguousarray(np.asarray(conv_w, f32)[0].reshape(3, NFC, 128).transpose(2, 0, 1).reshape(128, 3 * NFC)),
        "cb": np.ascontiguousarray(np.asarray(conv_b, f32)[0].reshape(NFC, 128).T),
    }
    shared.update(consts)
    in_maps = []
    for core in range(8):
        b, h = core // 2, core % 2
        base = 2048 * h - 2176
        xkc = np.zeros((NTOK, D), f32)
        lo = max(0, -base)
        xkc[lo:] = x[b, base + lo:base + NTOK]
        tok = np.arange(NTOK) + base
        valid = tok >= 0
        mbo = np.where(valid, 0.0, -30000.0).astype(f32)
        halo_valid = valid.copy()
        halo_valid[16 * 128:17 * 128] = True
        mbh = np.where(halo_valid, 0.0, -30000.0).astype(f32)
        C, Sn = _rope_tables(base)
        m = dict(shared)
        m["xk"] = xkc
        m["mb_own"] = np.ascontiguousarray(mbo.reshape(NBK, 128).T)
        m["mb_halo"] = np.ascontiguousarray(mbh.reshape(NBK, 128).T)
        m["ropeC"] = C
        m["ropeS"] = Sn
        m["hflag"] = np.full((128, 1), float(h), f32)
        in_maps.append(m)
    key = bool(_debug)
    if key not in _PROG:
        _PROG[key] = build_program(debug=key)
    nc = _PROG[key]
    res = run_bass_kernel_spmd(nc, in_maps, core_ids=list(range(8)))
    out = np.zeros((B, S, D), f32)
    for core in range(8):
        b, h = core // 2, core % 2
        out[b, 2048 * h:2048 * (h + 1)] = res.results[core]["y"]
    if _debug:
        return out, res.results
    return out
```
